# Optimizing a Trainium2 kernel written in Bass

```python
import math
import jax, jax.numpy as jnp
from jax import lax
import numpy as np

D_MODEL = 1024
BATCH = 2
SEQ = 8192
DEPTH = 2
DEC_BATCH = 128
DEC_SEQ = 8
PAST_LEN = 8192
PAGE_SIZE = 128

F32 = jnp.float32
EPS = 1e-6
SSD_HEADS = 8
SSD_HEADDIM = 64
SSD_INNER = SSD_HEADS * SSD_HEADDIM
SSD_GROUPS = 2
SSD_STATE = 64
CONV_W = 4
CONV_DIM = SSD_INNER + 2 * SSD_GROUPS * SSD_STATE
SSD_CHUNK = 128
MLA_HEADS = 4
Q_LORA = 256
KV_LORA = 128
D_NOPE = 64
D_ROPE = 32
D_QK = D_NOPE + D_ROPE
D_V = 64
MLA_WIDTH = MLA_HEADS * D_V
ROPE_THETA = 10000.0
Q_BLOCK = 128
GM_GROUPS = 4
GM_GROUP_DIM = 64
GM_WIDTH = GM_GROUPS * GM_GROUP_DIM
GM_CHUNK = 128
MIX_WIDTH = SSD_INNER + MLA_WIDTH + GM_WIDTH
_SIZES = (SSD_INNER, CONV_DIM, SSD_HEADS, Q_LORA, KV_LORA, D_ROPE, GM_WIDTH, GM_WIDTH)
IN_COLS = int(sum(_SIZES))
SPLITS = tuple(int(v) for v in np.cumsum(_SIZES)[:-1])
N_MEM = 256
MEM_HEADS = 4
MEM_HEAD_DIM = D_MODEL // MEM_HEADS
PEER_HEADS = 8
N_KEYS = 128
N_EXPERTS = N_KEYS * N_KEYS
PEER_QDIM = 256
PEER_HALF = PEER_QDIM // 2
PEER_TOPK = 16
PEER_BLOCK = 256
N_PAGES = PAST_LEN // PAGE_SIZE

kernel_name = "hymba_ssd_mla_gmlp_peer_step"


def rmsnorm(x, g):
    xf = x.astype(F32)
    y = xf * lax.rsqrt(jnp.mean(xf * xf, axis=-1, keepdims=True) + EPS)
    return (y * g.astype(F32)).astype(x.dtype)


def layernorm(x, g, b):
    xf = x.astype(F32)
    mu = jnp.mean(xf, axis=-1, keepdims=True)
    var = jnp.mean(jnp.square(xf - mu), axis=-1, keepdims=True)
    return ((xf - mu) * lax.rsqrt(var + EPS) * g.astype(F32) + b.astype(F32)).astype(x.dtype)


def rope_cos_sin(pos, dtype):
    half = D_ROPE // 2
    inv = 1.0 / (ROPE_THETA ** (jnp.arange(half, dtype=F32) / half))
    ang = pos.astype(F32)[:, None] * inv[None, :]
    return jnp.cos(ang).astype(dtype), jnp.sin(ang).astype(dtype)


def apply_rope(x, cos, sin):
    x1, x2 = jnp.split(x, 2, axis=-1)
    return jnp.concatenate([x1 * cos - x2 * sin, x2 * cos + x1 * sin], axis=-1)


def qk_gain(g):
    return jnp.concatenate([g, g[D_NOPE:]])


def causal_conv(xbc, prefix, w, b):
    xpad = jnp.concatenate([prefix.astype(xbc.dtype), xbc], axis=1)
    out = lax.conv_general_dilated(xpad, w[:, None, :], window_strides=(1,), padding="VALID",
                                   dimension_numbers=("NWC", "WIO", "NWC"), feature_group_count=CONV_DIM)
    return jax.nn.silu(out + b), xpad[:, -(CONV_W - 1):]


def ssd_scan(x, dt, a, bmat, cmat, h0):
    n, L = x.shape[:2]
    q = min(SSD_CHUNK, L)
    lp = -(-L // q) * q
    pad = lp - L
    if pad:
        padf = lambda t: jnp.pad(t, [(0, 0), (0, pad)] + [(0, 0)] * (t.ndim - 2))
        x, dt, bmat, cmat = padf(x), padf(dt), padf(bmat), padf(cmat)
    nc = lp // q
    rep = SSD_HEADS // SSD_GROUPS
    xc = x.reshape(n, nc, q, SSD_HEADS, SSD_HEADDIM).astype(F32)
    dtc = dt.reshape(n, nc, q, SSD_HEADS)
    bc = jnp.repeat(bmat.reshape(n, nc, q, SSD_GROUPS, SSD_STATE), rep, axis=3).astype(F32)
    cc = jnp.repeat(cmat.reshape(n, nc, q, SSD_GROUPS, SSD_STATE), rep, axis=3).astype(F32)
    acum = jnp.cumsum(dtc * a, axis=2)
    seg = acum[:, :, :, None, :] - acum[:, :, None, :, :]
    causal = jnp.tril(jnp.ones((q, q), bool))[None, None, :, :, None]
    decay = jnp.exp(jnp.where(causal, seg, -jnp.inf))
    scores = jnp.einsum("bcihn,bcjhn->bcijh", cc, bc) * decay
    y_diag = jnp.einsum("bcijh,bcjh,bcjhp->bcihp", scores, dtc, xc)
    to_end = jnp.exp(acum[:, :, -1:, :] - acum) * dtc
    s_chunk = jnp.einsum("bcjhn,bcjh,bcjhp->bchpn", bc, to_end, xc)
    chunk_decay = jnp.exp(acum[:, :, -1, :])

    def step(h, inp):
        s_c, d_c = inp
        return h * d_c[:, :, None, None] + s_c, h

    h_final, h_start = lax.scan(step, h0.astype(F32),
                                (jnp.moveaxis(s_chunk, 1, 0), jnp.moveaxis(chunk_decay, 1, 0)))
    h_start = jnp.moveaxis(h_start, 0, 1)
    y_off = jnp.einsum("bcihn,bchpn->bcihp", cc, h_start) * jnp.exp(acum)[..., None]
    y = (y_diag + y_off).reshape(n, lp, SSD_HEADS, SSD_HEADDIM)[:, :L]
    return y, h_final


def ssd_mixer(z, xbc_raw, dt_raw, conv_prefix, h0, conv_w, conv_b, dt_bias, a_log, d_skip, g_out):
    n, L, _ = z.shape
    xbc, conv_state = causal_conv(xbc_raw, conv_prefix, conv_w, conv_b)
    xs, bm, cm = jnp.split(xbc, [SSD_INNER, SSD_INNER + SSD_GROUPS * SSD_STATE], axis=-1)
    xs = xs.reshape(n, L, SSD_HEADS, SSD_HEADDIM)
    bm = bm.reshape(n, L, SSD_GROUPS, SSD_STATE)
    cm = cm.reshape(n, L, SSD_GROUPS, SSD_STATE)
    dt = jax.nn.softplus(dt_raw.astype(F32) + dt_bias.astype(F32))
    a = -jnp.exp(a_log.astype(F32))
    y, h = ssd_scan(xs, dt, a, bm, cm, h0)
    y = y + d_skip.astype(F32)[:, None] * xs.astype(F32)
    y = y.reshape(n, L, SSD_INNER) * jax.nn.silu(z.astype(F32))
    y = rmsnorm(y.reshape(n, L, SSD_GROUPS, SSD_INNER // SSD_GROUPS),
                g_out.reshape(SSD_GROUPS, SSD_INNER // SSD_GROUPS)).reshape(n, L, SSD_INNER)
    return y.astype(z.dtype), h.astype(z.dtype), conv_state


def mla_key_parts(ckv, kr, w_uk):
    k_nope = jnp.einsum("ntc,chd->nthd", ckv, w_uk)
    ss = jnp.sum(jnp.square(k_nope.astype(F32)), -1) + jnp.sum(jnp.square(kr.astype(F32)), -1)[..., None]
    r = lax.rsqrt(ss / D_QK + EPS)
    return k_nope, jnp.moveaxis(r, 1, 2)


def mla_attend(q, k_nope, kr, r_k, ckv, w_uv, g_k, q_pos, k_pos):
    qg = q * qk_gain(g_k)
    dots = (jnp.einsum("nshd,nthd->nhst", qg[..., :D_NOPE], k_nope)
            + jnp.einsum("nshd,ntd->nhst", qg[..., D_NOPE:], kr))
    s = dots.astype(F32) * r_k[:, :, None, :] * (D_QK ** -0.5)
    s = jnp.where((k_pos[None, :] <= q_pos[:, None])[None, None], s, -jnp.inf)
    p = jax.nn.softmax(s, axis=-1).astype(ckv.dtype)
    o_lat = jnp.einsum("nhst,ntc->nshc", p, ckv)
    return jnp.einsum("nshc,chd->nshd", o_lat, w_uv)


def mla_blocked(q, k_nope, kr, r_k, ckv, w_uv, g_k, q_pos, k_pos):
    n, L = q.shape[:2]
    qb = min(Q_BLOCK, L)
    lp = -(-L // qb) * qb
    pad = lp - L
    qp = jnp.pad(q, ((0, 0), (0, pad), (0, 0), (0, 0)))
    pp = jnp.pad(q_pos, (0, pad))
    nb = lp // qb
    qblocks = jnp.moveaxis(qp.reshape(n, nb, qb, MLA_HEADS, D_QK), 1, 0)
    out = lax.map(lambda a: mla_attend(a[0], k_nope, kr, r_k, ckv, w_uv, g_k, a[1], k_pos),
                  (qblocks, pp.reshape(nb, qb)))
    return jnp.moveaxis(out, 0, 1).reshape(n, lp, MLA_WIDTH)[:, :L]


def gmlp_mixer(u_raw, v_raw, ln_g, ln_b, ws, bs):
    n, L, _ = u_raw.shape
    u = jax.nn.gelu(u_raw)
    v = layernorm(jax.nn.gelu(v_raw).reshape(n, L, GM_GROUPS, GM_GROUP_DIM),
                  ln_g.reshape(GM_GROUPS, GM_GROUP_DIM), ln_b.reshape(GM_GROUPS, GM_GROUP_DIM))
    lp = -(-L // GM_CHUNK) * GM_CHUNK
    vp = jnp.pad(v, ((0, 0), (0, lp - L), (0, 0), (0, 0)))
    vc = vp.reshape(n, lp // GM_CHUNK, GM_CHUNK, GM_GROUPS, GM_GROUP_DIM)
    wm = ws * jnp.tril(jnp.ones((GM_CHUNK, GM_CHUNK), ws.dtype))
    mixed = jnp.einsum("gij,ncjgd->ncigd", wm, vc) + bs.T[:, :, None]
    mixed = mixed.reshape(n, lp, GM_WIDTH)[:, :L]
    start = ((L - 1) // GM_CHUNK) * GM_CHUNK
    return u * mixed, v.reshape(n, L, GM_WIDTH)[:, start:]


def mem_kv(mem, g_m, w_mk, w_mv, g_mk):
    n = mem.shape[0]
    m = rmsnorm(mem, g_m)
    k = rmsnorm((m @ w_mk).reshape(n, N_MEM, MEM_HEADS, MEM_HEAD_DIM), g_mk)
    v = (m @ w_mv).reshape(n, N_MEM, MEM_HEADS, MEM_HEAD_DIM)
    return k, v


def mem_attend(h, k, v, w_mq, g_mq, w_mo):
    n, L, _ = h.shape
    q = rmsnorm((h @ w_mq).reshape(n, L, MEM_HEADS, MEM_HEAD_DIM), g_mq)
    s = jnp.einsum("nlhd,nmhd->nhlm", q, k.astype(q.dtype)).astype(F32) * (MEM_HEAD_DIM ** -0.5)
    p = jax.nn.softmax(s, axis=-1).astype(h.dtype)
    o = jnp.einsum("nhlm,nmhd->nlhd", p, v.astype(h.dtype)).reshape(n, L, D_MODEL)
    return o @ w_mo


def peer(h, w_pq, k1, k2, u_tab, v_tab):
    n, L, D = h.shape
    t = h.reshape(n * L, D)
    T = n * L
    blk = min(PEER_BLOCK, T)
    tp = -(-T // blk) * blk
    t = jnp.pad(t, ((0, tp - T), (0, 0)))

    def one(xb):
        q = (xb @ w_pq).reshape(blk, PEER_HEADS, 2, PEER_HALF).astype(F32)
        s1 = jnp.einsum("thd,hkd->thk", q[:, :, 0], k1.astype(F32))
        s2 = jnp.einsum("thd,hkd->thk", q[:, :, 1], k2.astype(F32))
        v1, i1 = lax.top_k(s1, PEER_TOPK)
        v2, i2 = lax.top_k(s2, PEER_TOPK)
        cand = (v1[..., :, None] + v2[..., None, :]).reshape(blk, PEER_HEADS, PEER_TOPK * PEER_TOPK)
        sc, j = lax.top_k(cand, PEER_TOPK)
        e = (jnp.take_along_axis(i1, j // PEER_TOPK, axis=-1) * N_KEYS
             + jnp.take_along_axis(i2, j % PEER_TOPK, axis=-1))
        g = jax.nn.softmax(sc, axis=-1)
        ue = jnp.take(u_tab, e, axis=0)
        ve = jnp.take(v_tab, e, axis=0)
        act = jax.nn.gelu(jnp.einsum("td,thkd->thk", xb, ue).astype(F32))
        return jnp.einsum("thk,thkd->td", (g * act).astype(xb.dtype), ve)

    out = lax.map(one, t.reshape(tp // blk, blk, D)).reshape(tp, D)[:T]
    return out.reshape(n, L, D)


def mixer_sublayer(x, pos, conv_prefix, ssm0, past_ckv, past_kr, lw):
    n, L, _ = x.shape
    h = rmsnorm(x, lw["g_mix"])
    z, xbc, dt_raw, cq, ckv_raw, kr_raw, u_raw, v_raw = jnp.split(h @ lw["w_in"], SPLITS, axis=-1)
    y_ssd, ssm_new, conv_new = ssd_mixer(z, xbc, dt_raw, conv_prefix, ssm0, lw["conv_w"], lw["conv_b"],
                                         lw["dt_bias"], lw["a_log"], lw["d_skip"], lw["g_ssd_out"])
    cos, sin = rope_cos_sin(pos, x.dtype)
    q = (rmsnorm(cq, lw["g_cq"]) @ lw["w_uq"]).reshape(n, L, MLA_HEADS, D_QK)
    q = jnp.concatenate([q[..., :D_NOPE], apply_rope(q[..., D_NOPE:], cos[:, None], sin[:, None])], axis=-1)
    q = rmsnorm(q, qk_gain(lw["g_qk_q"]))
    ckv = rmsnorm(ckv_raw, lw["g_ckv"])
    kr = apply_rope(kr_raw, cos, sin)
    if past_ckv is None:
        ckv_all, kr_all = ckv, kr
    else:
        ckv_all = jnp.concatenate([past_ckv.astype(ckv.dtype), ckv], axis=1)
        kr_all = jnp.concatenate([past_kr.astype(kr.dtype), kr], axis=1)
    k_nope, r_k = mla_key_parts(ckv_all, kr_all, lw["w_uk"])
    k_pos = jnp.arange(ckv_all.shape[1])
    y_mla = mla_blocked(q, k_nope, kr_all, r_k, ckv_all, lw["w_uv"], lw["g_qk_k"], pos, k_pos)
    y_gm, v_open = gmlp_mixer(u_raw, v_raw, lw["gm_ln_g"], lw["gm_ln_b"], lw["gm_ws"], lw["gm_bs"])
    x = x + jnp.concatenate([y_ssd, y_mla.astype(x.dtype), y_gm], axis=-1) @ lw["w_out"]
    return x, ssm_new, conv_new, ckv, kr, v_open


def mem_peer_sublayers(x, mem_k, mem_v, lw):
    x = x + mem_attend(rmsnorm(x, lw["g_memx"]), mem_k, mem_v, lw["w_mq"], lw["g_mq"], lw["w_mo"])
    x = x + peer(rmsnorm(x, lw["g_ffn"]), lw["w_pq"], lw["peer_k1"], lw["peer_k2"], lw["peer_u"], lw["peer_v"])
    return x


def setup_inputs(seed: int = 0) -> dict:
    key = jax.random.key(seed)
    keys = jax.random.split(key, 64)
    ctr = [0]

    def nk():
        k = keys[ctr[0]]
        ctr[0] += 1
        return k

    def nrm(shape, scale=1.0):
        return jax.random.normal(nk(), shape, F32) * scale

    def gain(shape):
        return 1.0 + 0.02 * jax.random.normal(nk(), shape, F32)

    n_phys = DEC_BATCH * N_PAGES + (DEC_BATCH * N_PAGES) // 4
    page_table = jax.random.permutation(nk(), n_phys)[: DEC_BATCH * N_PAGES].reshape(DEC_BATCH, N_PAGES).astype(jnp.int32)
    dt0 = jnp.exp(jax.random.uniform(nk(), (DEPTH, SSD_HEADS), F32, math.log(1e-3), math.log(1e-1)))
    dt_bias = dt0 + jnp.log(-jnp.expm1(-dt0))
    a_log = jnp.log(jax.random.uniform(nk(), (DEPTH, SSD_HEADS), F32, 1.0, 16.0))
    dsc = D_MODEL ** -0.5
    return {
        "x_prompt": nrm((BATCH, SEQ, D_MODEL)),
        "x_sample": nrm((DEC_BATCH, DEC_SEQ, D_MODEL)),
        "cache_mla_ckv": nrm((DEPTH, n_phys, PAGE_SIZE, KV_LORA)),
        "cache_mla_krope": nrm((DEPTH, n_phys, PAGE_SIZE, D_ROPE)),
        "state_ssm": nrm((DEPTH, DEC_BATCH, SSD_HEADS, SSD_HEADDIM, SSD_STATE), 0.3),
        "state_conv": nrm((DEPTH, DEC_BATCH, CONV_W - 1, CONV_DIM)),
        "cache_mem_k": nrm((DEPTH, DEC_BATCH, N_MEM, MEM_HEADS, MEM_HEAD_DIM)),
        "cache_mem_v": nrm((DEPTH, DEC_BATCH, N_MEM, MEM_HEADS, MEM_HEAD_DIM), 0.5),
        "page_table": page_table,
        "mem_prompt": nrm((BATCH, N_MEM, D_MODEL)),
        "g_mix": gain((DEPTH, D_MODEL)),
        "w_in": nrm((DEPTH, D_MODEL, IN_COLS), dsc),
        "conv_w": nrm((DEPTH, CONV_W, CONV_DIM), CONV_W ** -0.5),
        "conv_b": nrm((DEPTH, CONV_DIM), 0.02),
        "dt_bias": dt_bias,
        "a_log": a_log,
        "d_skip": gain((DEPTH, SSD_HEADS)),
        "g_ssd_out": gain((DEPTH, SSD_INNER)),
        "g_cq": gain((DEPTH, Q_LORA)),
        "w_uq": nrm((DEPTH, Q_LORA, MLA_HEADS * D_QK), Q_LORA ** -0.5),
        "g_ckv": gain((DEPTH, KV_LORA)),
        "w_uk": nrm((DEPTH, KV_LORA, MLA_HEADS, D_NOPE), KV_LORA ** -0.5),
        "w_uv": nrm((DEPTH, KV_LORA, MLA_HEADS, D_V), KV_LORA ** -0.5),
        "g_qk_q": gain((DEPTH, D_NOPE + D_ROPE // 2)),
        "g_qk_k": gain((DEPTH, D_NOPE + D_ROPE // 2)),
        "gm_ln_g": gain((DEPTH, GM_WIDTH)),
        "gm_ln_b": nrm((DEPTH, GM_WIDTH), 0.02),
        "gm_ws": nrm((DEPTH, GM_GROUPS, GM_CHUNK, GM_CHUNK), GM_CHUNK ** -0.5),
        "gm_bs": gain((DEPTH, GM_GROUPS, GM_CHUNK)),
        "w_out": nrm((DEPTH, MIX_WIDTH, D_MODEL), MIX_WIDTH ** -0.5),
        "g_memx": gain((DEPTH, D_MODEL)),
        "g_memm": gain((DEPTH, D_MODEL)),
        "w_mq": nrm((DEPTH, D_MODEL, D_MODEL), dsc),
        "w_mk": nrm((DEPTH, D_MODEL, D_MODEL), dsc),
        "w_mv": nrm((DEPTH, D_MODEL, D_MODEL), dsc),
        "g_mq": gain((DEPTH, MEM_HEAD_DIM)),
        "g_mk": gain((DEPTH, MEM_HEAD_DIM)),
        "w_mo": nrm((DEPTH, D_MODEL, D_MODEL), dsc),
        "g_ffn": gain((DEPTH, D_MODEL)),
        "w_pq": nrm((DEPTH, D_MODEL, PEER_HEADS * PEER_QDIM), dsc),
        "peer_k1": nrm((DEPTH, PEER_HEADS, N_KEYS, PEER_HALF), PEER_HALF ** -0.5),
        "peer_k2": nrm((DEPTH, PEER_HEADS, N_KEYS, PEER_HALF), PEER_HALF ** -0.5),
        "peer_u": nrm((DEPTH, N_EXPERTS, D_MODEL), dsc),
        "peer_v": nrm((DEPTH, N_EXPERTS, D_MODEL), dsc),
    }


def reference(x_prompt, x_sample, cache_mla_ckv, cache_mla_krope, state_ssm, state_conv, cache_mem_k, cache_mem_v,
              page_table, mem_prompt, g_mix, w_in, conv_w, conv_b, dt_bias, a_log, d_skip, g_ssd_out,
              g_cq, w_uq, g_ckv, w_uk, w_uv, g_qk_q, g_qk_k, gm_ln_g, gm_ln_b, gm_ws, gm_bs, w_out,
              g_memx, g_memm, w_mq, w_mk, w_mv, g_mq, g_mk, w_mo, g_ffn, w_pq, peer_k1, peer_k2, peer_u, peer_v):
    nb_p, seq_p = x_prompt.shape[:2]
    nb_s, seq_s = x_sample.shape[:2]
    pos_p = jnp.arange(seq_p)
    pos_s = PAST_LEN + jnp.arange(seq_s)
    xp, xs = x_prompt, x_sample
    p_ckv, p_kr, p_ssm, p_conv, p_mk, p_mv, p_gmv = [], [], [], [], [], [], []
    s_ckv, s_kr, s_ssm, s_conv, s_gmv = [], [], [], [], []
    for l in range(DEPTH):
        lw = dict(g_mix=g_mix[l], w_in=w_in[l], conv_w=conv_w[l], conv_b=conv_b[l], dt_bias=dt_bias[l],
                  a_log=a_log[l], d_skip=d_skip[l], g_ssd_out=g_ssd_out[l], g_cq=g_cq[l], w_uq=w_uq[l],
                  g_ckv=g_ckv[l], w_uk=w_uk[l], w_uv=w_uv[l], g_qk_q=g_qk_q[l], g_qk_k=g_qk_k[l],
                  gm_ln_g=gm_ln_g[l], gm_ln_b=gm_ln_b[l], gm_ws=gm_ws[l], gm_bs=gm_bs[l], w_out=w_out[l],
                  g_memx=g_memx[l], w_mq=w_mq[l], g_mq=g_mq[l], w_mo=w_mo[l], g_ffn=g_ffn[l], w_pq=w_pq[l],
                  peer_k1=peer_k1[l], peer_k2=peer_k2[l], peer_u=peer_u[l], peer_v=peer_v[l])
        conv0 = jnp.zeros((nb_p, CONV_W - 1, CONV_DIM), xp.dtype)
        ssm0 = jnp.zeros((nb_p, SSD_HEADS, SSD_HEADDIM, SSD_STATE), F32)
        xp, ssm_n, conv_n, ckv_n, kr_n, gmv_n = mixer_sublayer(xp, pos_p, conv0, ssm0, None, None, lw)
        mk, mv = mem_kv(mem_prompt, g_memm[l], w_mk[l], w_mv[l], g_mk[l])
        xp = mem_peer_sublayers(xp, mk, mv, lw)
        p_ckv.append(ckv_n); p_kr.append(kr_n); p_ssm.append(ssm_n); p_conv.append(conv_n)
        p_mk.append(mk); p_mv.append(mv); p_gmv.append(gmv_n)
        past_ckv = cache_mla_ckv[l][page_table].reshape(nb_s, N_PAGES * PAGE_SIZE, KV_LORA)
        past_kr = cache_mla_krope[l][page_table].reshape(nb_s, N_PAGES * PAGE_SIZE, D_ROPE)
        xs, ssm_n, conv_n, ckv_n, kr_n, gmv_n = mixer_sublayer(xs, pos_s, state_conv[l], state_ssm[l],
                                                               past_ckv, past_kr, lw)
        xs = mem_peer_sublayers(xs, cache_mem_k[l], cache_mem_v[l], lw)
        s_ckv.append(ckv_n); s_kr.append(kr_n); s_ssm.append(ssm_n); s_conv.append(conv_n); s_gmv.append(gmv_n)
    y_prompt, y_sample = xp, xs
    new_ckv_p, new_krope_p = jnp.stack(p_ckv), jnp.stack(p_kr)
    new_ssm_p, new_conv_p = jnp.stack(p_ssm), jnp.stack(p_conv)
    new_mem_k_p, new_mem_v_p = jnp.stack(p_mk), jnp.stack(p_mv)
    new_gmv_p = jnp.stack(p_gmv)
    new_ckv_s, new_krope_s = jnp.stack(s_ckv), jnp.stack(s_kr)
    new_ssm_s, new_conv_s = jnp.stack(s_ssm), jnp.stack(s_conv)
    new_gmv_s = jnp.stack(s_gmv)
    return (y_prompt, y_sample, new_ckv_p, new_krope_p, new_ssm_p, new_conv_p, new_mem_k_p, new_mem_v_p, new_gmv_p,
            new_ckv_s, new_krope_s, new_ssm_s, new_conv_s, new_gmv_s)
```

```python
import math
from contextlib import ExitStack
import numpy as np
import concourse.bass as bass
import concourse.mybir as mybir
from concourse.bass_utils import run_bass_kernel_spmd

F32 = mybir.dt.float32
BF16 = mybir.dt.bfloat16
I32 = mybir.dt.int32
AF = mybir.ActivationFunctionType
ALU = mybir.AluOpType
AX = mybir.AxisListType
NDMA_SEM = 48
EPS = 1e-6
NEG = -30000.0


class Prog:
    ENGS = ("pe", "act", "dve", "pool", "sp")

    def __init__(self, nc):
        self.nc = nc
        self.ops = {e: [] for e in self.ENGS}
        self.cnt = {e: 0 for e in self.ENGS}
        self.lastw = {}
        self.readers = {}
        self.waited = {e: {} for e in self.ENGS}
        self.ndma = {"sp": 0, "pool": 0}
        self.dma_tokens = {"sp": [], "pool": []}
        self.sems = {}
        self.semval = {}

    @staticmethod
    def _key(r):
        if isinstance(r, (str, tuple)):
            return r
        return r.tensor.name

    def _deps(self, reads, writes):
        deps = []
        for r in reads:
            t = self.lastw.get(r)
            if t is not None:
                deps.append(t)
        for w in writes:
            t = self.lastw.get(w)
            if t is not None:
                deps.append(t)
            deps.extend(self.readers.get(w, ()))
        return deps

    def _waits(self, eng, deps, skip_pe=False):
        waits = []
        wd = self.waited[eng]
        for (semkey, val, deng) in deps:
            if skip_pe and deng == "pe" and eng == "pe":
                continue
            if wd.get(semkey, 0) >= val:
                continue
            wd[semkey] = val
            waits.append((semkey, val))
        return waits

    def _commit(self, tok, reads, writes):
        for r in reads:
            lst = self.readers.setdefault(r, [])
            lst.append(tok)
            if len(lst) > 64:
                best = {}
                for t in lst:
                    if t[0] not in best or best[t[0]][1] < t[1]:
                        best[t[0]] = t
                self.readers[r] = list(best.values())
        for w in writes:
            self.lastw[w] = tok
            self.readers[w] = []

    def op(self, eng, fn, reads=(), writes=()):
        reads = [self._key(r) for r in reads]
        writes = [self._key(w) for w in writes]
        deps = self._deps(reads, writes)
        waits = self._waits(eng, deps, skip_pe=True)
        self.cnt[eng] += 1
        tok = (eng, self.cnt[eng], eng)
        self.ops[eng].append((fn, waits, (eng, 1)))
        self._commit(tok, reads, writes)
        return tok

    def dma(self, fn, reads=(), writes=(), q="sp", inc=16):
        reads = [self._key(r) for r in reads]
        writes = [self._key(w) for w in writes]
        deps = self._deps(reads, writes)
        k = self.ndma[q]
        self.ndma[q] += 1
        semkey = ("dma", q, k % NDMA_SEM)
        prev = self.semval.get(semkey, 0)
        val = prev + inc
        self.semval[semkey] = val
        if prev > 0:
            deps.append((semkey, prev, "dma"))
        waits = self._waits(q, deps)
        tok = (semkey, val, "dma")
        self.ops[q].append((fn, waits, (semkey, inc)))
        self._commit(tok, reads, writes)
        self.dma_tokens[q].append(tok)
        return tok

    def barrier(self):
        toks = [(e, self.cnt[e], e) for e in self.ENGS if self.cnt[e] > 0]
        for q in ("sp", "pool"):
            toks.extend(self.dma_tokens[q][-NDMA_SEM:])
        for e in self.ENGS:
            waits = self._waits(e, toks)
            if waits:
                self.ops[e].append((None, waits, None))
        self.lastw = {}
        self.readers = {}

    def emit(self, stack):
        nc = self.nc
        self.barrier()
        semnames = set(self.ENGS)
        for q in ("sp", "pool"):
            for i in range(min(self.ndma[q], NDMA_SEM)):
                semnames.add(("dma", q, i))
        for sk in sorted(semnames, key=str):
            nm = sk if isinstance(sk, str) else "d_%s_%d" % (sk[1], sk[2])
            self.sems[sk] = stack.enter_context(nc.semaphore("s_" + nm))
        block = stack.enter_context(nc.Block())
        sems = self.sems

        def run(e, lst):
            for (fn, waits, inc) in lst:
                for (sk, val) in waits:
                    e.wait_ge(sems[sk], val)
                if fn is not None:
                    fn(e).then_inc(sems[inc[0]], inc[1])

        @block.tensor
        def _(e):
            run(e, self.ops["pe"])

        @block.scalar
        def _(e):
            run(e, self.ops["act"])

        @block.vector
        def _(e):
            run(e, self.ops["dve"])

        @block.gpsimd
        def _(e):
            run(e, self.ops["pool"])

        @block.sync
        def _(e):
            run(e, self.ops["sp"])


D = 1024
C_Z, C_XBC, C_DT, C_CQ, C_CKV, C_KR, C_U, C_V = 0, 512, 1280, 1288, 1544, 1672, 1704, 1960


class Cfg:
    def __init__(self, depth=2, np_=16, npg=64, nphys=10240, phases="ACDE"):
        self.DEPTH, self.NP, self.NPG, self.NPHYS, self.phases = depth, np_, npg, nphys, phases
        self.NT = np_ + 1
        self.GRP = 4
        self.NC = 8


DBG = set()

def build(cfg):
    NP, NT, DEPTH, GRP, NPG, NPHYS = cfg.NP, cfg.NT, cfg.DEPTH, cfg.GRP, cfg.NPG, cfg.NPHYS
    NPT = NP * 128
    nc = bass.Bass("TRN2", target_bir_lowering=False)
    I = {}
    O = {}

    def inp(name, shape, dt=F32):
        I[name] = nc.dram_tensor(name, list(shape), dt, kind="ExternalInput")
        return I[name]

    def outp(name, shape, dt=F32):
        O[name] = nc.dram_tensor(name, list(shape), dt, kind="ExternalOutput")
        return O[name]

    def scr(name, shape, dt=F32):
        return nc.dram_tensor(name, list(shape), dt, kind="Internal")

    inp("xin", [NT * 128, D]); inp("consts", [8, 128, 128]); inp("blkind", [128, 16]); inp("rope", [NT * 128, 32]); inp("vis", [128, 12])
    inp("w_in", [DEPTH, D, 2216]); inp("g_mix", [DEPTH, D]); inp("g_ckv", [DEPTH, 128]); inp("g_cq", [DEPTH, 256])
    inp("w_uq", [DEPTH, 256, 384]); inp("w_uk", [DEPTH, 128, 256]); inp("w_ukT", [DEPTH, 4, 64, 128]); inp("w_uv", [DEPTH, 128, 256])
    inp("gq96", [DEPTH, 96]); inp("gk96", [DEPTH, 96])
    inp("conv_wT", [DEPTH, 768, 4]); inp("conv_b", [DEPTH, 768]); inp("dt_bias", [DEPTH, 8]); inp("a_log", [DEPTH, 8]); inp("d_skip", [DEPTH, 8])
    inp("g_ssd_out", [DEPTH, 512]); inp("state_conv", [DEPTH, 16, 3, 768]); inp("state_ssm", [DEPTH, 16, 8, 64, 64])
    inp("gm_ln_g", [DEPTH, 256]); inp("gm_ln_b", [DEPTH, 256]); inp("gm_wsT", [DEPTH, 2, 4, 128, 128]); inp("gm_bsT", [DEPTH, 2, 128, 4])
    inp("mem_rows", [128, D]); inp("g_memm_l", [1, D]); inp("g_mk_l", [1, 256]); inp("w_mk_l", [D, D]); inp("w_mv_l", [D, D])
    outp("mem_k", [128, D]); outp("mem_v", [128, D])
    inp("mem_full", [256, D]); inp("g_memm", [DEPTH, D]); inp("g_mk", [DEPTH, 256]); inp("w_mk", [DEPTH, D, D]); inp("w_mv", [DEPTH, D, D])
    inp("g_memx", [DEPTH, D]); inp("w_mq", [DEPTH, D, D]); inp("g_mq", [DEPTH, 256]); inp("w_mo", [DEPTH, D, D])
    inp("cache_mem_k", [DEPTH, 16, 256, D]); inp("cache_mem_v", [DEPTH, 16, 256, D]); inp("colmask", [128, 16, 128])
    inp("g_ffn", [DEPTH, D]); inp("w_pq", [DEPTH, D, 2048]); inp("peer_kT", [DEPTH, 128, 16, 128])
    inp("peer_uT", [DEPTH, D, 16384]); inp("peer_v", [DEPTH, 16384, D])
    inp("w_out", [DEPTH, D, D]); inp("page_idx", [1, 16 * NPG], I32)
    inp("cache_ckv", [DEPTH, NPHYS * 128, 128]); inp("cache_kr", [DEPTH, NPHYS * 128, 32])
    outp("y", [NT * 128, D]); outp("new_ckv", [DEPTH, NT * 128, 128]); outp("new_kr", [DEPTH, NT * 128, 32])
    outp("new_conv", [DEPTH, 17, 3, 768]); outp("new_gmv", [DEPTH, 2 * 128, 256]); outp("new_ssm", [DEPTH, 17, 8, 64, 64])

    UTb = scr("UTb", [D, 16384], BF16); Vb = scr("Vb", [16384, D], BF16)
    s_d = scr("s_d", [NT, 128, 2048]); hT_d = scr("hT_d", [NT, 128, 1024], BF16)
    ccH = scr("ccH", [3, 768]); ccHg = scr("ccHg", [GRP * 3, 768])
    ccS = scr("ccS", [128, 264]); ccSg = scr("ccSg", [GRP * 128, 264])
    ctm_d = scr("ctm_d", [NT, 128, 2, 128], BF16); hs_d = scr("hs_d", [16, 128, 256]); ys_d = scr("ys_d", [NT, 128, 512]); zs_d = scr("zs_d", [NT, 128, 512]); ygm_d = scr("ygm_d", [NT, 128, 256], BF16)
    qlt_d = scr("qlt_d", [NT, 128, 512], BF16); qrt_d = scr("qrt_d", [NT, 32, 512], BF16)
    ccKT = scr("ccKT", [160, NPT], BF16); ccKTg = scr("ccKTg", [GRP * 160, NPT], BF16)
    ccKV = scr("ccKV", [NPT, 128], BF16); ccKVg = scr("ccKVg", [GRP * NPT, 128], BF16)
    ccRK = scr("ccRK", [NPT, 4]); ccRKg = scr("ccRKg", [GRP * NPT, 4])

    P = Prog(nc)
    with ExitStack() as st:
        uniq = [0]

        def sbt(stack, name, shape, dt=F32):
            uniq[0] += 1
            return stack.enter_context(nc.sbuf_tensor("%s_%d" % (name, uniq[0]), list(shape), dt))

        def dma(out, in_, q="sp", **kw):
            P.dma(lambda e: e.dma_start(out=out, in_=in_, **kw), reads=[in_], writes=[out], q=q)

        def mm(out, lhsT, rhs, start=True, stop=True):
            P.op("pe", lambda e: e.matmul(out, lhsT=lhsT, rhs=rhs, start=start, stop=stop), reads=[lhsT, rhs], writes=[out])

        def tr(out, in_, ident):
            P.op("pe", lambda e: e.transpose(out=out, in_=in_, identity=ident), reads=[in_, ident], writes=[out])

        def actf(out, in_, func, bias=None, scale=None, accum=None):
            kw = {}
            if bias is not None:
                kw["bias"] = bias
            if scale is not None:
                kw["scale"] = scale
            if accum is not None:
                kw["accum_out"] = accum
            r = [in_] + [a for a in (bias, scale) if a is not None and not isinstance(a, (int, float))]
            w = [out] + ([accum] if accum is not None else [])
            P.op("act", lambda e: e.activation(out=out, in_=in_, func=func, **kw), reads=r, writes=w)

        def cp(eng, out, in_):
            if eng == "act":
                P.op("act", lambda e: e.copy(out=out, in_=in_), reads=[in_], writes=[out])
            else:
                P.op(eng, lambda e: e.tensor_copy(out=out, in_=in_), reads=[in_], writes=[out])

        def tt(eng, out, a, b, op):
            P.op(eng, lambda e: e.tensor_tensor(out=out, in0=a, in1=b, op=op), reads=[a, b], writes=[out])

        def ts(eng, out, a, s1, op0, s2=None, op1=None):
            r = [a] + [s for s in (s1, s2) if s is not None and not isinstance(s, (int, float))]
            kw = {"op1": op1} if op1 is not None else {}
            P.op(eng, lambda e: e.tensor_scalar(out=out, in0=a, scalar1=s1, scalar2=s2, op0=op0, **kw), reads=r, writes=[out])

        def stt(out, a, s, b, op0, op1):
            r = [a, b] + ([s] if not isinstance(s, (int, float)) else [])
            P.op("dve", lambda e: e.scalar_tensor_tensor(out=out, in0=a, scalar=s, in1=b, op0=op0, op1=op1), reads=r, writes=[out])

        def red(out, in_, op=ALU.add):
            P.op("dve", lambda e: e.tensor_reduce(out=out, in_=in_, axis=AX.X, op=op), reads=[in_], writes=[out])

        def recip(out, in_):
            P.op("dve", lambda e: e.reciprocal(out=out, in_=in_), reads=[in_], writes=[out])

        def memset(eng, ap, val):
            P.op(eng, lambda e: e.memset(ap, val), writes=[ap])

        def bc_row(dt_, off, n):
            return bass.AP(dt_, off, [[0, 128], [1, n]])

        xres = sbt(st, "xres", [128, NT, D])
        cst = sbt(st, "cst", [128, 8, 128]); cstb = sbt(st, "cstb", [128, 8, 128], BF16)
        blkind = sbt(st, "blkind_sb", [128, 16])
        rope = sbt(st, "rope_sb", [128, NT, 32]); vis = sbt(st, "vis_sb", [128, 12])
        epsT = sbt(st, "epsT", [128, 1]); oneT = sbt(st, "oneT", [128, 1])
        junk = sbt(st, "junk", [128, D], BF16)
        sA = sbt(st, "sA", [128, 16]); sB = sbt(st, "sB", [128, 16])
        h = sbt(st, "h", [128, D], BF16); hT = sbt(st, "hT", [128, 8, 128], BF16)
        ps = [st.enter_context(nc.psum_tensor("ps%d" % i, [128, 512], F32)) for i in range(8)]
        IDf, IDb = cst[:, 0, :], cstb[:, 0, :]
        dma(xres[:], I["xin"].ap().rearrange("(k p) d -> p k d", p=128))
        dma(cst[:], I["consts"].ap().rearrange("k p n -> p k n"))
        dma(blkind[:], I["blkind"].ap())
        dma(rope[:], I["rope"].ap().rearrange("(k p) d -> p k d", p=128))
        dma(vis[:], I["vis"].ap())
        cp("dve", cstb[:], cst[:])
        memset("dve", epsT[:], EPS); memset("dve", oneT[:], 1.0)

        def rstd_of(out, ss, n):
            actf(out, ss, AF.Sqrt, bias=epsT[:, 0:1], scale=1.0 / n)
            recip(out, out)

        def norm_rows(dst, src, gain, n, col):
            actf(junk[:, 0:n], src, AF.Square, accum=sA[:, col:col + 1])
            rstd_of(sB[:, col:col + 1], sA[:, col:col + 1], n)
            stt(dst, src, sB[:, col:col + 1], gain, ALU.mult, ALU.mult)

        def to_featmajor(dstT, src_bf16, nchunk=8):
            pT = ps[0][:].bitcast(BF16).rearrange("p (k t) -> p k t", k=8)
            for c in range(nchunk):
                tr(pT[:, c, :], src_bf16[:, c * 128:(c + 1) * 128], IDb)
            cp("act", dstT[:, 0:nchunk, :], pT[:, 0:nchunk, :])

        def load_w(wst, wb, dram_ap2d, ncols, kch=8, step=280):
            src = dram_ap2d.rearrange("(k p) n -> p k n", p=128)
            for i, c0 in enumerate(range(0, ncols, step)):
                c1 = min(ncols, c0 + step)
                dma(wst[:, 0:kch, 0:c1 - c0], src[:, :, c0:c1])
                cp("pool" if i % 2 == 0 else "dve", wb[:, 0:kch, c0:c1], wst[:, 0:kch, 0:c1 - c0])

        def proj(dst_f32, srcT, wb, ncols, kch=8):
            for nb in range((ncols + 511) // 512):
                n0, n1 = nb * 512, min(ncols, nb * 512 + 512)
                pp = ps[1 + (nb % 2)]
                for c in range(kch):
                    mm(pp[:, 0:n1 - n0], srcT[:, c, :], wb[:, c, n0:n1], start=(c == 0), stop=(c == kch - 1))
                cp("dve" if nb % 2 == 0 else "act", dst_f32[:, n0:n1], pp[:, 0:n1 - n0])

        with ExitStack() as s0:
            wst0 = sbt(s0, "wst0", [128, 8, 280]); wb0 = sbt(s0, "wb0", [128, 8, 1024], BF16)
            gB0 = sbt(s0, "gB0", [128, D]); g20 = sbt(s0, "g20", [128, 256])
            memx = sbt(s0, "memx", [128, D]); kv0 = sbt(s0, "kv0", [128, D]); kvo = sbt(s0, "kvo", [128, D])
            dma(memx[:], I["mem_rows"].ap()); dma(gB0[:], bc_row(I["g_memm_l"], 0, D)); dma(g20[:], bc_row(I["g_mk_l"], 0, 256))
            norm_rows(h[:], memx[:], gB0[:], D, 0)
            to_featmajor(hT, h)
            load_w(wst0, wb0, I["w_mk_l"].ap(), D)
            proj(kv0, hT, wb0, D)
            for hd in range(4):
                norm_rows(kvo[:, hd * 256:(hd + 1) * 256], kv0[:, hd * 256:(hd + 1) * 256], g20[:], 256, 1 + hd)
            dma(O["mem_k"].ap(), kvo[:])
            load_w(wst0, wb0, I["w_mv_l"].ap(), D)
            proj(kv0, hT, wb0, D)
            dma(O["mem_v"].ap(), kv0[:])
            P.barrier()
        P.barrier()
        for l in range(DEPTH):
            with ExitStack() as sm:
                wst = sbt(sm, "wst", [128, 8, 280])
                wb = sbt(sm, "wb", [128, 8, 2216], BF16)
                ecor = sbt(sm, "ecor", [128, NT, 8])
                CT1 = sbt(sm, "CT1", [128, 2, 128], BF16)
                Hinb = sbt(sm, "Hinb", [128, 256], BF16)
                wuv = sbt(sm, "wuv", [128, 256], BF16)
                wuk2 = sbt(sm, "wuk2", [128, 256], BF16)
                KT1 = sbt(sm, "KT1", [128, 2, 128], BF16)
                KRT1 = sbt(sm, "KRT1", [32, 2, 128], BF16)
                KV1 = sbt(sm, "KV1", [128, 2, 129], BF16)
                RK1 = sbt(sm, "RK1", [128, 2, 4])
                sa = ExitStack(); sa.__enter__()
                gB = sbt(sa, "gB", [128, D])
                g2 = sbt(sa, "g2", [128, 1024])
                tok = sbt(sa, "tok", [128, 2216])
                ckv = sbt(sa, "ckv", [128, 128])
                kr = sbt(sa, "kr", [128, 32])
                r4 = sbt(sa, "r4", [128, 4, 64])
                xT = sbt(sa, "xT", [128, 6, 16, 11])
                xTp = sbt(sa, "xTp", [128, 6, 131])
                cwT = sbt(sa, "cwT", [128, 6, 4])
                cbT = sbt(sa, "cbT", [128, 6])
                acc = sbt(sa, "acc", [128, 128])
                xcT = sbt(sa, "xcT", [128, 6, 128])
                haloA = sbt(sa, "haloA", [128, GRP, 6, 3])
                hrow = None
                u_t = sbt(sa, "u_t", [128, 256])
                v_t = sbt(sa, "v_t", [128, 256])
                vn = sbt(sa, "vn", [128, 256])
                vnb = sbt(sa, "vnb", [128, 256], BF16)
                wm = tok[:, 0:1024].rearrange("p (s g i) -> p s g i", s=2, g=4)
                wmb = sbt(sa, "wmb", [128, 2, 4, 128], BF16)
                bsT = sbt(sa, "bsT", [128, 2, 4])
                dtb = sbt(sa, "dtb", [128, 8])
                aB = sbt(sa, "aB", [128, 8])
                dskB = sbt(sa, "dskB", [128, 8])
                dt = sbt(sa, "dt", [128, 8])
                dtA = sbt(sa, "dtA", [128, 8])
                eac = sbt(sa, "eac", [128, 8])
                te = sbt(sa, "te", [128, 8])
                cd = sbt(sa, "cd", [128, 8])
                cumD = sbt(sa, "cumD", [128, 8])
                Rm = sbt(sa, "Rm", [128, 8, 128])
                sc = sbt(sa, "sc", [128, 2, 128])
                dec = sbt(sa, "dec", [128, 8, 128], BF16)
                MT = sbt(sa, "MT", [128, 8, 128], BF16)
                xst = sbt(sa, "xst", [128, 512])
                Btok = sbt(sa, "Btok", [128, 128], BF16)
                BTm = sbt(sa, "BTm", [128, 2, 128], BF16)
                xdt = sbt(sa, "xdt", [128, 512], BF16)
                xte = sbt(sa, "xte", [128, 512], BF16)
                HT2 = sbt(sa, "HT2", [128, 256])
                HT2b = sbt(sa, "HT2b", [128, 256], BF16)
                ys = sbt(sa, "ys", [128, 512])
                ytmp = sbt(sa, "ytmp", [128, 512])
                SG = sbt(sa, "SG", [128, 264])
                Hin = sbt(sa, "Hin", [128, 256])
                dmr = sbt(sa, "dmr", [128, 8])
                Sm = sbt(sa, "Sm", [128, 256])
                Hf = sbt(sa, "Hf", [128, 256])
                Ho = None
                Hnat = None
                HS1 = sbt(sa, "HS1", [128, 256])
                HS1b = sbt(sa, "HS1b", [128, 256], BF16)
                CZ = [sbt(sa, "CZ%d" % i, [128, 2, 128], BF16) for i in range(2)]
                BZ = [sbt(sa, "BZ%d" % i, [128, 128], BF16) for i in range(2)]
                Wsel = sbt(sa, "Wsel", [128, 16, 8])
                cdS = sbt(sa, "cdS", [128, 16, 8])
                cqb = sbt(sa, "cqb", [128, 256], BF16)
                cqT = sbt(sa, "cqT", [128, 2, 128], BF16)
                wuq = sbt(sa, "wuq", [128, 2, 384], BF16)
                wuk = sbt(sa, "wuk", [128, 256], BF16)
                wukT = sbt(sa, "wukT", [64, 4, 128], BF16)
                gqk = sbt(sa, "gqk", [128, 96])
                q = sbt(sa, "q", [128, 4, 96])
                q2 = sbt(sa, "q2", [128, 4, 96])
                qgb = sbt(sa, "qgb", [128, 4, 96], BF16)
                qnT = sbt(sa, "qnT", [64, 4, 128], BF16)
                qrT = sbt(sa, "qrT", [32, 4, 128], BF16)
                qlT = sbt(sa, "qlT", [128, 512], BF16)
                ckvb = sbt(sa, "ckvb", [128, 128], BF16)
                krb = sbt(sa, "krb", [128, 32], BF16)
                ygb = sbt(sa, "ygb", [128, 256], BF16)
                dma(tok[:, 0:768].rearrange("p (k n) -> p k n", k=2), I["w_uq"].ap()[l].rearrange("(k p) n -> p k n", p=128))
                cp("dve", wuq[:], tok[:, 0:768].rearrange("p (k n) -> p k n", k=2))
                dma(tok[:, 768:1024], I["w_uk"].ap()[l]); cp("dve", wuk[:], tok[:, 768:1024]); cp("dve", wuk2[:], tok[:, 768:1024])
                dma(tok[:, 1024:1280], I["w_uv"].ap()[l]); cp("dve", wuv[:], tok[:, 1024:1280])
                dma(tok[0:64, 1280:1792].rearrange("p (h c) -> p h c", h=4), I["w_ukT"].ap()[l].rearrange("h d c -> d h c"))
                cp("dve", wukT[:], tok[0:64, 1280:1792].rearrange("p (h c) -> p h c", h=4))
                dma(g2[:, 640:896], bc_row(I["g_cq"], l * 256, 256))
                dma(gqk[:], bc_row(I["gq96"], l * 96, 96)); dma(g2[:, 896:992], bc_row(I["gk96"], l * 96, 96))
                tt("dve", gqk[:], gqk[:], g2[:, 896:992], ALU.mult)
                memset("dve", KV1[:, :, 128:129], 1.0)
                Ho = xst[0:64, :]; Hnat = ytmp[0:64, :]
                hrow = xcT[0:48, :, :].rearrange("p m t -> p (m t)")
                dma(dtb[:], bc_row(I["dt_bias"], l * 8, 8)); dma(aB[:], bc_row(I["a_log"], l * 8, 8)); dma(dskB[:], bc_row(I["d_skip"], l * 8, 8))
                actf(aB[:], aB[:], AF.Exp)
                ts("dve", aB[:], aB[:], -1.0, ALU.mult)
                memset("dve", HT2[:], 0.0); memset("dve", HT2b[:], 0.0); memset("dve", cumD[:], 1.0); pass
                def load_hs(b):
                    dma(Hnat.rearrange("p (h n) -> p h n", h=8), I["state_ssm"].ap()[l, b].rearrange("h p n -> p h n"))
                    for hh in range(8):
                        if hh < 4:
                            tr(ps[6][0:64, hh * 64:(hh + 1) * 64], Hnat[:, hh * 64:(hh + 1) * 64], IDf[0:64, 0:64])
                        else:
                            tr(ps[6][:, 256 + (hh - 4) * 64:256 + (hh - 3) * 64], Hnat[:, hh * 64 - 64:hh * 64 + 64], IDf[0:64, 0:64])
                    cp("act", HS1[0:64, :], ps[6][0:64, 0:256])
                    cp("dve", HS1[64:128, :], ps[6][64:128, 256:512])
                load_w(wst, wb, I["w_in"].ap()[l], 2216)
                dma(gB[:], bc_row(I["g_mix"], l * D, D))
                dma(g2[:, 0:128], bc_row(I["g_ckv"], l * 128, 128))
                dma(g2[:, 128:384], bc_row(I["gm_ln_g"], l * 256, 256))
                dma(g2[:, 384:640], bc_row(I["gm_ln_b"], l * 256, 256))
                dma(cwT[:], I["conv_wT"].ap()[l].rearrange("(m c) w -> c m w", c=128))
                dma(cbT[:], I["conv_b"].ap()[l].rearrange("(m c) -> c m", c=128), allow_slow_non_contiguous=True)
                for s_ in range(2):
                    dma(wm[:, s_], I["gm_wsT"].ap()[l, s_].rearrange("g j i -> j g i"))
                dma(bsT[:], I["gm_bsT"].ap()[l].rearrange("s i g -> i s g"))
                tt("dve", wm[:, 0], wm[:, 0], cst[:, 1:2, :].to_broadcast([128, 4, 128]), ALU.mult)
                tt("dve", wm[:, 1], wm[:, 1], cst[:, 4:5, :].to_broadcast([128, 4, 128]), ALU.mult)
                cp("dve", wmb[:], wm)

                def ssm_out(l_, slot):
                    for hl in range(4):
                        tr(ps[3][0:64, hl * 128:(hl + 1) * 128], Hf[:, hl * 64:(hl + 1) * 64], IDf)
                    cp("act", Ho.rearrange("p (g hl n) -> p hl g n", g=2, hl=4), ps[3][0:64, :].rearrange("p (hl g n) -> p hl g n", hl=4, g=2))
                    dma(O["new_ssm"].ap()[l_, slot].rearrange("h p n -> p h n"), Ho.rearrange("p (h n) -> p h n", h=8))

                def front(k):
                    norm_rows(h[:], xres[:, k, :], gB[:], D, 0)
                    to_featmajor(hT, h)
                    proj(tok, hT, wb, 2216)

                front(NP - 1)
                dma(ccH.ap(), tok[125:128, C_XBC:C_XBC + 768])
                dma(O["new_conv"].ap()[l, 16], tok[125:128, C_XBC:C_XBC + 768])
                P.dma(lambda e: e.collective_compute("AllGather", ALU.bypass, replica_groups=[[0, 1, 2, 3], [4, 5, 6, 7]],
                                                     ins=[ccH.ap()], outs=[ccHg.ap()]), reads=["ccH"], writes=["ccHg"], q="pool", inc=1)
                dma(hrow[0:GRP * 3, :], ccHg.ap())
                for m in range(6):
                    tr(ps[6][:, m * 48:m * 48 + GRP * 3], hrow[0:GRP * 3, m * 128:(m + 1) * 128], IDf[0:GRP * 3, 0:GRP * 3])
                cp("act", haloA[:].rearrange("p r m j -> p m r j"), ps[6][:, 0:288].rearrange("p (m x) -> p m x", m=6)[:, :, 0:GRP * 3].rearrange("p m (r j) -> p m r j", j=3))
                ts("dve", xTp[:, :, 0:3], haloA[:, 0], vis[:, 8:9], ALU.mult)
                for r in range(1, GRP):
                    stt(xTp[:, :, 0:3], haloA[:, r], vis[:, 8 + r:9 + r], xTp[:, :, 0:3], ALU.mult, ALU.add)
                dma(hrow[0:48, :], I["state_conv"].ap()[l].rearrange("b j c -> (b j) c"))
                for m in range(6):
                    tr(ps[7][:, m * 48:m * 48 + 48], hrow[0:48, m * 128:(m + 1) * 128], IDf[0:48, 0:48])
                cp("act", xT[:, :, :, 0:3], ps[7][:, 0:288].rearrange("p (m b j) -> p m b j", m=6, j=3))

                for k in range(NT):
                    smp = (k == NP)
                    if k != NP - 1 or NP == 1:
                        front(k)
                    elif NP > 1:
                        front(k)
                    norm_rows(ckv[:], tok[:, C_CKV:C_CKV + 128], g2[:, 0:128], 128, 1)
                    x1, x2 = tok[:, C_KR:C_KR + 16], tok[:, C_KR + 16:C_KR + 32]
                    cs, sn = rope[:, k, 0:16], rope[:, k, 16:32]
                    tt("dve", r4[:, 0, 0:16], x1, cs, ALU.mult); tt("dve", r4[:, 1, 0:16], x2, sn, ALU.mult)
                    tt("dve", kr[:, 0:16], r4[:, 0, 0:16], r4[:, 1, 0:16], ALU.subtract)
                    tt("dve", r4[:, 2, 0:16], x2, cs, ALU.mult); tt("dve", r4[:, 3, 0:16], x1, sn, ALU.mult)
                    tt("dve", kr[:, 16:32], r4[:, 2, 0:16], r4[:, 3, 0:16], ALU.add)
                    dma(O["new_ckv"].ap()[l, k * 128:(k + 1) * 128, :], ckv[:])
                    dma(O["new_kr"].ap()[l, k * 128:(k + 1) * 128, :], kr[:])
                    actf(u_t[:], tok[:, C_U:C_U + 256], AF.Gelu_apprx_tanh)
                    actf(v_t[:], tok[:, C_V:C_V + 256], AF.Gelu_apprx_tanh)
                    v3 = v_t[:].rearrange("p (g d) -> p g d", g=4)
                    red(sA[:, 4:8], v3)
                    ts("dve", sA[:, 4:8], sA[:, 4:8], -1.0 / 64, ALU.mult)
                    tt("dve", vn[:].rearrange("p (g d) -> p g d", g=4), v3, sA[:, 4:8].unsqueeze(2).to_broadcast([128, 4, 64]), ALU.add)
                    tt("dve", r4[:], vn[:].rearrange("p (g d) -> p g d", g=4), vn[:].rearrange("p (g d) -> p g d", g=4), ALU.mult)
                    red(sA[:, 8:12], r4[:])
                    rstd_of(sB[:, 8:12], sA[:, 8:12], 64)
                    tt("dve", vn[:].rearrange("p (g d) -> p g d", g=4), vn[:].rearrange("p (g d) -> p g d", g=4),
                       sB[:, 8:12].unsqueeze(2).to_broadcast([128, 4, 64]), ALU.mult)
                    tt("dve", vn[:], vn[:], g2[:, 128:384], ALU.mult)
                    tt("dve", vn[:], vn[:], g2[:, 384:640], ALU.add)
                    if k >= NP - 1:
                        dma(O["new_gmv"].ap()[l, (k - NP + 1) * 128:(k - NP + 2) * 128, :], vn[:])
                    cp("dve", vnb[:], vn[:])
                    si = 1 if smp else 0
                    for g in range(4):
                        mm(ps[3][:, g * 64:(g + 1) * 64], wmb[:, si, g, :], vnb[:, g * 64:(g + 1) * 64])
                    tt("dve", r4[:], ps[3][:, 0:256].rearrange("p (g d) -> p g d", g=4), bsT[:, si, :].unsqueeze(2).to_broadcast([128, 4, 64]), ALU.add)
                    tt("dve", ygb[:], r4[:].rearrange("p g d -> p (g d)"), u_t[:], ALU.mult)
                    dma(ygm_d.ap()[k], ygb[:])
                    dma(zs_d.ap()[k], tok[:, C_Z:C_Z + 512])
                    cp("pool", ckvb[:], ckv[:]); cp("pool", krb[:], kr[:])
                    cp("pool", KV1[:, si, 0:128], ckv[:])
                    pT0 = ps[0][:].bitcast(BF16)
                    tr(pT0[:, 0:128], ckvb[:], IDb)
                    tr(pT0[0:32, 128:256], krb[:], IDb)
                    cp("act", KT1[:, si, :], pT0[:, 0:128]); cp("act", KRT1[:, si, :], pT0[0:32, 128:256])
                    mm(ps[2][:, 0:256], KT1[:, si, :], wuk[:])
                    actf(r4[:].rearrange("p g d -> p (g d)"), ps[2][:, 0:256], AF.Square)
                    red(sA[:, 12:16], r4[:])
                    actf(junk[:, 0:32], kr[:], AF.Square, accum=sA[:, 2:3])
                    ts("dve", sA[:, 12:16], sA[:, 12:16], sA[:, 2:3], ALU.add)
                    rstd_of(sB[:, 12:16], sA[:, 12:16], 96)
                    ts("dve", RK1[:, si, :], sB[:, 12:16], 96 ** -0.5, ALU.mult)
                    if not smp:
                        dma(ccKT.ap()[0:128, k * 128:(k + 1) * 128], KT1[:, 0, :]); dma(ccKT.ap()[128:160, k * 128:(k + 1) * 128], KRT1[:, 0, :])
                        dma(ccKV.ap()[k * 128:(k + 1) * 128, :], KV1[:, 0, 0:128]); dma(ccRK.ap()[k * 128:(k + 1) * 128, :], RK1[:, 0, :])
                    norm_rows(cqb[:], tok[:, C_CQ:C_CQ + 256], g2[:, 640:896], 256, 3)
                    to_featmajor(cqT, cqb, 2)
                    for c in range(2):
                        mm(ps[1][:, 0:384], cqT[:, c, :], wuq[:, c, :], start=(c == 0), stop=(c == 1))
                    cp("dve", q[:].rearrange("p h d -> p (h d)"), ps[1][:, 0:384])
                    qx1, qx2 = q[:, :, 64:80], q[:, :, 80:96]
                    csb = cs.unsqueeze(1).to_broadcast([128, 4, 16]); snb = sn.unsqueeze(1).to_broadcast([128, 4, 16])
                    tt("dve", r4[:, :, 0:16], qx1, csb, ALU.mult); tt("dve", r4[:, :, 16:32], qx2, snb, ALU.mult)
                    tt("dve", r4[:, :, 32:48], qx2, csb, ALU.mult); tt("dve", r4[:, :, 48:64], qx1, snb, ALU.mult)
                    tt("dve", qx1, r4[:, :, 0:16], r4[:, :, 16:32], ALU.subtract)
                    tt("dve", qx2, r4[:, :, 32:48], r4[:, :, 48:64], ALU.add)
                    tt("dve", q2[:], q[:], q[:], ALU.mult)
                    red(sA[:, 12:16], q2[:])
                    rstd_of(sB[:, 12:16], sA[:, 12:16], 96)
                    tt("dve", q2[:], q[:], sB[:, 12:16].unsqueeze(2).to_broadcast([128, 4, 96]), ALU.mult)
                    tt("dve", qgb[:], q2[:], gqk[:].unsqueeze(1).to_broadcast([128, 4, 96]), ALU.mult)
                    pq = ps[0][:].bitcast(BF16).rearrange("p (k t) -> p k t", k=8)
                    for hh in range(4):
                        tr(pq[0:64, hh, :], qgb[:, hh, 0:64], IDb)
                        tr(pq[0:32, 4 + hh, :], qgb[:, hh, 64:96], IDb)
                    cp("act", qnT[:], pq[0:64, 0:4, :]); cp("act", qrT[:], pq[0:32, 4:8, :])
                    for hh in range(4):
                        mm(ps[1][:, hh * 128:(hh + 1) * 128], wukT[:, hh, :], qnT[:, hh, :])
                    cp("dve", qlT[:], ps[1][:])
                    dma(qlt_d.ap()[k], qlT[:]); dma(qrt_d.ap()[k], qrT[:].rearrange("p h t -> p (h t)"))
                    pX = ps[4][:]
                    pX2 = ps[5][:]
                    for m in range(6):
                        dst = (pX if m < 4 else pX2)[:, (m % 4) * 128:(m % 4 + 1) * 128]
                        tr(dst, tok[:, C_XBC + m * 128:C_XBC + (m + 1) * 128], IDf)
                    if not smp:
                        cp("act", xTp[:, 0:4, 3:131], pX.rearrange("p (m t) -> p m t", m=4))
                        cp("act", xTp[:, 4:6, 3:131], pX2[:, 0:256].rearrange("p (m t) -> p m t", m=2))
                    else:
                        cp("act", xT[:, 0:4, :, 3:11], pX.rearrange("p (m b i) -> p m b i", m=4, b=16))
                        cp("act", xT[:, 4:6, :, 3:11], pX2[:, 0:256].rearrange("p (m b i) -> p m b i", m=2, b=16))
                        for b in range(16):
                            dma(O["new_conv"].ap()[l, b], tok[8 * b + 5:8 * b + 8, C_XBC:C_XBC + 768])
                    for m in range(6):
                        if not smp:
                            win = lambda w: xTp[:, m, w:w + 128]
                            a_ = acc[:]
                        else:
                            win = lambda w: xT[:, m, :, w:w + 8]
                            a_ = acc[:].rearrange("p (b i) -> p b i", b=16)
                        ts("dve", a_, win(0), cwT[:, m, 0:1], ALU.mult)
                        for w in range(1, 4):
                            stt(a_, win(w), cwT[:, m, w:w + 1], a_, ALU.mult, ALU.add)
                        actf(xcT[:, m, :], acc[:], AF.Silu, bias=cbT[:, m:m + 1])
                    if not smp and k + 1 < NP:
                        cp("pool", xTp[:, :, 0:3], xTp[:, :, 128:131])
                    if 'nossd' in DBG:
                        continue
                    TRIm, LSTm, ALLm = cst[:, 1 + 3 * si, :], cst[:, 2 + 3 * si, :], cst[:, 3 + 3 * si, :]
                    tt("dve", dt[:], tok[:, C_DT:C_DT + 8], dtb[:], ALU.add)
                    actf(dt[:], dt[:], AF.Exp)
                    actf(dt[:], dt[:], AF.Ln, bias=oneT[:, 0:1])
                    tt("dve", dtA[:], dt[:], aB[:], ALU.mult)
                    for m in range(4):
                        tr(ps[4][:, m * 128:(m + 1) * 128], xcT[:, m, :], IDf)
                    cp("act", xst[:], ps[4][:])
                    tr(ps[5][:, 0:128], xcT[:, 4, :], IDf)
                    cp("dve", Btok[:], ps[5][:, 0:128])
                    gmask = cst[:, 7, 0:2]
                    tt("pool", BTm[:], xcT[:, 4:5, :].to_broadcast([128, 2, 128]), gmask.unsqueeze(2).to_broadcast([128, 2, 128]), ALU.mult)
                    tt("pool", CT1[:], xcT[:, 5:6, :].to_broadcast([128, 2, 128]), gmask.unsqueeze(2).to_broadcast([128, 2, 128]), ALU.mult)
                    dma(ctm_d.ap()[k], CT1[:])
                    for g in range(2):
                        mm(ps[6][:, g * 128:(g + 1) * 128], BTm[:, g, :], CT1[:, g, :])
                    tt("dve", sc[:], ps[6][:, 0:256].rearrange("p (g i) -> p g i", g=2), TRIm.unsqueeze(1).to_broadcast([128, 2, 128]), ALU.mult)
                    tt("pool", Rm[:], TRIm.unsqueeze(1).to_broadcast([128, 8, 128]), dtA[:].unsqueeze(2).to_broadcast([128, 8, 128]), ALU.mult)
                    mm(ps[1][:], LSTm, Rm[:, 0:4, :].rearrange("p h i -> p (h i)"))
                    mm(ps[2][:], LSTm, Rm[:, 4:8, :].rearrange("p h i -> p (h i)"))
                    actf(dec[:, 0:4, :].rearrange("p h i -> p (h i)"), ps[1][:], AF.Exp)
                    actf(dec[:, 4:8, :].rearrange("p h i -> p (h i)"), ps[2][:], AF.Exp)
                    for g in range(2):
                        tt("dve", MT[:, 4 * g:4 * g + 4, :], dec[:, 4 * g:4 * g + 4, :], sc[:, g:g + 1, :].to_broadcast([128, 4, 128]), ALU.mult)
                    mm(ps[7][:, 0:8], TRIm, dtA[:])
                    mm(ps[7][:, 8:16], ALLm, dtA[:])
                    actf(eac[:], ps[7][:, 0:8], AF.Exp)
                    actf(cd[:], ps[7][:, 8:16], AF.Exp)
                    tt("dve", te[:], ps[7][:, 8:16], ps[7][:, 0:8], ALU.subtract) if False else None
                    cp("dve", te[:], ps[7][:, 0:8])
                    tt("dve", te[:], ps[7][:, 8:16], te[:], ALU.subtract)
                    actf(te[:], te[:], AF.Exp)
                    tt("dve", te[:], te[:], dt[:], ALU.mult)
                    x3 = xst[:].rearrange("p (h d) -> p h d", h=8)
                    tt("dve", xdt[:].rearrange("p (h d) -> p h d", h=8), x3, dt[:].unsqueeze(2).to_broadcast([128, 8, 64]), ALU.mult)
                    tt("pool", xte[:].rearrange("p (h d) -> p h d", h=8), x3, te[:].unsqueeze(2).to_broadcast([128, 8, 64]), ALU.mult)
                    for hh in range(8):
                        mm(ps[4][:, hh * 64:(hh + 1) * 64], MT[:, hh, :], xdt[:, hh * 64:(hh + 1) * 64])
                    if not smp:
                        for g in range(2):
                            mm(ps[5][:, g * 256:(g + 1) * 256], CT1[:, g, :], HT2b[:])
                    else:
                        for b in range(16):
                            load_hs(b)
                            dma(hs_d.ap()[b], HS1[:])
                            cp("pool", HS1b[:], HS1[:])
                            memset("pool", CZ[b % 2][:], 0.0)
                            cp("pool", CZ[b % 2][:, :, 8 * b:8 * b + 8], CT1[:, :, 8 * b:8 * b + 8])
                            for g in range(2):
                                mm((ps[5] if g == 0 else ps[2])[:, 0:256], CZ[b % 2][:, g, :], HS1b[:], start=(b == 0), stop=(b == 15))
                    for g in range(2):
                        yo = (ps[5][:, 256 * g:256 * g + 256] if not smp else (ps[5] if g == 0 else ps[2])[:, 0:256])
                        tt("dve", ys[:, 256 * g:256 * g + 256].rearrange("p (h d) -> p h d", h=4), yo.rearrange("p (h d) -> p h d", h=4),
                           eac[:, 4 * g:4 * g + 4].unsqueeze(2).to_broadcast([128, 4, 64]), ALU.mult)
                    tt("dve", ys[:], ys[:], ps[4][:], ALU.add)
                    tt("pool", ytmp[:].rearrange("p (h d) -> p h d", h=8), x3, dskB[:].unsqueeze(2).to_broadcast([128, 8, 64]), ALU.mult)
                    tt("dve", ys[:], ys[:], ytmp[:], ALU.add)
                    dma(ys_d.ap()[k], ys[:])
                    if not smp:
                        tt("dve", ecor[:, k, :], eac[:], cumD[:], ALU.mult)
                        tt("dve", cumD[:], cumD[:], cd[:], ALU.mult)
                        for g in range(2):
                            mm(ps[6][:, 256 * g:256 * g + 256], Btok[:], xte[:, 256 * g:256 * g + 256])
                        for g in range(2):
                            pg = slice(64 * g, 64 * g + 64)
                            tt("dve", HT2[pg, :].rearrange("p (h d) -> p h d", h=4), HT2[pg, :].rearrange("p (h d) -> p h d", h=4),
                               cd[pg, 4 * g:4 * g + 4].unsqueeze(2).to_broadcast([64, 4, 64]), ALU.mult)
                            tt("dve", HT2[pg, :], HT2[pg, :], ps[6][pg, 256 * g:256 * g + 256], ALU.add)
                        cp("pool", HT2b[:], HT2[:])
                    else:
                        tt("dve", Wsel[:], blkind[:].unsqueeze(2).to_broadcast([128, 16, 8]), dtA[:].unsqueeze(1).to_broadcast([128, 16, 8]), ALU.mult)
                        mm(ps[7][:, 16:144], cst[:, 3, :], Wsel[:].rearrange("p b h -> p (b h)"))
                        actf(cdS[:].rearrange("p b h -> p (b h)"), ps[7][:, 16:144], AF.Exp)
                        for b in range(16):
                            pb = ps[1 + (b % 2)]
                            dma(HS1[:], hs_d.ap()[b])
                            ts("pool", BZ[b % 2][:], Btok[:], blkind[:, b:b + 1], ALU.mult)
                            for g in range(2):
                                mm(pb[:, 256 * g:256 * g + 256], BZ[b % 2][:], xte[:, 256 * g:256 * g + 256])
                            for g in range(2):
                                pg = slice(64 * g, 64 * g + 64)
                                tt("dve", Hf[pg, :].rearrange("p (h d) -> p h d", h=4), HS1[pg, :].rearrange("p (h d) -> p h d", h=4),
                                   cdS[pg, b, 4 * g:4 * g + 4].unsqueeze(2).to_broadcast([64, 4, 64]), ALU.mult)
                                tt("dve", Hf[pg, :], Hf[pg, :], pb[pg, 256 * g:256 * g + 256], ALU.add)
                            ssm_out(l, b)
                if 'noxch' in DBG:
                    P.barrier(); sa.close(); continue
                dma(ccS.ap()[:, 0:256], HT2[:]); dma(ccS.ap()[:, 256:264], cumD[:])
                P.dma(lambda e: e.collective_compute("AllGather", ALU.bypass, replica_groups=[[0, 1, 2, 3], [4, 5, 6, 7]],
                                                     ins=[ccS.ap()], outs=[ccSg.ap()]), reads=["ccS"], writes=["ccSg"], q="pool", inc=1)
                memset("dve", Hin[:], 0.0)
                for r in range(GRP):
                    mcol = vis[:, 4 + r:5 + r]
                    dma(SG[:], ccSg.ap()[r * 128:(r + 1) * 128, :])
                    ts("dve", dmr[:], SG[:, 256:264], -1.0, ALU.add, mcol, ALU.mult)
                    ts("dve", dmr[:], dmr[:], 1.0, ALU.add)
                    ts("dve", Sm[:], SG[:, 0:256], mcol, ALU.mult)
                    for g in range(2):
                        pg = slice(64 * g, 64 * g + 64)
                        tt("dve", Hin[pg, :].rearrange("p (h d) -> p h d", h=4), Hin[pg, :].rearrange("p (h d) -> p h d", h=4),
                           dmr[pg, 4 * g:4 * g + 4].unsqueeze(2).to_broadcast([64, 4, 64]), ALU.mult)
                    tt("dve", Hin[:], Hin[:], Sm[:], ALU.add)
                cp("dve", Hinb[:], Hin[:])
                for g in range(2):
                    pg = slice(64 * g, 64 * g + 64)
                    tt("dve", Hf[pg, :].rearrange("p (h d) -> p h d", h=4), Hin[pg, :].rearrange("p (h d) -> p h d", h=4),
                       cumD[pg, 4 * g:4 * g + 4].unsqueeze(2).to_broadcast([64, 4, 64]), ALU.mult)
                tt("dve", Hf[:], Hf[:], HT2[:], ALU.add)
                ssm_out(l, 16)
                RG = [[0, 1, 2, 3], [4, 5, 6, 7]]
                for (a_, b_) in ((ccKT, ccKTg), (ccKV, ccKVg), (ccRK, ccRKg)):
                    P.dma(lambda e, a_=a_, b_=b_: e.collective_compute("AllGather", ALU.bypass, replica_groups=RG, ins=[a_.ap()], outs=[b_.ap()]),
                          reads=[a_.ap()], writes=[b_.ap()], q="pool", inc=1)
                P.barrier()
                sa.close()
                P.barrier()
                if 'C' not in cfg.phases:
                    continue
                sc_ = ExitStack(); sc_.__enter__()
                NKT = GRP * NP
                NKT = (GRP - 1) * NP
                KTg = sbt(sc_, "KTg", [128, GRP - 1, NPT], BF16); KRTg = sbt(sc_, "KRTg", [32, GRP - 1, NPT], BF16)
                KVg = sbt(sc_, "KVg", [128, NKT, 129], BF16); RKg = sbt(sc_, "RKg", [128, NKT, 4])
                KTl = sbt(sc_, "KTl", [128, NP, 128], BF16); KRTl = sbt(sc_, "KRTl", [32, NP, 128], BF16)
                KVl = sbt(sc_, "KVl", [128, NP, 129], BF16); RKl = sbt(sc_, "RKl", [128, NP, 4])
                qlt = [sbt(sc_, "qlt%d" % i, [128, 512], BF16) for i in range(2)]; qrt = [sbt(sc_, "qrt%d" % i, [32, 512], BF16) for i in range(2)]
                pTb = [sbt(sc_, "pTb%d" % i, [128, 4, 128], BF16) for i in range(2)]
                orec = sbt(sc_, "orec", [128, 4]); olat = sbt(sc_, "olat", [128, 4, 128], BF16); olatT = sbt(sc_, "olatT", [128, 4, 128], BF16)
                cat = sbt(sc_, "cat", [128, 1024], BF16); ysl = sbt(sc_, "ysl", [128, 512]); zl = sbt(sc_, "zl", [128, 512]); gsB = sbt(sc_, "gsB", [128, 512])
                ycor = sbt(sc_, "ycor", [128, 512])
                IDXi = sbt(sc_, "IDXi", [128, 16 * NPG], I32); IDXf = sbt(sc_, "IDXf", [128, 256]); iotaP = sbt(sc_, "iotaP", [128, 1], I32); iotaF = sbt(sc_, "iotaF", [128, 1])
                pgcb = [sbt(sc_, "pgcb%d" % i, [128, 129], BF16) for i in range(2)]; pgkb = [sbt(sc_, "pgkb%d" % i, [128, 32], BF16) for i in range(2)]
                cTs = [sbt(sc_, "cTs%d" % i, [128, 128], BF16) for i in range(2)]; kTs = [sbt(sc_, "kTs%d" % i, [32, 128], BF16) for i in range(2)]
                rks = [sbt(sc_, "rks%d" % i, [128, 8]) for i in range(2)]; pTs = [sbt(sc_, "pTs%d" % i, [128, 4, 8], BF16) for i in range(2)]
                sq = sbt(sc_, "sq", [128, 4, 64]); ols = sbt(sc_, "ols", [32, 128], BF16); odn = sbt(sc_, "odn", [32, 1])
                load_w(wst, wb, I["w_out"].ap()[l], 1024)
                dma(gsB[:], bc_row(I["g_ssd_out"], l * 512, 512))
                for r in range(GRP - 1):
                    dma(KTg[:, r, :], ccKTg.ap()[r * 160:r * 160 + 128, :])
                    dma(KRTg[:, r, :], ccKTg.ap()[r * 160 + 128:r * 160 + 160, :])
                dma(KVg[:, :, 0:128], ccKVg.ap()[0:NKT * 128, :].rearrange("(n p) c -> p n c", p=128))
                dma(RKg[:], ccRKg.ap()[0:NKT * 128, :].rearrange("(n p) c -> p n c", p=128))
                memset("dve", KVg[:, :, 128:129], 1.0); memset("dve", KVl[:, :, 128:129], 1.0)
                dma(KTl[:], ccKT.ap()[0:128, :].rearrange("p (k t) -> p k t", k=NP)); dma(KRTl[:], ccKT.ap()[128:160, :].rearrange("p (k t) -> p k t", k=NP))
                dma(KVl[:, :, 0:128], ccKV.ap().rearrange("(k p) c -> p k c", p=128)); dma(RKl[:], ccRK.ap().rearrange("(k p) c -> p k c", p=128))
                for i in range(2):
                    memset("dve", pgcb[i][:, 128:129], 1.0)
                dma(IDXi[:], bc_row(I["page_idx"], 0, 16 * NPG))
                P.op("pool", lambda e, iotaP=iotaP: e.iota(iotaP[:], pattern=[[0, 1]], base=0, channel_multiplier=1), writes=[iotaP[:]])
                cp("dve", iotaF[:], iotaP[:])
                for c0 in range(0, 16 * NPG, 256):
                    c1 = min(16 * NPG, c0 + 256)
                    cp("dve", IDXf[:, 0:c1 - c0], IDXi[:, c0:c1])
                    ts("dve", IDXf[:, 0:c1 - c0], IDXf[:, 0:c1 - c0], 128.0, ALU.mult, iotaF[:, 0:1], ALU.add)
                    if l > 0:
                        ts("dve", IDXf[:, 0:c1 - c0], IDXf[:, 0:c1 - c0], float(l * NPHYS * 128), ALU.add)
                    cp("dve", IDXi[:, c0:c1], IDXf[:, 0:c1 - c0])
                ckv_flat = I["cache_ckv"].ap().rearrange("l r c -> (l r) c"); kr_flat = I["cache_kr"].ap().rearrange("l r c -> (l r) c")
                TRIb, TRISb = cstb[:, 1, :], cstb[:, 4, :]

                def attn_tail(k):
                    for hh in range(4):
                        ob = ps[3 + hh][:, 0:129]
                        recip(orec[:, hh:hh + 1], ob[:, 128:129])
                        ts("dve", olat[:, hh, :], ob[:, 0:128], orec[:, hh:hh + 1], ALU.mult)
                    po = ps[0][:].bitcast(BF16).rearrange("p (k t) -> p k t", k=8)
                    for hh in range(4):
                        tr(po[:, hh, :], olat[:, hh, :], IDb)
                    cp("act", olatT[:], po[:, 0:4, :])
                    ymla()

                def ymla():
                    for hh in range(4):
                        mm(ps[7][:, hh * 64:(hh + 1) * 64], olatT[:, hh, :], wuv[:, hh * 64:(hh + 1) * 64])
                    cp("act", cat[:, 512:768], ps[7][:, 0:256])

                for k in range(NT):
                    smp = (k == NP)
                    ql, qr = qlt[k % 2], qrt[k % 2]
                    dma(ql[:], qlt_d.ap()[k]); dma(qr[:], qrt_d.ap()[k])
                    if not smp:
                        keys = []
                        for r in range(GRP - 1):
                            for kt in range(NP):
                                keys.append((KTg[:, r, kt * 128:(kt + 1) * 128], KRTg[:, r, kt * 128:(kt + 1) * 128], KVg[:, r * NP + kt, :], RKg[:, r * NP + kt, :], vis[:, r:r + 1], False))
                        for kt in range(k + 1):
                            keys.append((KTl[:, kt, :], KRTl[:, kt, :], KVl[:, kt, :], RKl[:, kt, :], None, kt == k))
                        N = len(keys)
                        for n, (kT_, krT_, kv_, rk_, bias_, diag_) in enumerate(keys):
                            sps = ps[1 + n % 2]; pT = pTb[n % 2]
                            mm(sps[:], kT_, ql[:], start=True, stop=False)
                            mm(sps[:], krT_, qr[:], start=False, stop=True)
                            for hh in range(4):
                                actf(pT[:, hh, :], sps[:, hh * 128:(hh + 1) * 128], AF.Exp, scale=rk_[:, hh:hh + 1], bias=bias_)
                            if diag_:
                                tt("pool", pT[:], pT[:], TRIb.unsqueeze(1).to_broadcast([128, 4, 128]), ALU.mult)
                            for hh in range(4):
                                mm(ps[3 + hh][:, 0:129], pT[:, hh, :], kv_, start=(n == 0), stop=(n == N - 1))
                        attn_tail(k)
                    else:
                        q4 = ql[:].rearrange("p (h t) -> p h t", h=4); qr4 = qr[:].rearrange("p (h t) -> p h t", h=4)
                        for b in range(16 if 'nosamp' not in DBG else 0):
                            for pg in range(NPG + 1):
                                n = b * (NPG + 1) + pg; i2 = n % 2
                                if pg < NPG and 'nopage' in DBG:
                                    continue
                                if pg < NPG:
                                    col = b * NPG + pg
                                    P.dma(lambda e, i2=i2, col=col, pgcb=pgcb, ckv_flat=ckv_flat, IDXi=IDXi: e.indirect_dma_start(out=pgcb[i2][:, 0:128], out_offset=None, in_=ckv_flat,
                                          in_offset=bass.IndirectOffsetOnAxis(ap=IDXi[:, col:col + 1], axis=0)), reads=[IDXi[:]], writes=[pgcb[i2][:]], q="pool")
                                    P.dma(lambda e, i2=i2, col=col, pgkb=pgkb, kr_flat=kr_flat, IDXi=IDXi: e.indirect_dma_start(out=pgkb[i2][:], out_offset=None, in_=kr_flat,
                                          in_offset=bass.IndirectOffsetOnAxis(ap=IDXi[:, col:col + 1], axis=0)), reads=[IDXi[:]], writes=[pgkb[i2][:]], q="pool")
                                    pz = ps[0][:].bitcast(BF16)
                                    tr(pz[:, 0:128], pgcb[i2][:, 0:128], IDb); tr(pz[0:32, 128:256], pgkb[i2][:], IDb)
                                    cp("act", cTs[i2][:], pz[:, 0:128]); cp("act", kTs[i2][:], pz[0:32, 128:256])
                                    mm(ps[2][:, 0:256], cTs[i2][:], wuk2[:])
                                    actf(sq[:].rearrange("p g d -> p (g d)"), ps[2][:, 0:256], AF.Square)
                                    red(rks[i2][:, 0:4], sq[:])
                                    actf(junk[:, 0:32], pgkb[i2][:], AF.Square, accum=rks[i2][:, 4:5])
                                    ts("dve", rks[i2][:, 0:4], rks[i2][:, 0:4], rks[i2][:, 4:5], ALU.add)
                                    rstd_of(rks[i2][:, 0:4], rks[i2][:, 0:4], 96)
                                    ts("dve", rks[i2][:, 0:4], rks[i2][:, 0:4], 96 ** -0.5, ALU.mult)
                                    kT_, krT_, kv_, rk_ = cTs[i2][:], kTs[i2][:], pgcb[i2][:], rks[i2]
                                else:
                                    kT_, krT_, kv_, rk_ = KT1[:, 1, :], KRT1[:, 1, :], KV1[:, 1, :], RK1[:, 1, :]
                                sps = ps[1][:, (n % 4) * 32:(n % 4) * 32 + 32]
                                mm(sps, kT_, q4[:, :, 8 * b:8 * b + 8], start=True, stop=False)
                                mm(sps, krT_, qr4[:, :, 8 * b:8 * b + 8], start=False, stop=True)
                                pT = pTs[i2]
                                for hh in range(4):
                                    actf(pT[:, hh, :], sps[:, hh * 8:(hh + 1) * 8], AF.Exp, scale=rk_[:, hh:hh + 1])
                                if pg == NPG:
                                    tt("pool", pT[:], pT[:], TRISb[:, 8 * b:8 * b + 8].unsqueeze(1).to_broadcast([128, 4, 8]), ALU.mult)
                                mm(ps[3][0:32, 0:129], pT[:].rearrange("p h i -> p (h i)"), kv_, start=(pg == 0), stop=(pg == NPG))
                            recip(odn[:], ps[3][0:32, 128:129])
                            ts("dve", ols[:], ps[3][0:32, 0:128], odn[:, 0:1], ALU.mult)
                            pz = ps[0][:].bitcast(BF16)
                            tr(pz[:, 256:288], ols[:], IDb[0:32, 0:32])
                            cp("act", olatT[:, :, 8 * b:8 * b + 8], pz[:, 256:288].rearrange("p (h i) -> p h i", h=4))
                        ymla()
                    dma(ysl[:], ys_d.ap()[k]); dma(zl[:], zs_d.ap()[k]); dma(cat[:, 768:1024], ygm_d.ap()[k])
                    if not smp:
                        dma(CT1[:], ctm_d.ap()[k])
                        for g in range(2):
                            mm(ps[6][:, 256 * g:256 * g + 256], CT1[:, g, :], Hinb[:])
                        tt("dve", ycor[:].rearrange("p (h d) -> p h d", h=8), ps[6][:].rearrange("p (h d) -> p h d", h=8),
                           ecor[:, k, :].unsqueeze(2).to_broadcast([128, 8, 64]), ALU.mult)
                        tt("dve", ysl[:], ysl[:], ycor[:], ALU.add)
                    actf(zl[:], zl[:], AF.Silu)
                    tt("dve", ysl[:], ysl[:], zl[:], ALU.mult)
                    for g in range(2):
                        norm_rows(cat[:, 256 * g:256 * g + 256], ysl[:, 256 * g:256 * g + 256], gsB[:, 256 * g:256 * g + 256], 256, 4 + g)
                    to_featmajor(hT, cat)
                    for nb in range(2):
                        for c in range(8):
                            mm(ps[6 + nb][:], hT[:, c, :], wb[:, c, nb * 512:(nb + 1) * 512], start=(c == 0), stop=(c == 7))
                        tt("dve", xres[:, k, nb * 512:(nb + 1) * 512], xres[:, k, nb * 512:(nb + 1) * 512], ps[6 + nb][:], ALU.add)
                P.barrier()
                sc_.close()
                P.barrier()
            if 'D' in cfg.phases:
                with ExitStack() as sd:
                    wst = sbt(sd, "wstd", [128, 8, 280]); wq = sbt(sd, "wq", [128, 8, 1024], BF16); wo = sbt(sd, "wo", [128, 8, 1024], BF16)
                    gmx = sbt(sd, "gmx", [128, D]); gq = sbt(sd, "gq", [128, 256]); gk = sbt(sd, "gk", [128, 256]); gmm = sbt(sd, "gmm", [128, D])
                    memx = sbt(sd, "memxd", [128, D]); raw = sbt(sd, "raw", [128, D]); nb_ = sbt(sd, "nb_", [128, D], BF16)
                    hTm = sbt(sd, "hTm", [128, 2, 8, 128], BF16)
                    kT_p = sbt(sd, "kT_p", [128, 4, 2, 256], BF16); vE_p = sbt(sd, "vE_p", [128, 2, 4, 257], BF16)
                    kT_s = sbt(sd, "kT_s", [128, 4, 2, 256], BF16); vE_s = sbt(sd, "vE_s", [128, 2, 4, 257], BF16)
                    qT = sbt(sd, "qT", [128, 4, 2, 128], BF16); pTm = [sbt(sd, "pTm%d" % i, [128, 128], BF16) for i in range(2)]
                    ob = sbt(sd, "ob", [128, D], BF16); orc = sbt(sd, "orc", [128, 4])
                    Kf = sbt(sd, "Kf", [128, 2, D]); Vf = sbt(sd, "Vf", [128, 2, D]); kbs = sbt(sd, "kbs", [128, 2, D], BF16)
                    cmask = sbt(sd, "cmask", [128, 16, 128]); cmaskb = sbt(sd, "cmaskb", [128, 16, 128], BF16)
                    dma(gmx[:], bc_row(I["g_memx"], l * D, D)); dma(gq[:], bc_row(I["g_mq"], l * 256, 256))
                    dma(gk[:], bc_row(I["g_mk"], l * 256, 256)); dma(gmm[:], bc_row(I["g_memm"], l * D, D))
                    dma(cmask[:], I["colmask"].ap()); cp("pool", cmaskb[:], cmask[:])
                    memset("dve", vE_p[:, :, :, 256:257], 1.0); memset("dve", vE_s[:, :, :, 256:257], 1.0)

                    def make_kT(dst, src_b):
                        for mt in range(2):
                            pz = ps[0][:].bitcast(BF16).rearrange("p (k t) -> p k t", k=8)
                            for c8 in range(8):
                                tr(pz[:, c8, :], src_b[:, mt, c8 * 128:(c8 + 1) * 128], IDb)
                            cp("act", dst[:, :, :, mt * 128:(mt + 1) * 128], pz.rearrange("p (h dc) t -> p h dc t", h=4))

                    for mt in range(2):
                        dma(memx[:], I["mem_full"].ap()[mt * 128:(mt + 1) * 128, :])
                        norm_rows(h[:], memx[:], gmm[:], D, 0)
                        to_featmajor(hT, h)
                        cp("pool", hTm[:, mt], hT[:])
                    load_w(wst, wo, I["w_mk"].ap()[l], D)
                    for mt in range(2):
                        proj(raw, hTm[:, mt], wo, D)
                        for hd in range(4):
                            norm_rows(kbs[:, mt, hd * 256:(hd + 1) * 256], raw[:, hd * 256:(hd + 1) * 256], gk[:], 256, 1 + hd)
                    make_kT(kT_p, kbs)
                    load_w(wst, wo, I["w_mv"].ap()[l], D)
                    for mt in range(2):
                        proj(raw, hTm[:, mt], wo, D)
                        cp("pool", vE_p[:, mt, :, 0:256], raw[:].rearrange("p (h d) -> p h d", h=4))
                    load_w(wst, wq, I["w_mq"].ap()[l], D)
                    load_w(wst, wo, I["w_mo"].ap()[l], D)

                    for k in range(NT):
                        smp = (k == NP)
                        norm_rows(h[:], xres[:, k, :], gmx[:], D, 0)
                        to_featmajor(hT, h)
                        proj(raw, hT, wq, D)
                        for hd in range(4):
                            norm_rows(nb_[:, hd * 256:(hd + 1) * 256], raw[:, hd * 256:(hd + 1) * 256], gq[:], 256, 1 + hd)
                        pz = ps[0][:].bitcast(BF16).rearrange("p (k t) -> p k t", k=8)
                        for c8 in range(8):
                            tr(pz[:, c8, :], nb_[:, c8 * 128:(c8 + 1) * 128], IDb)
                        cp("act", qT[:], pz.rearrange("p (h dc) t -> p h dc t", h=4))
                        nseq = 16 if smp else 1
                        it = 0
                        for b in range(nseq):
                            if smp:
                                dma(Kf[:], I["cache_mem_k"].ap()[l, b].rearrange("(mt p) d -> p mt d", p=128))
                                dma(Vf[:], I["cache_mem_v"].ap()[l, b].rearrange("(mt p) d -> p mt d", p=128))
                                cp("pool", kbs[:], Kf[:])
                                for mt in range(2):
                                    cp("dve" if mt == 0 else "pool", vE_s[:, mt, :, 0:256], Vf[:, mt, :].rearrange("p (h d) -> p h d", h=4))
                                make_kT(kT_s, kbs)
                                kT_, vE_ = kT_s, vE_s
                            else:
                                kT_, vE_ = kT_p, vE_p
                            for hd in range(4):
                                for mt in range(2):
                                    sps = ps[1 + it % 2][:, 0:128]; pT = pTm[it % 2]; it += 1
                                    for dc in range(2):
                                        mm(sps, kT_[:, hd, dc, mt * 128:(mt + 1) * 128], qT[:, hd, dc, :], start=(dc == 0), stop=(dc == 1))
                                    actf(pT[:], sps, AF.Exp, scale=1.0 / 16.0)
                                    if smp:
                                        tt("pool", pT[:], pT[:], cmaskb[:, b, :], ALU.mult)
                                    mm(ps[3 + hd][:, 0:257], pT[:], vE_[:, mt, hd, :], start=(b == 0 and mt == 0), stop=(b == nseq - 1 and mt == 1))
                        for hd in range(4):
                            recip(orc[:, hd:hd + 1], ps[3 + hd][:, 256:257])
                            ts("dve", ob[:, hd * 256:(hd + 1) * 256], ps[3 + hd][:, 0:256], orc[:, hd:hd + 1], ALU.mult)
                        to_featmajor(hT, ob)
                        for nb2 in range(2):
                            for c in range(8):
                                mm(ps[1 + nb2][:], hT[:, c, :], wo[:, c, nb2 * 512:(nb2 + 1) * 512], start=(c == 0), stop=(c == 7))
                            tt("dve", xres[:, k, nb2 * 512:(nb2 + 1) * 512], xres[:, k, nb2 * 512:(nb2 + 1) * 512], ps[1 + nb2][:], ALU.add)
                    P.barrier()
                P.barrier()
            if 'E' in cfg.phases:
                with ExitStack() as se0:
                    stg = [sbt(se0, "stg%d" % i, [128, 2048]) for i in range(2)]; stb = [sbt(se0, "stb%d" % i, [128, 2048], BF16) for i in range(2)]
                    uT_v = I["peer_uT"].ap()[l].rearrange("(k p) e -> p k e", p=128); uTb_v = UTb.ap().rearrange("(k p) e -> p k e", p=128)
                    v_v = I["peer_v"].ap()[l].rearrange("(c p) d -> p c d", p=128); vb_v = Vb.ap().rearrange("(c p) d -> p c d", p=128)
                    engs = ("pool", "dve", "act")
                    for i in range(64):
                        a_, b_ = stg[i % 2], stb[i % 2]
                        dma(a_[:].rearrange("p (k e) -> p k e", k=8), uT_v[:, :, i * 256:(i + 1) * 256])
                        cp(engs[i % 3], b_[:], a_[:])
                        dma(uTb_v[:, :, i * 256:(i + 1) * 256], b_[:].rearrange("p (k e) -> p k e", k=8))
                    for i in range(64):
                        a_, b_ = stg[i % 2], stb[i % 2]
                        dma(a_[:].rearrange("p (c d) -> p c d", c=2), v_v[:, 2 * i:2 * i + 2, :])
                        cp(engs[i % 3], b_[:], a_[:])
                        dma(vb_v[:, 2 * i:2 * i + 2, :], b_[:].rearrange("p (c d) -> p c d", c=2))
                    P.barrier()
                P.barrier()
                thr_all = sbt(st, "thr_all", [128, NT, 8]); nb_all = sbt(st, "nb_all", [128, NT, 8])
                with ExitStack() as sea:
                    wst = sbt(sea, "wste", [128, 8, 280]); wpq = sbt(sea, "wpq", [128, 8, 2048], BF16)
                    kT = sbt(sea, "kTe", [128, 16, 128], BF16); gff = sbt(sea, "gff", [128, D])
                    qTs = sbt(sea, "qTs", [128, 16, 128], BF16); S = sbt(sea, "S", [128, 16, 128]); Sw = sbt(sea, "Sw", [128, 256])
                    V16 = sbt(sea, "V16", [128, 16, 16]); cand = sbt(sea, "cand", [128, 8, 256]); SC = sbt(sea, "SC", [128, 8, 16])
                    SCm = sbt(sea, "SCm", [128, 8, 16]); Zs = sbt(sea, "Zs", [128, 8])
                    load_w(wst, wpq, I["w_pq"].ap()[l], 2048)
                    dma(S[:].rearrange("p c k -> p (c k)"), I["peer_kT"].ap()[l].rearrange("d c k -> d (c k)")); cp("dve", kT[:], S[:])
                    dma(gff[:], bc_row(I["g_ffn"], l * D, D))
                    for k in range(NT):
                        norm_rows(h[:], xres[:, k, :], gff[:], D, 0)
                        to_featmajor(hT, h)
                        dma(hT_d.ap()[k], hT[:].rearrange("p k t -> p (k t)"))
                        for c4 in range(4):
                            bank = ps[1 + c4 % 2]
                            for ci in range(4):
                                c = c4 * 4 + ci
                                for kk in range(8):
                                    mm(bank[:, ci * 128:(ci + 1) * 128], wpq[:, kk, c * 128:(c + 1) * 128], hT[:, kk, :], start=(kk == 0), stop=(kk == 7))
                            cp("act", qTs[:, c4 * 4:(c4 + 1) * 4, :], bank[:].rearrange("p (c t) -> p c t", c=4))
                        for c4 in range(4):
                            bank = ps[3 + c4 % 2]
                            for ci in range(4):
                                c = c4 * 4 + ci
                                mm(bank[:, ci * 128:(ci + 1) * 128], qTs[:, c, :], kT[:, c, :])
                            cp("dve", S[:, c4 * 4:(c4 + 1) * 4, :], bank[:].rearrange("p (c t) -> p c t", c=4))
                        dma(s_d.ap()[k], S[:].rearrange("p c k -> p (c k)"))
                        for c in range(16):
                            P.op("dve", lambda e, c=c, S=S, V16=V16: e.max(out=V16[:, c, 0:8], in_=S[:, c, :]), reads=[S[:]], writes=[V16[:]])
                            P.op("dve", lambda e, c=c, S=S, V16=V16, Sw=Sw: e.match_replace(out=Sw[:, 0:128], in_to_replace=V16[:, c, 0:8], in_values=S[:, c, :], imm_value=-1e30), reads=[S[:], V16[:]], writes=[Sw[:]])
                            P.op("dve", lambda e, c=c, V16=V16, Sw=Sw: e.max(out=V16[:, c, 8:16], in_=Sw[:, 0:128]), reads=[Sw[:]], writes=[V16[:]])
                        V4 = V16[:].rearrange("p (h s) k -> p h s k", s=2)
                        tt("pool", cand[:].rearrange("p h (a b) -> p h a b", a=16), V4[:, :, 0, :].unsqueeze(3).to_broadcast([128, 8, 16, 16]),
                           V4[:, :, 1, :].unsqueeze(2).to_broadcast([128, 8, 16, 16]), ALU.add)
                        for hh in range(8):
                            P.op("dve", lambda e, hh=hh, SC=SC, cand=cand: e.max(out=SC[:, hh, 0:8], in_=cand[:, hh, :]), reads=[cand[:]], writes=[SC[:]])
                            P.op("dve", lambda e, hh=hh, SC=SC, cand=cand, Sw=Sw: e.match_replace(out=Sw[:], in_to_replace=SC[:, hh, 0:8], in_values=cand[:, hh, :], imm_value=-1e30), reads=[cand[:], SC[:]], writes=[Sw[:]])
                            P.op("dve", lambda e, hh=hh, SC=SC, Sw=Sw: e.max(out=SC[:, hh, 8:16], in_=Sw[:]), reads=[Sw[:]], writes=[SC[:]])
                        cp("dve", thr_all[:, k, :], SC[:, :, 15])
                        tt("dve", SCm[:], SC[:], SC[:, :, 0:1].to_broadcast([128, 8, 16]), ALU.subtract)
                        actf(SCm[:], SCm[:], AF.Exp)
                        red(Zs[:], SCm[:])
                        actf(Zs[:], Zs[:], AF.Ln)
                        tt("dve", Zs[:], Zs[:], SC[:, :, 0], ALU.add)
                        ts("dve", nb_all[:, k, :], Zs[:], -1.0, ALU.mult)
                    P.barrier()
                P.barrier()
                with ExitStack() as seb:
                    S2 = [sbt(seb, "Sb%d" % i, [128, 16, 128]) for i in range(2)]; SUM = [sbt(seb, "SUM%d" % i, [128, 8, 128]) for i in range(2)]
                    Eb = [sbt(seb, "Eb%d" % i, [128, 1024], BF16) for i in range(2)]; Gh2 = [sbt(seb, "Gh%d" % i, [128, 8, 1024], BF16) for i in range(2)]
                    hT2 = sbt(seb, "hT2", [128, 8, 256], BF16)
                    UTk = [sbt(seb, "UTk%d" % i, [128, 8, 1024], BF16) for i in range(2)]; Vk = [sbt(seb, "Vk%d" % i, [128, 8, 1024], BF16) for i in range(1)]
                    gA = [sbt(seb, "gA%d" % i, [128, 512], BF16) for i in range(2)]; WT = [sbt(seb, "WT%d" % i, [128, 512], BF16) for i in range(2)]
                    uTb_v = UTb.ap().rearrange("(k p) e -> p k e", p=128); vb_v = Vb.ap().rearrange("(c p) d -> p c d", p=128)
                    it = 0
                    for k0 in range(0, NT, 2):
                        gs = min(2, NT - k0)
                        for j in range(gs):
                            dma(S2[j][:].rearrange("p c k -> p (c k)"), s_d.ap()[k0 + j])
                            dma(hT2[:, :, j * 128:(j + 1) * 128], hT_d.ap()[k0 + j].rearrange("p (k t) -> p k t", k=8))
                        for ab in range(16):
                            ub, vb2 = UTk[ab % 2], Vk[0]
                            dma(ub[:], uTb_v[:, :, ab * 1024:(ab + 1) * 1024]); dma(vb2[:], vb_v[:, ab * 8:(ab + 1) * 8, :])
                            for j in range(gs):
                                S = S2[j]
                                for hh in range(8):
                                    sm_, eb_ = SUM[it % 2], Eb[it % 2]; it += 1
                                    tt("pool", sm_[:], S[:, 2 * hh, ab * 8:(ab + 1) * 8].unsqueeze(2).to_broadcast([128, 8, 128]),
                                       S[:, 2 * hh + 1, :].unsqueeze(1).to_broadcast([128, 8, 128]), ALU.add)
                                    sflat = sm_[:].rearrange("p a b -> p (a b)")
                                    actf(eb_[:], sflat, AF.Exp, bias=nb_all[:, k0 + j, hh:hh + 1])
                                    stt(Gh2[j][:, hh, :], sflat, thr_all[:, k0 + j, hh:hh + 1], eb_[:], ALU.is_ge, ALU.mult)
                            for c2 in range(4):
                                gtb, atb = ps[c2 % 2], ps[2 + c2 % 2]
                                for ci in range(2):
                                    cc_ = c2 * 2 + ci
                                    for j in range(gs):
                                        for hh in range(8):
                                            mm(gtb[:, ci * 256 + j * 128:ci * 256 + (j + 1) * 128], Gh2[j][:, hh, cc_ * 128:(cc_ + 1) * 128], IDb, start=(hh == 0), stop=(hh == 7))
                                    for kk in range(8):
                                        mm(atb[:, ci * 256:ci * 256 + gs * 128], ub[:, kk, cc_ * 128:(cc_ + 1) * 128], hT2[:, kk, 0:gs * 128], start=(kk == 0), stop=(kk == 7))
                                actf(gA[c2 % 2][:], atb[:], AF.Gelu_apprx_tanh)
                                tt("dve", WT[c2 % 2][:], gA[c2 % 2][:], gtb[:], ALU.mult)
                                for ci in range(2):
                                    cc_ = c2 * 2 + ci
                                    first = (ab == 0 and cc_ == 0); last = (ab == 15 and cc_ == 7)
                                    for j in range(gs):
                                        for half in range(2):
                                            mm(ps[4 + 2 * j + half][:], WT[c2 % 2][:, ci * 256 + j * 128:ci * 256 + (j + 1) * 128], vb2[:, cc_, half * 512:(half + 1) * 512], start=first, stop=last)
                        for j in range(gs):
                            for half in range(2):
                                tt("dve", xres[:, k0 + j, half * 512:(half + 1) * 512], xres[:, k0 + j, half * 512:(half + 1) * 512], ps[4 + 2 * j + half][:], ALU.add)
                    P.barrier()
                P.barrier()
        dma(O["y"].ap().rearrange("(k p) d -> p k d", p=128), xres[:])
        P.emit(st)
    return nc


def host_inputs(inputs, cfg):
    f = lambda k: np.ascontiguousarray(np.asarray(inputs[k]))
    NP, NT, NPG = cfg.NP, cfg.NT, cfg.NPG
    x_prompt, x_sample = f("x_prompt"), f("x_sample")
    past_len = NPG * 128
    inv = (1.0 / (10000.0 ** (np.arange(16, dtype=np.float32) / 16))).astype(np.float32)
    ii = np.arange(128)
    consts = np.zeros((8, 128, 128), np.float32)
    same = (ii[:, None] // 8) == (ii[None, :] // 8)
    consts[0] = np.eye(128)
    consts[1] = (ii[None, :] >= ii[:, None])
    consts[2] = (ii[:, None] > ii[None, :])
    consts[3] = 1.0
    consts[4] = consts[1] * same
    consts[5] = consts[2] * same
    consts[6] = same
    consts[7, :64, 0] = 1.0
    consts[7, 64:, 1] = 1.0
    blkind = (ii[:, None] // 8 == np.arange(16)[None, :]).astype(np.float32)
    qk = lambda g: np.concatenate([g, g[:, 64:]], axis=1)
    wsT = f("gm_ws").transpose(0, 1, 3, 2)
    wsT_s = np.tile(wsT[:, :, :8, :8], (1, 1, 16, 16))
    bs = f("gm_bs")
    bsT = bs.transpose(0, 2, 1)
    bsT_s = np.tile(bsT[:, :8, :], (1, 16, 1))
    shared = {
        "consts": consts, "blkind": blkind,
        "w_in": f("w_in"), "g_mix": f("g_mix"), "g_ckv": f("g_ckv"), "g_cq": f("g_cq"), "w_uq": f("w_uq"),
        "w_uk": f("w_uk").reshape(-1, 128, 256), "w_ukT": np.ascontiguousarray(f("w_uk").transpose(0, 2, 3, 1)),
        "w_uv": f("w_uv").reshape(-1, 128, 256), "gq96": qk(f("g_qk_q")), "gk96": qk(f("g_qk_k")),
        "conv_wT": np.ascontiguousarray(f("conv_w").transpose(0, 2, 1)), "conv_b": f("conv_b"), "dt_bias": f("dt_bias"),
        "a_log": f("a_log"), "d_skip": f("d_skip"), "g_ssd_out": f("g_ssd_out"),
        "gm_ln_g": f("gm_ln_g"), "gm_ln_b": f("gm_ln_b"),
        "gm_wsT": np.ascontiguousarray(np.stack([wsT, wsT_s], 1)), "gm_bsT": np.ascontiguousarray(np.stack([bsT, bsT_s], 1)),
        "g_ffn": f("g_ffn"), "w_pq": f("w_pq"),
        "peer_kT": np.ascontiguousarray(np.stack([f("peer_k1"), f("peer_k2")], 2).transpose(0, 4, 1, 2, 3).reshape(f("peer_k1").shape[0], 128, 16, 128)),
        "peer_uT": np.ascontiguousarray(f("peer_u").transpose(0, 2, 1)), "peer_v": f("peer_v"),
        "w_out": f("w_out"), "g_memm": f("g_memm"), "g_mk": f("g_mk"), "w_mk": f("w_mk"), "w_mv": f("w_mv"),
        "g_memx": f("g_memx"), "w_mq": f("w_mq"), "g_mq": f("g_mq"), "w_mo": f("w_mo"),
        "colmask": np.ascontiguousarray(np.broadcast_to((np.arange(128)[None, None, :] // 8 == np.arange(16)[None, :, None]), (128, 16, 128)).astype(np.float32)),
        "cache_ckv": f("cache_mla_ckv").reshape(f("cache_mla_ckv").shape[0], -1, 128),
        "cache_kr": f("cache_mla_krope").reshape(f("cache_mla_krope").shape[0], -1, 32),
    }
    maps = []
    for c in range(8):
        g, r = c // 4, c % 4
        xin = np.concatenate([x_prompt[g, r * NP * 128:(r + 1) * NP * 128], x_sample[c * 16:(c + 1) * 16].reshape(128, D)], 0)
        pos = np.concatenate([r * NP * 128 + np.arange(NP * 128), past_len + (np.arange(128) % 8)]).astype(np.float32)
        ang = pos[:, None] * inv[None, :]
        rope = np.concatenate([np.cos(ang), np.sin(ang)], 1).astype(np.float32)
        vis = np.zeros((128, 12), np.float32)
        for rr in range(4):
            vis[:, rr] = 0.0 if rr < r else NEG
            vis[:, 4 + rr] = 1.0 if rr < r else 0.0
            vis[:, 8 + rr] = 1.0 if rr == r - 1 else 0.0
        m = dict(shared)
        ml, half = r // 2, r % 2
        ml = min(ml, f("g_memm").shape[0] - 1)
        m.update({"mem_rows": np.ascontiguousarray(f("mem_prompt")[g, half * 128:(half + 1) * 128]),
                  "g_memm_l": f("g_memm")[ml:ml + 1], "g_mk_l": f("g_mk")[ml:ml + 1], "w_mk_l": f("w_mk")[ml], "w_mv_l": f("w_mv")[ml]})
        m.update({"mem_full": np.ascontiguousarray(f("mem_prompt")[g]),
                  "cache_mem_k": np.ascontiguousarray(f("cache_mem_k")[:, c * 16:(c + 1) * 16].reshape(f("cache_mem_k").shape[0], 16, 256, 1024)),
                  "cache_mem_v": np.ascontiguousarray(f("cache_mem_v")[:, c * 16:(c + 1) * 16].reshape(f("cache_mem_v").shape[0], 16, 256, 1024))})
        m.update({"xin": np.ascontiguousarray(xin), "rope": rope, "vis": vis,
                  "page_idx": np.ascontiguousarray(f("page_table")[c * 16:(c + 1) * 16].reshape(1, -1).astype(np.int32)),
                  "state_conv": np.ascontiguousarray(f("state_conv")[:, c * 16:(c + 1) * 16]),
                  "state_ssm": np.ascontiguousarray(f("state_ssm")[:, c * 16:(c + 1) * 16])})
        maps.append(m)
    return maps


def kernel(**inputs):
    cfg = Cfg(depth=2, np_=16, npg=64, nphys=int(np.asarray(inputs["cache_mla_ckv"]).shape[1]), phases="ACDE")
    NP_, NPT_ = cfg.NP, cfg.NP * 128
    maps = host_inputs(inputs, cfg)
    nc = build(cfg)
    res = run_bass_kernel_spmd(nc, maps, core_ids=list(range(8))).results
    B, SEQ, NS, LS, DEPTH_ = 2, 4 * NPT_, 128, 8, cfg.DEPTH
    z = lambda *s: np.zeros(s, np.float32)
    y_p, y_s = z(B, SEQ, D), z(NS, LS, D)
    ckv_p, kr_p = z(DEPTH_, B, SEQ, 128), z(DEPTH_, B, SEQ, 32)
    ssm_p, conv_p = z(DEPTH_, B, 8, 64, 64), z(DEPTH_, B, 3, 768)
    mk, mv = z(DEPTH_, B, 256, 4, 256), z(DEPTH_, B, 256, 4, 256)
    gmv_p = z(DEPTH_, B, 128, 256)
    ckv_s, kr_s = z(DEPTH_, NS, LS, 128), z(DEPTH_, NS, LS, 32)
    ssm_s, conv_s, gmv_s = z(DEPTH_, NS, 8, 64, 64), z(DEPTH_, NS, 3, 768), z(DEPTH_, NS, LS, 256)
    for c in range(8):
        g, r = c // 4, c % 4
        o = res[c]
        sl = slice(r * NPT_, (r + 1) * NPT_)
        bs = slice(c * 16, (c + 1) * 16)
        y_p[g, sl] = o["y"][:NPT_]
        y_s[bs] = o["y"][NPT_:].reshape(16, 8, D)
        for l in range(DEPTH_):
            ckv_p[l, g, sl] = o["new_ckv"][l, :NPT_]; kr_p[l, g, sl] = o["new_kr"][l, :NPT_]
            ckv_s[l, bs] = o["new_ckv"][l, NPT_:].reshape(16, 8, 128); kr_s[l, bs] = o["new_kr"][l, NPT_:].reshape(16, 8, 32)
            ssm_s[l, bs] = o["new_ssm"][l, :16]; conv_s[l, bs] = o["new_conv"][l, :16]
            gmv_s[l, bs] = o["new_gmv"][l, 128:].reshape(16, 8, 256)
            if r == 3:
                ssm_p[l, g] = o["new_ssm"][l, 16]; conv_p[l, g] = o["new_conv"][l, 16]; gmv_p[l, g] = o["new_gmv"][l, :128]
        ml, half = r // 2, r % 2
        mk[ml, g, half * 128:(half + 1) * 128] = o["mem_k"].reshape(128, 4, 256)
        mv[ml, g, half * 128:(half + 1) * 128] = o["mem_v"].reshape(128, 4, 256)
    return (y_p, y_s, ckv_p, kr_p, ssm_p, conv_p, mk, mv, gmv_p, ckv_s, kr_s, ssm_s, conv_s, gmv_s)
```

```python
import math
from contextlib import ExitStack
import numpy as np
import concourse.bass as bass
import concourse.mybir as mybir
from concourse.bass_utils import run_bass_kernel_spmd

F32 = mybir.dt.float32
BF16 = mybir.dt.bfloat16
I32 = mybir.dt.int32
AF = mybir.ActivationFunctionType
ALU = mybir.AluOpType
AX = mybir.AxisListType
NDMA_SEM = 48
EPS = 1e-6
NEG = -30000.0


class Prog:
    ENGS = ("pe", "act", "dve", "pool", "sp")

    def __init__(self, nc):
        self.nc = nc
        self.ops = {e: [] for e in self.ENGS}
        self.cnt = {e: 0 for e in self.ENGS}
        self.lastw = {}
        self.readers = {}
        self.waited = {e: {} for e in self.ENGS}
        self.ndma = {"sp": 0, "pool": 0}
        self.dma_tokens = {"sp": [], "pool": []}
        self.sems = {}
        self.semval = {}

    @staticmethod
    def _key(r):
        if isinstance(r, (str, tuple)):
            return r
        return r.tensor.name

    def _deps(self, reads, writes):
        deps = []
        for r in reads:
            t = self.lastw.get(r)
            if t is not None:
                deps.append(t)
        for w in writes:
            t = self.lastw.get(w)
            if t is not None:
                deps.append(t)
            deps.extend(self.readers.get(w, ()))
        return deps

    def _waits(self, eng, deps, skip_pe=False):
        waits = []
        wd = self.waited[eng]
        for (semkey, val, deng) in deps:
            if skip_pe and deng == "pe" and eng == "pe":
                continue
            if wd.get(semkey, 0) >= val:
                continue
            wd[semkey] = val
            waits.append((semkey, val))
        return waits

    def _commit(self, tok, reads, writes):
        for r in reads:
            lst = self.readers.setdefault(r, [])
            lst.append(tok)
            if len(lst) > 64:
                best = {}
                for t in lst:
                    if t[0] not in best or best[t[0]][1] < t[1]:
                        best[t[0]] = t
                self.readers[r] = list(best.values())
        for w in writes:
            self.lastw[w] = tok
            self.readers[w] = []

    def op(self, eng, fn, reads=(), writes=()):
        reads = [self._key(r) for r in reads]
        writes = [self._key(w) for w in writes]
        deps = self._deps(reads, writes)
        waits = self._waits(eng, deps, skip_pe=True)
        self.cnt[eng] += 1
        tok = (eng, self.cnt[eng], eng)
        self.ops[eng].append((fn, waits, (eng, 1)))
        self._commit(tok, reads, writes)
        return tok

    def dma(self, fn, reads=(), writes=(), q="sp", inc=16):
        reads = [self._key(r) for r in reads]
        writes = [self._key(w) for w in writes]
        deps = self._deps(reads, writes)
        k = self.ndma[q]
        self.ndma[q] += 1
        semkey = ("dma", q, k % NDMA_SEM)
        prev = self.semval.get(semkey, 0)
        val = prev + inc
        self.semval[semkey] = val
        if prev > 0:
            deps.append((semkey, prev, "dma"))
        waits = self._waits(q, deps)
        tok = (semkey, val, "dma")
        self.ops[q].append((fn, waits, (semkey, inc)))
        self._commit(tok, reads, writes)
        self.dma_tokens[q].append(tok)
        return tok

    def barrier(self):
        toks = [(e, self.cnt[e], e) for e in self.ENGS if self.cnt[e] > 0]
        for q in ("sp", "pool"):
            toks.extend(self.dma_tokens[q][-NDMA_SEM:])
        for e in self.ENGS:
            waits = self._waits(e, toks)
            if waits:
                self.ops[e].append((None, waits, None))
        self.lastw = {}
        self.readers = {}

    def emit(self, stack):
        nc = self.nc
        self.barrier()
        semnames = set(self.ENGS)
        for q in ("sp", "pool"):
            for i in range(min(self.ndma[q], NDMA_SEM)):
                semnames.add(("dma", q, i))
        for sk in sorted(semnames, key=str):
            nm = sk if isinstance(sk, str) else "d_%s_%d" % (sk[1], sk[2])
            self.sems[sk] = stack.enter_context(nc.semaphore("s_" + nm))
        block = stack.enter_context(nc.Block())
        sems = self.sems

        def run(e, lst):
            for (fn, waits, inc) in lst:
                for (sk, val) in waits:
                    e.wait_ge(sems[sk], val)
                if fn is not None:
                    fn(e).then_inc(sems[inc[0]], inc[1])

        @block.tensor
        def _(e):
            run(e, self.ops["pe"])

        @block.scalar
        def _(e):
            run(e, self.ops["act"])

        @block.vector
        def _(e):
            run(e, self.ops["dve"])

        @block.gpsimd
        def _(e):
            run(e, self.ops["pool"])

        @block.sync
        def _(e):
            run(e, self.ops["sp"])


D = 1024
C_Z, C_XBC, C_DT, C_CQ, C_CKV, C_KR, C_U, C_V = 0, 512, 1280, 1288, 1544, 1672, 1704, 1960


class Cfg:
    def __init__(self, depth=2, np_=16, npg=64, nphys=10240, phases="ACDE"):
        self.DEPTH, self.NP, self.NPG, self.NPHYS, self.phases = depth, np_, npg, nphys, phases
        self.NT = np_ + 1
        self.GRP = 4
        self.NC = 8


DBG = set()

def build(cfg):
    NP, NT, DEPTH, GRP, NPG, NPHYS = cfg.NP, cfg.NT, cfg.DEPTH, cfg.GRP, cfg.NPG, cfg.NPHYS
    NPT = NP * 128
    nc = bass.Bass("TRN2", target_bir_lowering=False)
    I = {}
    O = {}

    def inp(name, shape, dt=F32):
        I[name] = nc.dram_tensor(name, list(shape), dt, kind="ExternalInput")
        return I[name]

    def outp(name, shape, dt=F32):
        O[name] = nc.dram_tensor(name, list(shape), dt, kind="ExternalOutput")
        return O[name]

    def scr(name, shape, dt=F32):
        return nc.dram_tensor(name, list(shape), dt, kind="Internal")

    inp("xin", [NT * 128, D]); inp("consts", [8, 128, 128]); inp("blkind", [128, 16]); inp("rope", [NT * 128, 32]); inp("vis", [128, 12])
    inp("w_in", [DEPTH, D, 2216]); inp("g_mix", [DEPTH, D]); inp("g_ckv", [DEPTH, 128]); inp("g_cq", [DEPTH, 256])
    inp("w_uq", [DEPTH, 256, 384]); inp("w_uk", [DEPTH, 128, 256]); inp("w_ukT", [DEPTH, 4, 64, 128]); inp("w_uv", [DEPTH, 128, 256])
    inp("gq96", [DEPTH, 96]); inp("gk96", [DEPTH, 96])
    inp("conv_wT", [DEPTH, 768, 4]); inp("conv_b", [DEPTH, 768]); inp("dt_bias", [DEPTH, 8]); inp("a_log", [DEPTH, 8]); inp("d_skip", [DEPTH, 8])
    inp("g_ssd_out", [DEPTH, 512]); inp("state_conv", [DEPTH, 16, 3, 768]); inp("state_ssm", [DEPTH, 16, 8, 64, 64])
    inp("gm_ln_g", [DEPTH, 256]); inp("gm_ln_b", [DEPTH, 256]); inp("gm_wsT", [DEPTH, 2, 4, 128, 128]); inp("gm_bsT", [DEPTH, 2, 128, 4])
    inp("mem_rows", [128, D]); inp("g_memm_l", [1, D]); inp("g_mk_l", [1, 256]); inp("w_mk_l", [D, D]); inp("w_mv_l", [D, D])
    outp("mem_k", [128, D]); outp("mem_v", [128, D])
    inp("mem_full", [256, D]); inp("g_memm", [DEPTH, D]); inp("g_mk", [DEPTH, 256]); inp("w_mk", [DEPTH, D, D]); inp("w_mv", [DEPTH, D, D])
    inp("g_memx", [DEPTH, D]); inp("w_mq", [DEPTH, D, D]); inp("g_mq", [DEPTH, 256]); inp("w_mo", [DEPTH, D, D])
    inp("cache_mem_k", [DEPTH, 16, 256, D]); inp("cache_mem_v", [DEPTH, 16, 256, D]); inp("colmask", [128, 16, 128])
    inp("g_ffn", [DEPTH, D]); inp("w_pq", [DEPTH, D, 2048]); inp("peer_kT", [DEPTH, 128, 16, 128])
    inp("peer_uT", [DEPTH, D, 16384]); inp("peer_v", [DEPTH, 16384, D])
    inp("w_out", [DEPTH, D, D]); inp("page_idx", [1, 16 * NPG], I32)
    inp("cache_ckv", [DEPTH, NPHYS * 128, 128]); inp("cache_kr", [DEPTH, NPHYS * 128, 32])
    outp("y", [NT * 128, D]); outp("new_ckv", [DEPTH, NT * 128, 128]); outp("new_kr", [DEPTH, NT * 128, 32])
    outp("new_conv", [DEPTH, 17, 3, 768]); outp("new_gmv", [DEPTH, 2 * 128, 256]); outp("new_ssm", [DEPTH, 17, 8, 64, 64])

    UTb = scr("UTb", [D, 16384], BF16); Vb = scr("Vb", [16384, D], BF16)
    s_d = scr("s_d", [NT, 128, 2048]); hT_d = scr("hT_d", [NT, 128, 1024], BF16)
    ccH = scr("ccH", [3, 768]); ccHg = scr("ccHg", [GRP * 3, 768])
    ccS = scr("ccS", [128, 264]); ccSg = scr("ccSg", [GRP * 128, 264])
    ctm_d = scr("ctm_d", [NT, 128, 2, 128], BF16); hs_d = scr("hs_d", [16, 128, 256]); ys_d = scr("ys_d", [NT, 128, 512]); zs_d = scr("zs_d", [NT, 128, 512]); ygm_d = scr("ygm_d", [NT, 128, 256], BF16)
    qlt_d = scr("qlt_d", [NT, 128, 512], BF16); qrt_d = scr("qrt_d", [NT, 32, 512], BF16)
    ccKT = scr("ccKT", [160, NPT], BF16); ccKTg = scr("ccKTg", [GRP * 160, NPT], BF16)
    ccKV = scr("ccKV", [NPT, 128], BF16); ccKVg = scr("ccKVg", [GRP * NPT, 128], BF16)
    ccRK = scr("ccRK", [NPT, 4]); ccRKg = scr("ccRKg", [GRP * NPT, 4])

    P = Prog(nc)
    with ExitStack() as st:
        uniq = [0]

        def sbt(stack, name, shape, dt=F32):
            uniq[0] += 1
            return stack.enter_context(nc.sbuf_tensor("%s_%d" % (name, uniq[0]), list(shape), dt))

        def dma(out, in_, q="sp", **kw):
            P.dma(lambda e: e.dma_start(out=out, in_=in_, **kw), reads=[in_], writes=[out], q=q)

        def mm(out, lhsT, rhs, start=True, stop=True, rk=None):
            P.op("pe", lambda e: e.matmul(out, lhsT=lhsT, rhs=rhs, start=start, stop=stop), reads=(rk if rk is not None else [lhsT, rhs]), writes=[out])

        def tr(out, in_, ident):
            P.op("pe", lambda e: e.transpose(out=out, in_=in_, identity=ident), reads=[in_, ident], writes=[out])

        def actf(out, in_, func, bias=None, scale=None, accum=None):
            kw = {}
            if bias is not None:
                kw["bias"] = bias
            if scale is not None:
                kw["scale"] = scale
            if accum is not None:
                kw["accum_out"] = accum
            r = [in_] + [a for a in (bias, scale) if a is not None and not isinstance(a, (int, float))]
            w = [out] + ([accum] if accum is not None else [])
            P.op("act", lambda e: e.activation(out=out, in_=in_, func=func, **kw), reads=r, writes=w)

        def cp(eng, out, in_):
            if eng == "act":
                P.op("act", lambda e: e.copy(out=out, in_=in_), reads=[in_], writes=[out])
            else:
                P.op(eng, lambda e: e.tensor_copy(out=out, in_=in_), reads=[in_], writes=[out])

        def tt(eng, out, a, b, op):
            P.op(eng, lambda e: e.tensor_tensor(out=out, in0=a, in1=b, op=op), reads=[a, b], writes=[out])

        def ts(eng, out, a, s1, op0, s2=None, op1=None):
            r = [a] + [s for s in (s1, s2) if s is not None and not isinstance(s, (int, float))]
            kw = {"op1": op1} if op1 is not None else {}
            P.op(eng, lambda e: e.tensor_scalar(out=out, in0=a, scalar1=s1, scalar2=s2, op0=op0, **kw), reads=r, writes=[out])

        def stt(out, a, s, b, op0, op1, wk=None):
            r = [a, b] + ([s] if not isinstance(s, (int, float)) else [])
            P.op("dve", lambda e: e.scalar_tensor_tensor(out=out, in0=a, scalar=s, in1=b, op0=op0, op1=op1), reads=r, writes=(wk if wk is not None else [out]))

        def red(out, in_, op=ALU.add):
            P.op("dve", lambda e: e.tensor_reduce(out=out, in_=in_, axis=AX.X, op=op), reads=[in_], writes=[out])

        def recip(out, in_):
            P.op("dve", lambda e: e.reciprocal(out=out, in_=in_), reads=[in_], writes=[out])

        def memset(eng, ap, val):
            P.op(eng, lambda e: e.memset(ap, val), writes=[ap])

        def bc_row(dt_, off, n):
            return bass.AP(dt_, off, [[0, 128], [1, n]])

        xres = sbt(st, "xres", [128, NT, D])
        cst = sbt(st, "cst", [128, 8, 128]); cstb = sbt(st, "cstb", [128, 8, 128], BF16)
        blkind = sbt(st, "blkind_sb", [128, 16])
        rope = sbt(st, "rope_sb", [128, NT, 32]); vis = sbt(st, "vis_sb", [128, 12])
        epsT = sbt(st, "epsT", [128, 1]); oneT = sbt(st, "oneT", [128, 1])
        junk = sbt(st, "junk", [128, D], BF16)
        sA = sbt(st, "sA", [128, 16]); sB = sbt(st, "sB", [128, 16])
        h = sbt(st, "h", [128, D], BF16); hT = sbt(st, "hT", [128, 8, 128], BF16)
        ps = [st.enter_context(nc.psum_tensor("ps%d" % i, [128, 512], F32)) for i in range(8)]
        IDf, IDb = cst[:, 0, :], cstb[:, 0, :]
        dma(xres[:], I["xin"].ap().rearrange("(k p) d -> p k d", p=128))
        dma(cst[:], I["consts"].ap().rearrange("k p n -> p k n"))
        dma(blkind[:], I["blkind"].ap())
        dma(rope[:], I["rope"].ap().rearrange("(k p) d -> p k d", p=128))
        dma(vis[:], I["vis"].ap())
        cp("dve", cstb[:], cst[:])
        memset("dve", epsT[:], EPS); memset("dve", oneT[:], 1.0)

        def rstd_of(out, ss, n):
            actf(out, ss, AF.Sqrt, bias=epsT[:, 0:1], scale=1.0 / n)
            recip(out, out)

        def norm_rows(dst, src, gain, n, col):
            actf(junk[:, 0:n], src, AF.Square, accum=sA[:, col:col + 1])
            rstd_of(sB[:, col:col + 1], sA[:, col:col + 1], n)
            stt(dst, src, sB[:, col:col + 1], gain, ALU.mult, ALU.mult)

        def to_featmajor(dstT, src_bf16, nchunk=8):
            pT = ps[0][:].bitcast(BF16).rearrange("p (k t) -> p k t", k=8)
            for c in range(nchunk):
                tr(pT[:, c, :], src_bf16[:, c * 128:(c + 1) * 128], IDb)
            cp("act", dstT[:, 0:nchunk, :], pT[:, 0:nchunk, :])

        def load_w(wst, wb, dram_ap2d, ncols, kch=8, step=280):
            src = dram_ap2d.rearrange("(k p) n -> p k n", p=128)
            for i, c0 in enumerate(range(0, ncols, step)):
                c1 = min(ncols, c0 + step)
                dma(wst[:, 0:kch, 0:c1 - c0], src[:, :, c0:c1])
                cp("pool" if i % 2 == 0 else "dve", wb[:, 0:kch, c0:c1], wst[:, 0:kch, 0:c1 - c0])

        def proj(dst_f32, srcT, wb, ncols, kch=8):
            for nb in range((ncols + 511) // 512):
                n0, n1 = nb * 512, min(ncols, nb * 512 + 512)
                pp = ps[1 + (nb % 2)]
                for c in range(kch):
                    mm(pp[:, 0:n1 - n0], srcT[:, c, :], wb[:, c, n0:n1], start=(c == 0), stop=(c == kch - 1))
                cp("dve" if nb % 2 == 0 else "act", dst_f32[:, n0:n1], pp[:, 0:n1 - n0])

        with ExitStack() as s0:
            wst0 = sbt(s0, "wst0", [128, 8, 280]); wb0 = sbt(s0, "wb0", [128, 8, 1024], BF16)
            gB0 = sbt(s0, "gB0", [128, D]); g20 = sbt(s0, "g20", [128, 256])
            memx = sbt(s0, "memx", [128, D]); kv0 = sbt(s0, "kv0", [128, D]); kvo = sbt(s0, "kvo", [128, D])
            dma(memx[:], I["mem_rows"].ap()); dma(gB0[:], bc_row(I["g_memm_l"], 0, D)); dma(g20[:], bc_row(I["g_mk_l"], 0, 256))
            norm_rows(h[:], memx[:], gB0[:], D, 0)
            to_featmajor(hT, h)
            load_w(wst0, wb0, I["w_mk_l"].ap(), D)
            proj(kv0, hT, wb0, D)
            for hd in range(4):
                norm_rows(kvo[:, hd * 256:(hd + 1) * 256], kv0[:, hd * 256:(hd + 1) * 256], g20[:], 256, 1 + hd)
            dma(O["mem_k"].ap(), kvo[:])
            load_w(wst0, wb0, I["w_mv_l"].ap(), D)
            proj(kv0, hT, wb0, D)
            dma(O["mem_v"].ap(), kv0[:])
            P.barrier()
        P.barrier()
        for l in range(DEPTH):
            with ExitStack() as sm:
                wst = sbt(sm, "wst", [128, 8, 280])
                wb = sbt(sm, "wb", [128, 8, 2216], BF16)
                ecor = sbt(sm, "ecor", [128, NT, 8])
                CT1 = sbt(sm, "CT1", [128, 2, 128], BF16)
                Hinb = sbt(sm, "Hinb", [128, 256], BF16)
                wuv = sbt(sm, "wuv", [128, 256], BF16)
                wuk2 = sbt(sm, "wuk2", [128, 256], BF16)
                KT1 = sbt(sm, "KT1", [128, 2, 128], BF16)
                KRT1 = sbt(sm, "KRT1", [32, 2, 128], BF16)
                KV1 = sbt(sm, "KV1", [128, 2, 129], BF16)
                RK1 = sbt(sm, "RK1", [128, 2, 4])
                sa = ExitStack(); sa.__enter__()
                gB = sbt(sa, "gB", [128, D])
                g2 = sbt(sa, "g2", [128, 1024])
                tok = sbt(sa, "tok", [128, 2216])
                ckv = sbt(sa, "ckv", [128, 128])
                kr = sbt(sa, "kr", [128, 32])
                r4 = sbt(sa, "r4", [128, 4, 64])
                xT = sbt(sa, "xT", [128, 6, 16, 11])
                xTp = sbt(sa, "xTp", [128, 6, 131])
                cwT = sbt(sa, "cwT", [128, 6, 4])
                cbT = sbt(sa, "cbT", [128, 6])
                acc = sbt(sa, "acc", [128, 128])
                xcT = sbt(sa, "xcT", [128, 6, 128])
                haloA = sbt(sa, "haloA", [128, GRP, 6, 3])
                hrow = None
                u_t = sbt(sa, "u_t", [128, 256])
                v_t = sbt(sa, "v_t", [128, 256])
                vn = sbt(sa, "vn", [128, 256])
                vnb = sbt(sa, "vnb", [128, 256], BF16)
                wm = tok[:, 0:1024].rearrange("p (s g i) -> p s g i", s=2, g=4)
                wmb = sbt(sa, "wmb", [128, 2, 4, 128], BF16)
                bsT = sbt(sa, "bsT", [128, 2, 4])
                dtb = sbt(sa, "dtb", [128, 8])
                aB = sbt(sa, "aB", [128, 8])
                dskB = sbt(sa, "dskB", [128, 8])
                dt = sbt(sa, "dt", [128, 8])
                dtA = sbt(sa, "dtA", [128, 8])
                eac = sbt(sa, "eac", [128, 8])
                te = sbt(sa, "te", [128, 8])
                cd = sbt(sa, "cd", [128, 8])
                cumD = sbt(sa, "cumD", [128, 8])
                Rm = sbt(sa, "Rm", [128, 8, 128])
                sc = sbt(sa, "sc", [128, 2, 128])
                dec = sbt(sa, "dec", [128, 8, 128], BF16)
                MT = sbt(sa, "MT", [128, 8, 128], BF16)
                xst = sbt(sa, "xst", [128, 512])
                Btok = sbt(sa, "Btok", [128, 128], BF16)
                BTm = sbt(sa, "BTm", [128, 2, 128], BF16)
                xdt = sbt(sa, "xdt", [128, 512], BF16)
                xte = sbt(sa, "xte", [128, 512], BF16)
                HT2 = sbt(sa, "HT2", [128, 256])
                HT2b = sbt(sa, "HT2b", [128, 256], BF16)
                ys = sbt(sa, "ys", [128, 512])
                ytmp = sbt(sa, "ytmp", [128, 512])
                SG = sbt(sa, "SG", [128, 264])
                Hin = sbt(sa, "Hin", [128, 256])
                dmr = sbt(sa, "dmr", [128, 8])
                Sm = sbt(sa, "Sm", [128, 256])
                Hf = sbt(sa, "Hf", [128, 256])
                Ho = None
                Hnat = None
                HS1 = sbt(sa, "HS1", [128, 256])
                HS1b = sbt(sa, "HS1b", [128, 256], BF16)
                CZ = [sbt(sa, "CZ%d" % i, [128, 2, 128], BF16) for i in range(2)]
                BZ = [sbt(sa, "BZ%d" % i, [128, 128], BF16) for i in range(2)]
                Wsel = sbt(sa, "Wsel", [128, 16, 8])
                cdS = sbt(sa, "cdS", [128, 16, 8])
                cqb = sbt(sa, "cqb", [128, 256], BF16)
                cqT = sbt(sa, "cqT", [128, 2, 128], BF16)
                wuq = sbt(sa, "wuq", [128, 2, 384], BF16)
                wuk = sbt(sa, "wuk", [128, 256], BF16)
                wukT = sbt(sa, "wukT", [64, 4, 128], BF16)
                gqk = sbt(sa, "gqk", [128, 96])
                q = sbt(sa, "q", [128, 4, 96])
                q2 = sbt(sa, "q2", [128, 4, 96])
                qgb = sbt(sa, "qgb", [128, 4, 96], BF16)
                qnT = sbt(sa, "qnT", [64, 4, 128], BF16)
                qrT = sbt(sa, "qrT", [32, 4, 128], BF16)
                qlT = sbt(sa, "qlT", [128, 512], BF16)
                ckvb = sbt(sa, "ckvb", [128, 128], BF16)
                krb = sbt(sa, "krb", [128, 32], BF16)
                ygb = sbt(sa, "ygb", [128, 256], BF16)
                dma(tok[:, 0:768].rearrange("p (k n) -> p k n", k=2), I["w_uq"].ap()[l].rearrange("(k p) n -> p k n", p=128))
                cp("dve", wuq[:], tok[:, 0:768].rearrange("p (k n) -> p k n", k=2))
                dma(tok[:, 768:1024], I["w_uk"].ap()[l]); cp("dve", wuk[:], tok[:, 768:1024]); cp("dve", wuk2[:], tok[:, 768:1024])
                dma(tok[:, 1024:1280], I["w_uv"].ap()[l]); cp("dve", wuv[:], tok[:, 1024:1280])
                dma(tok[0:64, 1280:1792].rearrange("p (h c) -> p h c", h=4), I["w_ukT"].ap()[l].rearrange("h d c -> d h c"))
                cp("dve", wukT[:], tok[0:64, 1280:1792].rearrange("p (h c) -> p h c", h=4))
                dma(g2[:, 640:896], bc_row(I["g_cq"], l * 256, 256))
                dma(gqk[:], bc_row(I["gq96"], l * 96, 96)); dma(g2[:, 896:992], bc_row(I["gk96"], l * 96, 96))
                tt("dve", gqk[:], gqk[:], g2[:, 896:992], ALU.mult)
                memset("dve", KV1[:, :, 128:129], 1.0)
                Ho = xst[0:64, :]; Hnat = ytmp[0:64, :]
                hrow = xcT[0:48, :, :].rearrange("p m t -> p (m t)")
                dma(dtb[:], bc_row(I["dt_bias"], l * 8, 8)); dma(aB[:], bc_row(I["a_log"], l * 8, 8)); dma(dskB[:], bc_row(I["d_skip"], l * 8, 8))
                actf(aB[:], aB[:], AF.Exp)
                ts("dve", aB[:], aB[:], -1.0, ALU.mult)
                memset("dve", HT2[:], 0.0); memset("dve", HT2b[:], 0.0); memset("dve", cumD[:], 1.0); pass
                def load_hs(b):
                    dma(Hnat.rearrange("p (h n) -> p h n", h=8), I["state_ssm"].ap()[l, b].rearrange("h p n -> p h n"))
                    for hh in range(8):
                        if hh < 4:
                            tr(ps[6][0:64, hh * 64:(hh + 1) * 64], Hnat[:, hh * 64:(hh + 1) * 64], IDf[0:64, 0:64])
                        else:
                            tr(ps[6][:, 256 + (hh - 4) * 64:256 + (hh - 3) * 64], Hnat[:, hh * 64 - 64:hh * 64 + 64], IDf[0:64, 0:64])
                    cp("act", HS1[0:64, :], ps[6][0:64, 0:256])
                    cp("dve", HS1[64:128, :], ps[6][64:128, 256:512])
                load_w(wst, wb, I["w_in"].ap()[l], 2216)
                dma(gB[:], bc_row(I["g_mix"], l * D, D))
                dma(g2[:, 0:128], bc_row(I["g_ckv"], l * 128, 128))
                dma(g2[:, 128:384], bc_row(I["gm_ln_g"], l * 256, 256))
                dma(g2[:, 384:640], bc_row(I["gm_ln_b"], l * 256, 256))
                dma(cwT[:], I["conv_wT"].ap()[l].rearrange("(m c) w -> c m w", c=128))
                dma(cbT[:], I["conv_b"].ap()[l].rearrange("(m c) -> c m", c=128), allow_slow_non_contiguous=True)
                for s_ in range(2):
                    dma(wm[:, s_], I["gm_wsT"].ap()[l, s_].rearrange("g j i -> j g i"))
                dma(bsT[:], I["gm_bsT"].ap()[l].rearrange("s i g -> i s g"))
                tt("dve", wm[:, 0], wm[:, 0], cst[:, 1:2, :].to_broadcast([128, 4, 128]), ALU.mult)
                tt("dve", wm[:, 1], wm[:, 1], cst[:, 4:5, :].to_broadcast([128, 4, 128]), ALU.mult)
                cp("dve", wmb[:], wm)

                def ssm_out(l_, slot):
                    for hl in range(4):
                        tr(ps[3][0:64, hl * 128:(hl + 1) * 128], Hf[:, hl * 64:(hl + 1) * 64], IDf)
                    cp("act", Ho.rearrange("p (g hl n) -> p hl g n", g=2, hl=4), ps[3][0:64, :].rearrange("p (hl g n) -> p hl g n", hl=4, g=2))
                    dma(O["new_ssm"].ap()[l_, slot].rearrange("h p n -> p h n"), Ho.rearrange("p (h n) -> p h n", h=8))

                def front(k):
                    norm_rows(h[:], xres[:, k, :], gB[:], D, 0)
                    to_featmajor(hT, h)
                    proj(tok, hT, wb, 2216)

                front(NP - 1)
                dma(ccH.ap(), tok[125:128, C_XBC:C_XBC + 768])
                dma(O["new_conv"].ap()[l, 16], tok[125:128, C_XBC:C_XBC + 768])
                P.dma(lambda e: e.collective_compute("AllGather", ALU.bypass, replica_groups=[[0, 1, 2, 3], [4, 5, 6, 7]],
                                                     ins=[ccH.ap()], outs=[ccHg.ap()]), reads=["ccH"], writes=["ccHg"], q="pool", inc=1)
                dma(hrow[0:GRP * 3, :], ccHg.ap())
                for m in range(6):
                    tr(ps[6][:, m * 48:m * 48 + GRP * 3], hrow[0:GRP * 3, m * 128:(m + 1) * 128], IDf[0:GRP * 3, 0:GRP * 3])
                cp("act", haloA[:].rearrange("p r m j -> p m r j"), ps[6][:, 0:288].rearrange("p (m x) -> p m x", m=6)[:, :, 0:GRP * 3].rearrange("p m (r j) -> p m r j", j=3))
                ts("dve", xTp[:, :, 0:3], haloA[:, 0], vis[:, 8:9], ALU.mult)
                for r in range(1, GRP):
                    stt(xTp[:, :, 0:3], haloA[:, r], vis[:, 8 + r:9 + r], xTp[:, :, 0:3], ALU.mult, ALU.add)
                dma(hrow[0:48, :], I["state_conv"].ap()[l].rearrange("b j c -> (b j) c"))
                for m in range(6):
                    tr(ps[7][:, m * 48:m * 48 + 48], hrow[0:48, m * 128:(m + 1) * 128], IDf[0:48, 0:48])
                cp("act", xT[:, :, :, 0:3], ps[7][:, 0:288].rearrange("p (m b j) -> p m b j", m=6, j=3))

                for k in range(NT):
                    smp = (k == NP)
                    if k != NP - 1 or NP == 1:
                        front(k)
                    elif NP > 1:
                        front(k)
                    norm_rows(ckv[:], tok[:, C_CKV:C_CKV + 128], g2[:, 0:128], 128, 1)
                    x1, x2 = tok[:, C_KR:C_KR + 16], tok[:, C_KR + 16:C_KR + 32]
                    cs, sn = rope[:, k, 0:16], rope[:, k, 16:32]
                    tt("dve", r4[:, 0, 0:16], x1, cs, ALU.mult); tt("dve", r4[:, 1, 0:16], x2, sn, ALU.mult)
                    tt("dve", kr[:, 0:16], r4[:, 0, 0:16], r4[:, 1, 0:16], ALU.subtract)
                    tt("dve", r4[:, 2, 0:16], x2, cs, ALU.mult); tt("dve", r4[:, 3, 0:16], x1, sn, ALU.mult)
                    tt("dve", kr[:, 16:32], r4[:, 2, 0:16], r4[:, 3, 0:16], ALU.add)
                    dma(O["new_ckv"].ap()[l, k * 128:(k + 1) * 128, :], ckv[:])
                    dma(O["new_kr"].ap()[l, k * 128:(k + 1) * 128, :], kr[:])
                    actf(u_t[:], tok[:, C_U:C_U + 256], AF.Gelu_apprx_tanh)
                    actf(v_t[:], tok[:, C_V:C_V + 256], AF.Gelu_apprx_tanh)
                    v3 = v_t[:].rearrange("p (g d) -> p g d", g=4)
                    red(sA[:, 4:8], v3)
                    ts("dve", sA[:, 4:8], sA[:, 4:8], -1.0 / 64, ALU.mult)
                    tt("dve", vn[:].rearrange("p (g d) -> p g d", g=4), v3, sA[:, 4:8].unsqueeze(2).to_broadcast([128, 4, 64]), ALU.add)
                    tt("dve", r4[:], vn[:].rearrange("p (g d) -> p g d", g=4), vn[:].rearrange("p (g d) -> p g d", g=4), ALU.mult)
                    red(sA[:, 8:12], r4[:])
                    rstd_of(sB[:, 8:12], sA[:, 8:12], 64)
                    tt("dve", vn[:].rearrange("p (g d) -> p g d", g=4), vn[:].rearrange("p (g d) -> p g d", g=4),
                       sB[:, 8:12].unsqueeze(2).to_broadcast([128, 4, 64]), ALU.mult)
                    tt("dve", vn[:], vn[:], g2[:, 128:384], ALU.mult)
                    tt("dve", vn[:], vn[:], g2[:, 384:640], ALU.add)
                    if k >= NP - 1:
                        dma(O["new_gmv"].ap()[l, (k - NP + 1) * 128:(k - NP + 2) * 128, :], vn[:])
                    cp("dve", vnb[:], vn[:])
                    si = 1 if smp else 0
                    for g in range(4):
                        mm(ps[3][:, g * 64:(g + 1) * 64], wmb[:, si, g, :], vnb[:, g * 64:(g + 1) * 64])
                    tt("dve", r4[:], ps[3][:, 0:256].rearrange("p (g d) -> p g d", g=4), bsT[:, si, :].unsqueeze(2).to_broadcast([128, 4, 64]), ALU.add)
                    tt("dve", ygb[:], r4[:].rearrange("p g d -> p (g d)"), u_t[:], ALU.mult)
                    dma(ygm_d.ap()[k], ygb[:])
                    dma(zs_d.ap()[k], tok[:, C_Z:C_Z + 512])
                    cp("pool", ckvb[:], ckv[:]); cp("pool", krb[:], kr[:])
                    cp("pool", KV1[:, si, 0:128], ckv[:])
                    pT0 = ps[0][:].bitcast(BF16)
                    tr(pT0[:, 0:128], ckvb[:], IDb)
                    tr(pT0[0:32, 128:256], krb[:], IDb)
                    cp("act", KT1[:, si, :], pT0[:, 0:128]); cp("act", KRT1[:, si, :], pT0[0:32, 128:256])
                    mm(ps[2][:, 0:256], KT1[:, si, :], wuk[:])
                    actf(r4[:].rearrange("p g d -> p (g d)"), ps[2][:, 0:256], AF.Square)
                    red(sA[:, 12:16], r4[:])
                    actf(junk[:, 0:32], kr[:], AF.Square, accum=sA[:, 2:3])
                    ts("dve", sA[:, 12:16], sA[:, 12:16], sA[:, 2:3], ALU.add)
                    rstd_of(sB[:, 12:16], sA[:, 12:16], 96)
                    ts("dve", RK1[:, si, :], sB[:, 12:16], 96 ** -0.5, ALU.mult)
                    if not smp:
                        dma(ccKT.ap()[0:128, k * 128:(k + 1) * 128], KT1[:, 0, :]); dma(ccKT.ap()[128:160, k * 128:(k + 1) * 128], KRT1[:, 0, :])
                        dma(ccKV.ap()[k * 128:(k + 1) * 128, :], KV1[:, 0, 0:128]); dma(ccRK.ap()[k * 128:(k + 1) * 128, :], RK1[:, 0, :])
                    norm_rows(cqb[:], tok[:, C_CQ:C_CQ + 256], g2[:, 640:896], 256, 3)
                    to_featmajor(cqT, cqb, 2)
                    for c in range(2):
                        mm(ps[1][:, 0:384], cqT[:, c, :], wuq[:, c, :], start=(c == 0), stop=(c == 1))
                    cp("dve", q[:].rearrange("p h d -> p (h d)"), ps[1][:, 0:384])
                    qx1, qx2 = q[:, :, 64:80], q[:, :, 80:96]
                    csb = cs.unsqueeze(1).to_broadcast([128, 4, 16]); snb = sn.unsqueeze(1).to_broadcast([128, 4, 16])
                    tt("dve", r4[:, :, 0:16], qx1, csb, ALU.mult); tt("dve", r4[:, :, 16:32], qx2, snb, ALU.mult)
                    tt("dve", r4[:, :, 32:48], qx2, csb, ALU.mult); tt("dve", r4[:, :, 48:64], qx1, snb, ALU.mult)
                    tt("dve", qx1, r4[:, :, 0:16], r4[:, :, 16:32], ALU.subtract)
                    tt("dve", qx2, r4[:, :, 32:48], r4[:, :, 48:64], ALU.add)
                    tt("dve", q2[:], q[:], q[:], ALU.mult)
                    red(sA[:, 12:16], q2[:])
                    rstd_of(sB[:, 12:16], sA[:, 12:16], 96)
                    tt("dve", q2[:], q[:], sB[:, 12:16].unsqueeze(2).to_broadcast([128, 4, 96]), ALU.mult)
                    tt("dve", qgb[:], q2[:], gqk[:].unsqueeze(1).to_broadcast([128, 4, 96]), ALU.mult)
                    pq = ps[0][:].bitcast(BF16).rearrange("p (k t) -> p k t", k=8)
                    for hh in range(4):
                        tr(pq[0:64, hh, :], qgb[:, hh, 0:64], IDb)
                        tr(pq[0:32, 4 + hh, :], qgb[:, hh, 64:96], IDb)
                    cp("act", qnT[:], pq[0:64, 0:4, :]); cp("act", qrT[:], pq[0:32, 4:8, :])
                    for hh in range(4):
                        mm(ps[1][:, hh * 128:(hh + 1) * 128], wukT[:, hh, :], qnT[:, hh, :])
                    cp("dve", qlT[:], ps[1][:])
                    dma(qlt_d.ap()[k], qlT[:]); dma(qrt_d.ap()[k], qrT[:].rearrange("p h t -> p (h t)"))
                    pX = ps[4][:]
                    pX2 = ps[5][:]
                    for m in range(6):
                        dst = (pX if m < 4 else pX2)[:, (m % 4) * 128:(m % 4 + 1) * 128]
                        tr(dst, tok[:, C_XBC + m * 128:C_XBC + (m + 1) * 128], IDf)
                    if not smp:
                        cp("act", xTp[:, 0:4, 3:131], pX.rearrange("p (m t) -> p m t", m=4))
                        cp("act", xTp[:, 4:6, 3:131], pX2[:, 0:256].rearrange("p (m t) -> p m t", m=2))
                    else:
                        cp("act", xT[:, 0:4, :, 3:11], pX.rearrange("p (m b i) -> p m b i", m=4, b=16))
                        cp("act", xT[:, 4:6, :, 3:11], pX2[:, 0:256].rearrange("p (m b i) -> p m b i", m=2, b=16))
                        for b in range(16):
                            dma(O["new_conv"].ap()[l, b], tok[8 * b + 5:8 * b + 8, C_XBC:C_XBC + 768])
                    for m in range(6):
                        if not smp:
                            win = lambda w: xTp[:, m, w:w + 128]
                            a_ = acc[:]
                        else:
                            win = lambda w: xT[:, m, :, w:w + 8]
                            a_ = acc[:].rearrange("p (b i) -> p b i", b=16)
                        ts("dve", a_, win(0), cwT[:, m, 0:1], ALU.mult)
                        for w in range(1, 4):
                            stt(a_, win(w), cwT[:, m, w:w + 1], a_, ALU.mult, ALU.add)
                        actf(xcT[:, m, :], acc[:], AF.Silu, bias=cbT[:, m:m + 1])
                    if not smp and k + 1 < NP:
                        cp("pool", xTp[:, :, 0:3], xTp[:, :, 128:131])
                    if 'nossd' in DBG:
                        continue
                    TRIm, LSTm, ALLm = cst[:, 1 + 3 * si, :], cst[:, 2 + 3 * si, :], cst[:, 3 + 3 * si, :]
                    tt("dve", dt[:], tok[:, C_DT:C_DT + 8], dtb[:], ALU.add)
                    actf(dt[:], dt[:], AF.Exp)
                    actf(dt[:], dt[:], AF.Ln, bias=oneT[:, 0:1])
                    tt("dve", dtA[:], dt[:], aB[:], ALU.mult)
                    for m in range(4):
                        tr(ps[4][:, m * 128:(m + 1) * 128], xcT[:, m, :], IDf)
                    cp("act", xst[:], ps[4][:])
                    tr(ps[5][:, 0:128], xcT[:, 4, :], IDf)
                    cp("dve", Btok[:], ps[5][:, 0:128])
                    gmask = cst[:, 7, 0:2]
                    tt("pool", BTm[:], xcT[:, 4:5, :].to_broadcast([128, 2, 128]), gmask.unsqueeze(2).to_broadcast([128, 2, 128]), ALU.mult)
                    tt("pool", CT1[:], xcT[:, 5:6, :].to_broadcast([128, 2, 128]), gmask.unsqueeze(2).to_broadcast([128, 2, 128]), ALU.mult)
                    dma(ctm_d.ap()[k], CT1[:])
                    for g in range(2):
                        mm(ps[6][:, g * 128:(g + 1) * 128], BTm[:, g, :], CT1[:, g, :])
                    tt("dve", sc[:], ps[6][:, 0:256].rearrange("p (g i) -> p g i", g=2), TRIm.unsqueeze(1).to_broadcast([128, 2, 128]), ALU.mult)
                    tt("pool", Rm[:], TRIm.unsqueeze(1).to_broadcast([128, 8, 128]), dtA[:].unsqueeze(2).to_broadcast([128, 8, 128]), ALU.mult)
                    mm(ps[1][:], LSTm, Rm[:, 0:4, :].rearrange("p h i -> p (h i)"))
                    mm(ps[2][:], LSTm, Rm[:, 4:8, :].rearrange("p h i -> p (h i)"))
                    actf(dec[:, 0:4, :].rearrange("p h i -> p (h i)"), ps[1][:], AF.Exp)
                    actf(dec[:, 4:8, :].rearrange("p h i -> p (h i)"), ps[2][:], AF.Exp)
                    for g in range(2):
                        tt("dve", MT[:, 4 * g:4 * g + 4, :], dec[:, 4 * g:4 * g + 4, :], sc[:, g:g + 1, :].to_broadcast([128, 4, 128]), ALU.mult)
                    mm(ps[7][:, 0:8], TRIm, dtA[:])
                    mm(ps[7][:, 8:16], ALLm, dtA[:])
                    actf(eac[:], ps[7][:, 0:8], AF.Exp)
                    actf(cd[:], ps[7][:, 8:16], AF.Exp)
                    tt("dve", te[:], ps[7][:, 8:16], ps[7][:, 0:8], ALU.subtract) if False else None
                    cp("dve", te[:], ps[7][:, 0:8])
                    tt("dve", te[:], ps[7][:, 8:16], te[:], ALU.subtract)
                    actf(te[:], te[:], AF.Exp)
                    tt("dve", te[:], te[:], dt[:], ALU.mult)
                    x3 = xst[:].rearrange("p (h d) -> p h d", h=8)
                    tt("dve", xdt[:].rearrange("p (h d) -> p h d", h=8), x3, dt[:].unsqueeze(2).to_broadcast([128, 8, 64]), ALU.mult)
                    tt("pool", xte[:].rearrange("p (h d) -> p h d", h=8), x3, te[:].unsqueeze(2).to_broadcast([128, 8, 64]), ALU.mult)
                    for hh in range(8):
                        mm(ps[4][:, hh * 64:(hh + 1) * 64], MT[:, hh, :], xdt[:, hh * 64:(hh + 1) * 64])
                    if not smp:
                        for g in range(2):
                            mm(ps[5][:, g * 256:(g + 1) * 256], CT1[:, g, :], HT2b[:])
                    else:
                        for b in range(16):
                            load_hs(b)
                            dma(hs_d.ap()[b], HS1[:])
                            cp("pool", HS1b[:], HS1[:])
                            memset("pool", CZ[b % 2][:], 0.0)
                            cp("pool", CZ[b % 2][:, :, 8 * b:8 * b + 8], CT1[:, :, 8 * b:8 * b + 8])
                            for g in range(2):
                                mm((ps[5] if g == 0 else ps[2])[:, 0:256], CZ[b % 2][:, g, :], HS1b[:], start=(b == 0), stop=(b == 15))
                    for g in range(2):
                        yo = (ps[5][:, 256 * g:256 * g + 256] if not smp else (ps[5] if g == 0 else ps[2])[:, 0:256])
                        tt("dve", ys[:, 256 * g:256 * g + 256].rearrange("p (h d) -> p h d", h=4), yo.rearrange("p (h d) -> p h d", h=4),
                           eac[:, 4 * g:4 * g + 4].unsqueeze(2).to_broadcast([128, 4, 64]), ALU.mult)
                    tt("dve", ys[:], ys[:], ps[4][:], ALU.add)
                    tt("pool", ytmp[:].rearrange("p (h d) -> p h d", h=8), x3, dskB[:].unsqueeze(2).to_broadcast([128, 8, 64]), ALU.mult)
                    tt("dve", ys[:], ys[:], ytmp[:], ALU.add)
                    dma(ys_d.ap()[k], ys[:])
                    if not smp:
                        tt("dve", ecor[:, k, :], eac[:], cumD[:], ALU.mult)
                        tt("dve", cumD[:], cumD[:], cd[:], ALU.mult)
                        for g in range(2):
                            mm(ps[6][:, 256 * g:256 * g + 256], Btok[:], xte[:, 256 * g:256 * g + 256])
                        for g in range(2):
                            pg = slice(64 * g, 64 * g + 64)
                            tt("dve", HT2[pg, :].rearrange("p (h d) -> p h d", h=4), HT2[pg, :].rearrange("p (h d) -> p h d", h=4),
                               cd[pg, 4 * g:4 * g + 4].unsqueeze(2).to_broadcast([64, 4, 64]), ALU.mult)
                            tt("dve", HT2[pg, :], HT2[pg, :], ps[6][pg, 256 * g:256 * g + 256], ALU.add)
                        cp("pool", HT2b[:], HT2[:])
                    else:
                        tt("dve", Wsel[:], blkind[:].unsqueeze(2).to_broadcast([128, 16, 8]), dtA[:].unsqueeze(1).to_broadcast([128, 16, 8]), ALU.mult)
                        mm(ps[7][:, 16:144], cst[:, 3, :], Wsel[:].rearrange("p b h -> p (b h)"))
                        actf(cdS[:].rearrange("p b h -> p (b h)"), ps[7][:, 16:144], AF.Exp)
                        for b in range(16):
                            pb = ps[1 + (b % 2)]
                            dma(HS1[:], hs_d.ap()[b])
                            ts("pool", BZ[b % 2][:], Btok[:], blkind[:, b:b + 1], ALU.mult)
                            for g in range(2):
                                mm(pb[:, 256 * g:256 * g + 256], BZ[b % 2][:], xte[:, 256 * g:256 * g + 256])
                            for g in range(2):
                                pg = slice(64 * g, 64 * g + 64)
                                tt("dve", Hf[pg, :].rearrange("p (h d) -> p h d", h=4), HS1[pg, :].rearrange("p (h d) -> p h d", h=4),
                                   cdS[pg, b, 4 * g:4 * g + 4].unsqueeze(2).to_broadcast([64, 4, 64]), ALU.mult)
                                tt("dve", Hf[pg, :], Hf[pg, :], pb[pg, 256 * g:256 * g + 256], ALU.add)
                            ssm_out(l, b)
                if 'noxch' in DBG:
                    P.barrier(); sa.close(); continue
                dma(ccS.ap()[:, 0:256], HT2[:]); dma(ccS.ap()[:, 256:264], cumD[:])
                P.dma(lambda e: e.collective_compute("AllGather", ALU.bypass, replica_groups=[[0, 1, 2, 3], [4, 5, 6, 7]],
                                                     ins=[ccS.ap()], outs=[ccSg.ap()]), reads=["ccS"], writes=["ccSg"], q="pool", inc=1)
                memset("dve", Hin[:], 0.0)
                for r in range(GRP):
                    mcol = vis[:, 4 + r:5 + r]
                    dma(SG[:], ccSg.ap()[r * 128:(r + 1) * 128, :])
                    ts("dve", dmr[:], SG[:, 256:264], -1.0, ALU.add, mcol, ALU.mult)
                    ts("dve", dmr[:], dmr[:], 1.0, ALU.add)
                    ts("dve", Sm[:], SG[:, 0:256], mcol, ALU.mult)
                    for g in range(2):
                        pg = slice(64 * g, 64 * g + 64)
                        tt("dve", Hin[pg, :].rearrange("p (h d) -> p h d", h=4), Hin[pg, :].rearrange("p (h d) -> p h d", h=4),
                           dmr[pg, 4 * g:4 * g + 4].unsqueeze(2).to_broadcast([64, 4, 64]), ALU.mult)
                    tt("dve", Hin[:], Hin[:], Sm[:], ALU.add)
                cp("dve", Hinb[:], Hin[:])
                for g in range(2):
                    pg = slice(64 * g, 64 * g + 64)
                    tt("dve", Hf[pg, :].rearrange("p (h d) -> p h d", h=4), Hin[pg, :].rearrange("p (h d) -> p h d", h=4),
                       cumD[pg, 4 * g:4 * g + 4].unsqueeze(2).to_broadcast([64, 4, 64]), ALU.mult)
                tt("dve", Hf[:], Hf[:], HT2[:], ALU.add)
                ssm_out(l, 16)
                RG = [[0, 1, 2, 3], [4, 5, 6, 7]]
                for (a_, b_) in ((ccKT, ccKTg), (ccKV, ccKVg), (ccRK, ccRKg)):
                    P.dma(lambda e, a_=a_, b_=b_: e.collective_compute("AllGather", ALU.bypass, replica_groups=RG, ins=[a_.ap()], outs=[b_.ap()]),
                          reads=[a_.ap()], writes=[b_.ap()], q="pool", inc=1)
                P.barrier()
                sa.close()
                P.barrier()
                if 'C' not in cfg.phases:
                    continue
                sc_ = ExitStack(); sc_.__enter__()
                NKT = GRP * NP
                NKT = (GRP - 1) * NP
                KTg = sbt(sc_, "KTg", [128, GRP - 1, NPT], BF16); KRTg = sbt(sc_, "KRTg", [32, GRP - 1, NPT], BF16)
                KVg = sbt(sc_, "KVg", [128, NKT, 129], BF16); RKg = sbt(sc_, "RKg", [128, NKT, 4])
                KTl = sbt(sc_, "KTl", [128, NP, 128], BF16); KRTl = sbt(sc_, "KRTl", [32, NP, 128], BF16)
                KVl = sbt(sc_, "KVl", [128, NP, 129], BF16); RKl = sbt(sc_, "RKl", [128, NP, 4])
                qlt = [sbt(sc_, "qlt%d" % i, [128, 512], BF16) for i in range(2)]; qrt = [sbt(sc_, "qrt%d" % i, [32, 512], BF16) for i in range(2)]
                pTb = [sbt(sc_, "pTb%d" % i, [128, 4, 128], BF16) for i in range(2)]
                orec = sbt(sc_, "orec", [128, 4]); olat = sbt(sc_, "olat", [128, 4, 128], BF16); olatT = sbt(sc_, "olatT", [128, 4, 128], BF16)
                cat = sbt(sc_, "cat", [128, 1024], BF16); ysl = sbt(sc_, "ysl", [128, 512]); zl = sbt(sc_, "zl", [128, 512]); gsB = sbt(sc_, "gsB", [128, 512])
                ycor = sbt(sc_, "ycor", [128, 512])
                IDXi = sbt(sc_, "IDXi", [128, 16 * NPG], I32); IDXf = sbt(sc_, "IDXf", [128, 256]); iotaP = sbt(sc_, "iotaP", [128, 1], I32); iotaF = sbt(sc_, "iotaF", [128, 1])
                pgcb = [sbt(sc_, "pgcb%d" % i, [128, 129], BF16) for i in range(2)]; pgkb = [sbt(sc_, "pgkb%d" % i, [128, 32], BF16) for i in range(2)]
                cTs = [sbt(sc_, "cTs%d" % i, [128, 128], BF16) for i in range(2)]; kTs = [sbt(sc_, "kTs%d" % i, [32, 128], BF16) for i in range(2)]
                rks = [sbt(sc_, "rks%d" % i, [128, 8]) for i in range(2)]; pTs = [sbt(sc_, "pTs%d" % i, [128, 4, 8], BF16) for i in range(2)]
                sq = sbt(sc_, "sq", [128, 4, 64]); ols = sbt(sc_, "ols", [32, 128], BF16); odn = sbt(sc_, "odn", [32, 1])
                load_w(wst, wb, I["w_out"].ap()[l], 1024)
                dma(gsB[:], bc_row(I["g_ssd_out"], l * 512, 512))
                for r in range(GRP - 1):
                    dma(KTg[:, r, :], ccKTg.ap()[r * 160:r * 160 + 128, :])
                    dma(KRTg[:, r, :], ccKTg.ap()[r * 160 + 128:r * 160 + 160, :])
                dma(KVg[:, :, 0:128], ccKVg.ap()[0:NKT * 128, :].rearrange("(n p) c -> p n c", p=128))
                dma(RKg[:], ccRKg.ap()[0:NKT * 128, :].rearrange("(n p) c -> p n c", p=128))
                memset("dve", KVg[:, :, 128:129], 1.0); memset("dve", KVl[:, :, 128:129], 1.0)
                dma(KTl[:], ccKT.ap()[0:128, :].rearrange("p (k t) -> p k t", k=NP)); dma(KRTl[:], ccKT.ap()[128:160, :].rearrange("p (k t) -> p k t", k=NP))
                dma(KVl[:, :, 0:128], ccKV.ap().rearrange("(k p) c -> p k c", p=128)); dma(RKl[:], ccRK.ap().rearrange("(k p) c -> p k c", p=128))
                for i in range(2):
                    memset("dve", pgcb[i][:, 128:129], 1.0)
                dma(IDXi[:], bc_row(I["page_idx"], 0, 16 * NPG))
                P.op("pool", lambda e, iotaP=iotaP: e.iota(iotaP[:], pattern=[[0, 1]], base=0, channel_multiplier=1), writes=[iotaP[:]])
                cp("dve", iotaF[:], iotaP[:])
                for c0 in range(0, 16 * NPG, 256):
                    c1 = min(16 * NPG, c0 + 256)
                    cp("dve", IDXf[:, 0:c1 - c0], IDXi[:, c0:c1])
                    ts("dve", IDXf[:, 0:c1 - c0], IDXf[:, 0:c1 - c0], 128.0, ALU.mult, iotaF[:, 0:1], ALU.add)
                    if l > 0:
                        ts("dve", IDXf[:, 0:c1 - c0], IDXf[:, 0:c1 - c0], float(l * NPHYS * 128), ALU.add)
                    cp("dve", IDXi[:, c0:c1], IDXf[:, 0:c1 - c0])
                ckv_flat = I["cache_ckv"].ap().rearrange("l r c -> (l r) c"); kr_flat = I["cache_kr"].ap().rearrange("l r c -> (l r) c")
                TRIb, TRISb = cstb[:, 1, :], cstb[:, 4, :]

                def attn_tail(k):
                    for hh in range(4):
                        ob = ps[3 + hh][:, 0:129]
                        recip(orec[:, hh:hh + 1], ob[:, 128:129])
                        ts("dve", olat[:, hh, :], ob[:, 0:128], orec[:, hh:hh + 1], ALU.mult)
                    po = ps[0][:].bitcast(BF16).rearrange("p (k t) -> p k t", k=8)
                    for hh in range(4):
                        tr(po[:, hh, :], olat[:, hh, :], IDb)
                    cp("act", olatT[:], po[:, 0:4, :])
                    ymla()

                def ymla():
                    for hh in range(4):
                        mm(ps[7][:, hh * 64:(hh + 1) * 64], olatT[:, hh, :], wuv[:, hh * 64:(hh + 1) * 64])
                    cp("act", cat[:, 512:768], ps[7][:, 0:256])

                for k in range(NT):
                    smp = (k == NP)
                    ql, qr = qlt[k % 2], qrt[k % 2]
                    dma(ql[:], qlt_d.ap()[k]); dma(qr[:], qrt_d.ap()[k])
                    if not smp:
                        keys = []
                        for r in range(GRP - 1):
                            for kt in range(NP):
                                keys.append((KTg[:, r, kt * 128:(kt + 1) * 128], KRTg[:, r, kt * 128:(kt + 1) * 128], KVg[:, r * NP + kt, :], RKg[:, r * NP + kt, :], vis[:, r:r + 1], False))
                        for kt in range(k + 1):
                            keys.append((KTl[:, kt, :], KRTl[:, kt, :], KVl[:, kt, :], RKl[:, kt, :], None, kt == k))
                        N = len(keys)
                        for n, (kT_, krT_, kv_, rk_, bias_, diag_) in enumerate(keys):
                            sps = ps[1 + n % 2]; pT = pTb[n % 2]
                            mm(sps[:], kT_, ql[:], start=True, stop=False)
                            mm(sps[:], krT_, qr[:], start=False, stop=True)
                            for hh in range(4):
                                actf(pT[:, hh, :], sps[:, hh * 128:(hh + 1) * 128], AF.Exp, scale=rk_[:, hh:hh + 1], bias=bias_)
                            if diag_:
                                tt("pool", pT[:], pT[:], TRIb.unsqueeze(1).to_broadcast([128, 4, 128]), ALU.mult)
                            for hh in range(4):
                                mm(ps[3 + hh][:, 0:129], pT[:, hh, :], kv_, start=(n == 0), stop=(n == N - 1))
                        attn_tail(k)
                    else:
                        q4 = ql[:].rearrange("p (h t) -> p h t", h=4); qr4 = qr[:].rearrange("p (h t) -> p h t", h=4)
                        for b in range(16 if 'nosamp' not in DBG else 0):
                            for pg in range(NPG + 1):
                                n = b * (NPG + 1) + pg; i2 = n % 2
                                if pg < NPG and 'nopage' in DBG:
                                    continue
                                if pg < NPG:
                                    col = b * NPG + pg
                                    P.dma(lambda e, i2=i2, col=col, pgcb=pgcb, ckv_flat=ckv_flat, IDXi=IDXi: e.indirect_dma_start(out=pgcb[i2][:, 0:128], out_offset=None, in_=ckv_flat,
                                          in_offset=bass.IndirectOffsetOnAxis(ap=IDXi[:, col:col + 1], axis=0)), reads=[IDXi[:]], writes=[pgcb[i2][:]], q="pool")
                                    P.dma(lambda e, i2=i2, col=col, pgkb=pgkb, kr_flat=kr_flat, IDXi=IDXi: e.indirect_dma_start(out=pgkb[i2][:], out_offset=None, in_=kr_flat,
                                          in_offset=bass.IndirectOffsetOnAxis(ap=IDXi[:, col:col + 1], axis=0)), reads=[IDXi[:]], writes=[pgkb[i2][:]], q="pool")
                                    pz = ps[0][:].bitcast(BF16)
                                    tr(pz[:, 0:128], pgcb[i2][:, 0:128], IDb); tr(pz[0:32, 128:256], pgkb[i2][:], IDb)
                                    cp("act", cTs[i2][:], pz[:, 0:128]); cp("act", kTs[i2][:], pz[0:32, 128:256])
                                    mm(ps[2][:, 0:256], cTs[i2][:], wuk2[:])
                                    actf(sq[:].rearrange("p g d -> p (g d)"), ps[2][:, 0:256], AF.Square)
                                    red(rks[i2][:, 0:4], sq[:])
                                    actf(junk[:, 0:32], pgkb[i2][:], AF.Square, accum=rks[i2][:, 4:5])
                                    ts("dve", rks[i2][:, 0:4], rks[i2][:, 0:4], rks[i2][:, 4:5], ALU.add)
                                    rstd_of(rks[i2][:, 0:4], rks[i2][:, 0:4], 96)
                                    ts("dve", rks[i2][:, 0:4], rks[i2][:, 0:4], 96 ** -0.5, ALU.mult)
                                    kT_, krT_, kv_, rk_ = cTs[i2][:], kTs[i2][:], pgcb[i2][:], rks[i2]
                                else:
                                    kT_, krT_, kv_, rk_ = KT1[:, 1, :], KRT1[:, 1, :], KV1[:, 1, :], RK1[:, 1, :]
                                sps = ps[1][:, (n % 4) * 32:(n % 4) * 32 + 32]
                                mm(sps, kT_, q4[:, :, 8 * b:8 * b + 8], start=True, stop=False)
                                mm(sps, krT_, qr4[:, :, 8 * b:8 * b + 8], start=False, stop=True)
                                pT = pTs[i2]
                                for hh in range(4):
                                    actf(pT[:, hh, :], sps[:, hh * 8:(hh + 1) * 8], AF.Exp, scale=rk_[:, hh:hh + 1])
                                if pg == NPG:
                                    tt("pool", pT[:], pT[:], TRISb[:, 8 * b:8 * b + 8].unsqueeze(1).to_broadcast([128, 4, 8]), ALU.mult)
                                mm(ps[3][0:32, 0:129], pT[:].rearrange("p h i -> p (h i)"), kv_, start=(pg == 0), stop=(pg == NPG))
                            recip(odn[:], ps[3][0:32, 128:129])
                            ts("dve", ols[:], ps[3][0:32, 0:128], odn[:, 0:1], ALU.mult)
                            pz = ps[0][:].bitcast(BF16)
                            tr(pz[:, 256:288], ols[:], IDb[0:32, 0:32])
                            cp("act", olatT[:, :, 8 * b:8 * b + 8], pz[:, 256:288].rearrange("p (h i) -> p h i", h=4))
                        ymla()
                    dma(ysl[:], ys_d.ap()[k]); dma(zl[:], zs_d.ap()[k]); dma(cat[:, 768:1024], ygm_d.ap()[k])
                    if not smp:
                        dma(CT1[:], ctm_d.ap()[k])
                        for g in range(2):
                            mm(ps[6][:, 256 * g:256 * g + 256], CT1[:, g, :], Hinb[:])
                        tt("dve", ycor[:].rearrange("p (h d) -> p h d", h=8), ps[6][:].rearrange("p (h d) -> p h d", h=8),
                           ecor[:, k, :].unsqueeze(2).to_broadcast([128, 8, 64]), ALU.mult)
                        tt("dve", ysl[:], ysl[:], ycor[:], ALU.add)
                    actf(zl[:], zl[:], AF.Silu)
                    tt("dve", ysl[:], ysl[:], zl[:], ALU.mult)
                    for g in range(2):
                        norm_rows(cat[:, 256 * g:256 * g + 256], ysl[:, 256 * g:256 * g + 256], gsB[:, 256 * g:256 * g + 256], 256, 4 + g)
                    to_featmajor(hT, cat)
                    for nb in range(2):
                        for c in range(8):
                            mm(ps[6 + nb][:], hT[:, c, :], wb[:, c, nb * 512:(nb + 1) * 512], start=(c == 0), stop=(c == 7))
                        tt("dve", xres[:, k, nb * 512:(nb + 1) * 512], xres[:, k, nb * 512:(nb + 1) * 512], ps[6 + nb][:], ALU.add)
                P.barrier()
                sc_.close()
                P.barrier()
            if 'D' in cfg.phases:
                with ExitStack() as sd:
                    wst = sbt(sd, "wstd", [128, 8, 280]); wq = sbt(sd, "wq", [128, 8, 1024], BF16); wo = sbt(sd, "wo", [128, 8, 1024], BF16)
                    gmx = sbt(sd, "gmx", [128, D]); gq = sbt(sd, "gq", [128, 256]); gk = sbt(sd, "gk", [128, 256]); gmm = sbt(sd, "gmm", [128, D])
                    memx = sbt(sd, "memxd", [128, D]); raw = sbt(sd, "raw", [128, D]); nb_ = sbt(sd, "nb_", [128, D], BF16)
                    hTm = sbt(sd, "hTm", [128, 2, 8, 128], BF16)
                    kT_p = sbt(sd, "kT_p", [128, 4, 2, 256], BF16); vE_p = sbt(sd, "vE_p", [128, 2, 4, 257], BF16)
                    kT_s = sbt(sd, "kT_s", [128, 4, 2, 256], BF16); vE_s = sbt(sd, "vE_s", [128, 2, 4, 257], BF16)
                    qT = sbt(sd, "qT", [128, 4, 2, 128], BF16); pTm = [sbt(sd, "pTm%d" % i, [128, 128], BF16) for i in range(2)]
                    ob = sbt(sd, "ob", [128, D], BF16); orc = sbt(sd, "orc", [128, 4])
                    Kf = sbt(sd, "Kf", [128, 2, D]); Vf = sbt(sd, "Vf", [128, 2, D]); kbs = sbt(sd, "kbs", [128, 2, D], BF16)
                    cmask = sbt(sd, "cmask", [128, 16, 128]); cmaskb = sbt(sd, "cmaskb", [128, 16, 128], BF16)
                    dma(gmx[:], bc_row(I["g_memx"], l * D, D)); dma(gq[:], bc_row(I["g_mq"], l * 256, 256))
                    dma(gk[:], bc_row(I["g_mk"], l * 256, 256)); dma(gmm[:], bc_row(I["g_memm"], l * D, D))
                    dma(cmask[:], I["colmask"].ap()); cp("pool", cmaskb[:], cmask[:])
                    memset("dve", vE_p[:, :, :, 256:257], 1.0); memset("dve", vE_s[:, :, :, 256:257], 1.0)

                    def make_kT(dst, src_b):
                        for mt in range(2):
                            pz = ps[0][:].bitcast(BF16).rearrange("p (k t) -> p k t", k=8)
                            for c8 in range(8):
                                tr(pz[:, c8, :], src_b[:, mt, c8 * 128:(c8 + 1) * 128], IDb)
                            cp("act", dst[:, :, :, mt * 128:(mt + 1) * 128], pz.rearrange("p (h dc) t -> p h dc t", h=4))

                    for mt in range(2):
                        dma(memx[:], I["mem_full"].ap()[mt * 128:(mt + 1) * 128, :])
                        norm_rows(h[:], memx[:], gmm[:], D, 0)
                        to_featmajor(hT, h)
                        cp("pool", hTm[:, mt], hT[:])
                    load_w(wst, wo, I["w_mk"].ap()[l], D)
                    for mt in range(2):
                        proj(raw, hTm[:, mt], wo, D)
                        for hd in range(4):
                            norm_rows(kbs[:, mt, hd * 256:(hd + 1) * 256], raw[:, hd * 256:(hd + 1) * 256], gk[:], 256, 1 + hd)
                    make_kT(kT_p, kbs)
                    load_w(wst, wo, I["w_mv"].ap()[l], D)
                    for mt in range(2):
                        proj(raw, hTm[:, mt], wo, D)
                        cp("pool", vE_p[:, mt, :, 0:256], raw[:].rearrange("p (h d) -> p h d", h=4))
                    load_w(wst, wq, I["w_mq"].ap()[l], D)
                    load_w(wst, wo, I["w_mo"].ap()[l], D)

                    for k in range(NT):
                        smp = (k == NP)
                        norm_rows(h[:], xres[:, k, :], gmx[:], D, 0)
                        to_featmajor(hT, h)
                        proj(raw, hT, wq, D)
                        for hd in range(4):
                            norm_rows(nb_[:, hd * 256:(hd + 1) * 256], raw[:, hd * 256:(hd + 1) * 256], gq[:], 256, 1 + hd)
                        pz = ps[0][:].bitcast(BF16).rearrange("p (k t) -> p k t", k=8)
                        for c8 in range(8):
                            tr(pz[:, c8, :], nb_[:, c8 * 128:(c8 + 1) * 128], IDb)
                        cp("act", qT[:], pz.rearrange("p (h dc) t -> p h dc t", h=4))
                        nseq = 16 if smp else 1
                        it = 0
                        for b in range(nseq):
                            if smp:
                                dma(Kf[:], I["cache_mem_k"].ap()[l, b].rearrange("(mt p) d -> p mt d", p=128))
                                dma(Vf[:], I["cache_mem_v"].ap()[l, b].rearrange("(mt p) d -> p mt d", p=128))
                                cp("pool", kbs[:], Kf[:])
                                for mt in range(2):
                                    cp("dve" if mt == 0 else "pool", vE_s[:, mt, :, 0:256], Vf[:, mt, :].rearrange("p (h d) -> p h d", h=4))
                                make_kT(kT_s, kbs)
                                kT_, vE_ = kT_s, vE_s
                            else:
                                kT_, vE_ = kT_p, vE_p
                            for hd in range(4):
                                for mt in range(2):
                                    sps = ps[1 + it % 2][:, 0:128]; pT = pTm[it % 2]; it += 1
                                    for dc in range(2):
                                        mm(sps, kT_[:, hd, dc, mt * 128:(mt + 1) * 128], qT[:, hd, dc, :], start=(dc == 0), stop=(dc == 1))
                                    actf(pT[:], sps, AF.Exp, scale=1.0 / 16.0)
                                    if smp:
                                        tt("pool", pT[:], pT[:], cmaskb[:, b, :], ALU.mult)
                                    mm(ps[3 + hd][:, 0:257], pT[:], vE_[:, mt, hd, :], start=(b == 0 and mt == 0), stop=(b == nseq - 1 and mt == 1))
                        for hd in range(4):
                            recip(orc[:, hd:hd + 1], ps[3 + hd][:, 256:257])
                            ts("dve", ob[:, hd * 256:(hd + 1) * 256], ps[3 + hd][:, 0:256], orc[:, hd:hd + 1], ALU.mult)
                        to_featmajor(hT, ob)
                        for nb2 in range(2):
                            for c in range(8):
                                mm(ps[1 + nb2][:], hT[:, c, :], wo[:, c, nb2 * 512:(nb2 + 1) * 512], start=(c == 0), stop=(c == 7))
                            tt("dve", xres[:, k, nb2 * 512:(nb2 + 1) * 512], xres[:, k, nb2 * 512:(nb2 + 1) * 512], ps[1 + nb2][:], ALU.add)
                    P.barrier()
                P.barrier()
            if 'E' in cfg.phases:
                with ExitStack() as se0:
                    stg = [sbt(se0, "stg%d" % i, [128, 2048]) for i in range(2)]; stb = [sbt(se0, "stb%d" % i, [128, 2048], BF16) for i in range(2)]
                    uT_v = I["peer_uT"].ap()[l].rearrange("(k p) e -> p k e", p=128); uTb_v = UTb.ap().rearrange("(k p) e -> p k e", p=128)
                    v_v = I["peer_v"].ap()[l].rearrange("(c p) d -> p c d", p=128); vb_v = Vb.ap().rearrange("(c p) d -> p c d", p=128)
                    engs = ("pool", "dve", "act")
                    for i in range(64):
                        a_, b_ = stg[i % 2], stb[i % 2]
                        dma(a_[:].rearrange("p (k e) -> p k e", k=8), uT_v[:, :, i * 256:(i + 1) * 256])
                        cp(engs[i % 3], b_[:], a_[:])
                        dma(uTb_v[:, :, i * 256:(i + 1) * 256], b_[:].rearrange("p (k e) -> p k e", k=8))
                    for i in range(64):
                        a_, b_ = stg[i % 2], stb[i % 2]
                        dma(a_[:].rearrange("p (c d) -> p c d", c=2), v_v[:, 2 * i:2 * i + 2, :])
                        cp(engs[i % 3], b_[:], a_[:])
                        dma(vb_v[:, 2 * i:2 * i + 2, :], b_[:].rearrange("p (c d) -> p c d", c=2))
                    P.barrier()
                P.barrier()
                thr_all = sbt(st, "thr_all", [128, NT, 8]); nb_all = sbt(st, "nb_all", [128, NT, 8])
                with ExitStack() as sea:
                    wst = sbt(sea, "wste", [128, 8, 280]); wpq = sbt(sea, "wpq", [128, 8, 2048], BF16)
                    kT = sbt(sea, "kTe", [128, 16, 128], BF16); gff = sbt(sea, "gff", [128, D])
                    qTs = sbt(sea, "qTs", [128, 16, 128], BF16); S = sbt(sea, "S", [128, 16, 128]); Sw = sbt(sea, "Sw", [128, 256])
                    V16 = sbt(sea, "V16", [128, 16, 16]); cand = sbt(sea, "cand", [128, 8, 256]); SC = sbt(sea, "SC", [128, 8, 16])
                    SCm = sbt(sea, "SCm", [128, 8, 16]); Zs = sbt(sea, "Zs", [128, 8])
                    load_w(wst, wpq, I["w_pq"].ap()[l], 2048)
                    dma(S[:].rearrange("p c k -> p (c k)"), I["peer_kT"].ap()[l].rearrange("d c k -> d (c k)")); cp("dve", kT[:], S[:])
                    dma(gff[:], bc_row(I["g_ffn"], l * D, D))
                    for k in range(NT):
                        norm_rows(h[:], xres[:, k, :], gff[:], D, 0)
                        to_featmajor(hT, h)
                        dma(hT_d.ap()[k], hT[:].rearrange("p k t -> p (k t)"))
                        for c4 in range(4):
                            bank = ps[1 + c4 % 2]
                            for ci in range(4):
                                c = c4 * 4 + ci
                                for kk in range(8):
                                    mm(bank[:, ci * 128:(ci + 1) * 128], wpq[:, kk, c * 128:(c + 1) * 128], hT[:, kk, :], start=(kk == 0), stop=(kk == 7))
                            cp("act", qTs[:, c4 * 4:(c4 + 1) * 4, :], bank[:].rearrange("p (c t) -> p c t", c=4))
                        for c4 in range(4):
                            bank = ps[3 + c4 % 2]
                            for ci in range(4):
                                c = c4 * 4 + ci
                                mm(bank[:, ci * 128:(ci + 1) * 128], qTs[:, c, :], kT[:, c, :])
                            cp("dve", S[:, c4 * 4:(c4 + 1) * 4, :], bank[:].rearrange("p (c t) -> p c t", c=4))
                        dma(s_d.ap()[k], S[:].rearrange("p c k -> p (c k)"))
                        for c in range(16):
                            P.op("dve", lambda e, c=c, S=S, V16=V16: e.max(out=V16[:, c, 0:8], in_=S[:, c, :]), reads=[S[:]], writes=[V16[:]])
                            P.op("dve", lambda e, c=c, S=S, V16=V16, Sw=Sw: e.match_replace(out=Sw[:, 0:128], in_to_replace=V16[:, c, 0:8], in_values=S[:, c, :], imm_value=-1e30), reads=[S[:], V16[:]], writes=[Sw[:]])
                            P.op("dve", lambda e, c=c, V16=V16, Sw=Sw: e.max(out=V16[:, c, 8:16], in_=Sw[:, 0:128]), reads=[Sw[:]], writes=[V16[:]])
                        V4 = V16[:].rearrange("p (h s) k -> p h s k", s=2)
                        tt("pool", cand[:].rearrange("p h (a b) -> p h a b", a=16), V4[:, :, 0, :].unsqueeze(3).to_broadcast([128, 8, 16, 16]),
                           V4[:, :, 1, :].unsqueeze(2).to_broadcast([128, 8, 16, 16]), ALU.add)
                        for hh in range(8):
                            P.op("dve", lambda e, hh=hh, SC=SC, cand=cand: e.max(out=SC[:, hh, 0:8], in_=cand[:, hh, :]), reads=[cand[:]], writes=[SC[:]])
                            P.op("dve", lambda e, hh=hh, SC=SC, cand=cand, Sw=Sw: e.match_replace(out=Sw[:], in_to_replace=SC[:, hh, 0:8], in_values=cand[:, hh, :], imm_value=-1e30), reads=[cand[:], SC[:]], writes=[Sw[:]])
                            P.op("dve", lambda e, hh=hh, SC=SC, Sw=Sw: e.max(out=SC[:, hh, 8:16], in_=Sw[:]), reads=[Sw[:]], writes=[SC[:]])
                        cp("dve", thr_all[:, k, :], SC[:, :, 15])
                        tt("dve", SCm[:], SC[:], SC[:, :, 0:1].to_broadcast([128, 8, 16]), ALU.subtract)
                        actf(SCm[:], SCm[:], AF.Exp)
                        red(Zs[:], SCm[:])
                        actf(Zs[:], Zs[:], AF.Ln)
                        tt("dve", Zs[:], Zs[:], SC[:, :, 0], ALU.add)
                        ts("dve", nb_all[:, k, :], Zs[:], -1.0, ALU.mult)
                    P.barrier()
                P.barrier()
                with ExitStack() as seb:
                    S2 = [sbt(seb, "Sb%d" % i, [128, 16, 128]) for i in range(2)]; SUM = [sbt(seb, "SUM%d" % i, [128, 8, 128]) for i in range(2)]
                    Eb = [sbt(seb, "Eb%d" % i, [128, 1024], BF16) for i in range(2)]; Gh2 = [sbt(seb, "Gh%d" % i, [128, 8, 1024], BF16) for i in range(2)]
                    hT2 = sbt(seb, "hT2", [128, 8, 256], BF16)
                    UTk = [sbt(seb, "UTk%d" % i, [128, 8, 1024], BF16) for i in range(2)]; Vk = [sbt(seb, "Vk%d" % i, [128, 8, 1024], BF16) for i in range(1)]
                    gA = [sbt(seb, "gA%d" % i, [128, 512], BF16) for i in range(2)]; WT = [sbt(seb, "WT%d" % i, [128, 512], BF16) for i in range(2)]
                    uTb_v = UTb.ap().rearrange("(k p) e -> p k e", p=128); vb_v = Vb.ap().rearrange("(c p) d -> p c d", p=128)
                    it = 0
                    for k0 in range(0, NT, 2):
                        gs = min(2, NT - k0)
                        for j in range(gs):
                            dma(S2[j][:].rearrange("p c k -> p (c k)"), s_d.ap()[k0 + j])
                            dma(hT2[:, :, j * 128:(j + 1) * 128], hT_d.ap()[k0 + j].rearrange("p (k t) -> p k t", k=8))
                        for ab in range(16):
                            ub, vb2 = UTk[ab % 2], Vk[0]
                            dma(ub[:], uTb_v[:, :, ab * 1024:(ab + 1) * 1024]); dma(vb2[:], vb_v[:, ab * 8:(ab + 1) * 8, :])
                            for j in range(gs):
                                S = S2[j]
                                for hh in range(8):
                                    sm_, eb_ = SUM[it % 2], Eb[it % 2]; it += 1
                                    tt("pool", sm_[:], S[:, 2 * hh, ab * 8:(ab + 1) * 8].unsqueeze(2).to_broadcast([128, 8, 128]),
                                       S[:, 2 * hh + 1, :].unsqueeze(1).to_broadcast([128, 8, 128]), ALU.add)
                                    sflat = sm_[:].rearrange("p a b -> p (a b)")
                                    actf(eb_[:], sflat, AF.Exp, bias=nb_all[:, k0 + j, hh:hh + 1])
                                    stt(Gh2[j][:, hh, :], sflat, thr_all[:, k0 + j, hh:hh + 1], eb_[:], ALU.is_ge, ALU.mult, wk=[("Gh", j, hh)])
                            for c2 in range(4):
                                gtb, atb = ps[c2 % 2], ps[2 + c2 % 2]
                                for ci in range(2):
                                    cc_ = c2 * 2 + ci
                                    for j in range(gs):
                                        for hh in range(8):
                                            mm(gtb[:, ci * 256 + j * 128:ci * 256 + (j + 1) * 128], Gh2[j][:, hh, cc_ * 128:(cc_ + 1) * 128], IDb, start=(hh == 0), stop=(hh == 7), rk=[("Gh", j, hh), IDb])
                                    for kk in range(8):
                                        mm(atb[:, ci * 256:ci * 256 + gs * 128], ub[:, kk, cc_ * 128:(cc_ + 1) * 128], hT2[:, kk, 0:gs * 128], start=(kk == 0), stop=(kk == 7))
                                actf(gA[c2 % 2][:], atb[:], AF.Gelu_apprx_tanh)
                                tt("dve", WT[c2 % 2][:], gA[c2 % 2][:], gtb[:], ALU.mult)
                                for ci in range(2):
                                    cc_ = c2 * 2 + ci
                                    first = (ab == 0 and cc_ == 0); last = (ab == 15 and cc_ == 7)
                                    for j in range(gs):
                                        for half in range(2):
                                            mm(ps[4 + 2 * j + half][:], WT[c2 % 2][:, ci * 256 + j * 128:ci * 256 + (j + 1) * 128], vb2[:, cc_, half * 512:(half + 1) * 512], start=first, stop=last)
                        for j in range(gs):
                            for half in range(2):
                                tt("dve", xres[:, k0 + j, half * 512:(half + 1) * 512], xres[:, k0 + j, half * 512:(half + 1) * 512], ps[4 + 2 * j + half][:], ALU.add)
                    P.barrier()
                P.barrier()
        dma(O["y"].ap().rearrange("(k p) d -> p k d", p=128), xres[:])
        P.emit(st)
    return nc


def host_inputs(inputs, cfg):
    f = lambda k: np.ascontiguousarray(np.asarray(inputs[k]))
    NP, NT, NPG = cfg.NP, cfg.NT, cfg.NPG
    x_prompt, x_sample = f("x_prompt"), f("x_sample")
    past_len = NPG * 128
    inv = (1.0 / (10000.0 ** (np.arange(16, dtype=np.float32) / 16))).astype(np.float32)
    ii = np.arange(128)
    consts = np.zeros((8, 128, 128), np.float32)
    same = (ii[:, None] // 8) == (ii[None, :] // 8)
    consts[0] = np.eye(128)
    consts[1] = (ii[None, :] >= ii[:, None])
    consts[2] = (ii[:, None] > ii[None, :])
    consts[3] = 1.0
    consts[4] = consts[1] * same
    consts[5] = consts[2] * same
    consts[6] = same
    consts[7, :64, 0] = 1.0
    consts[7, 64:, 1] = 1.0
    blkind = (ii[:, None] // 8 == np.arange(16)[None, :]).astype(np.float32)
    qk = lambda g: np.concatenate([g, g[:, 64:]], axis=1)
    wsT = f("gm_ws").transpose(0, 1, 3, 2)
    wsT_s = np.tile(wsT[:, :, :8, :8], (1, 1, 16, 16))
    bs = f("gm_bs")
    bsT = bs.transpose(0, 2, 1)
    bsT_s = np.tile(bsT[:, :8, :], (1, 16, 1))
    shared = {
        "consts": consts, "blkind": blkind,
        "w_in": f("w_in"), "g_mix": f("g_mix"), "g_ckv": f("g_ckv"), "g_cq": f("g_cq"), "w_uq": f("w_uq"),
        "w_uk": f("w_uk").reshape(-1, 128, 256), "w_ukT": np.ascontiguousarray(f("w_uk").transpose(0, 2, 3, 1)),
        "w_uv": f("w_uv").reshape(-1, 128, 256), "gq96": qk(f("g_qk_q")), "gk96": qk(f("g_qk_k")),
        "conv_wT": np.ascontiguousarray(f("conv_w").transpose(0, 2, 1)), "conv_b": f("conv_b"), "dt_bias": f("dt_bias"),
        "a_log": f("a_log"), "d_skip": f("d_skip"), "g_ssd_out": f("g_ssd_out"),
        "gm_ln_g": f("gm_ln_g"), "gm_ln_b": f("gm_ln_b"),
        "gm_wsT": np.ascontiguousarray(np.stack([wsT, wsT_s], 1)), "gm_bsT": np.ascontiguousarray(np.stack([bsT, bsT_s], 1)),
        "g_ffn": f("g_ffn"), "w_pq": f("w_pq"),
        "peer_kT": np.ascontiguousarray(np.stack([f("peer_k1"), f("peer_k2")], 2).transpose(0, 4, 1, 2, 3).reshape(f("peer_k1").shape[0], 128, 16, 128)),
        "peer_uT": np.ascontiguousarray(f("peer_u").transpose(0, 2, 1)), "peer_v": f("peer_v"),
        "w_out": f("w_out"), "g_memm": f("g_memm"), "g_mk": f("g_mk"), "w_mk": f("w_mk"), "w_mv": f("w_mv"),
        "g_memx": f("g_memx"), "w_mq": f("w_mq"), "g_mq": f("g_mq"), "w_mo": f("w_mo"),
        "colmask": np.ascontiguousarray(np.broadcast_to((np.arange(128)[None, None, :] // 8 == np.arange(16)[None, :, None]), (128, 16, 128)).astype(np.float32)),
        "cache_ckv": f("cache_mla_ckv").reshape(f("cache_mla_ckv").shape[0], -1, 128),
        "cache_kr": f("cache_mla_krope").reshape(f("cache_mla_krope").shape[0], -1, 32),
    }
    maps = []
    for c in range(8):
        g, r = c // 4, c % 4
        xin = np.concatenate([x_prompt[g, r * NP * 128:(r + 1) * NP * 128], x_sample[c * 16:(c + 1) * 16].reshape(128, D)], 0)
        pos = np.concatenate([r * NP * 128 + np.arange(NP * 128), past_len + (np.arange(128) % 8)]).astype(np.float32)
        ang = pos[:, None] * inv[None, :]
        rope = np.concatenate([np.cos(ang), np.sin(ang)], 1).astype(np.float32)
        vis = np.zeros((128, 12), np.float32)
        for rr in range(4):
            vis[:, rr] = 0.0 if rr < r else NEG
            vis[:, 4 + rr] = 1.0 if rr < r else 0.0
            vis[:, 8 + rr] = 1.0 if rr == r - 1 else 0.0
        m = dict(shared)
        ml, half = r // 2, r % 2
        ml = min(ml, f("g_memm").shape[0] - 1)
        m.update({"mem_rows": np.ascontiguousarray(f("mem_prompt")[g, half * 128:(half + 1) * 128]),
                  "g_memm_l": f("g_memm")[ml:ml + 1], "g_mk_l": f("g_mk")[ml:ml + 1], "w_mk_l": f("w_mk")[ml], "w_mv_l": f("w_mv")[ml]})
        m.update({"mem_full": np.ascontiguousarray(f("mem_prompt")[g]),
                  "cache_mem_k": np.ascontiguousarray(f("cache_mem_k")[:, c * 16:(c + 1) * 16].reshape(f("cache_mem_k").shape[0], 16, 256, 1024)),
                  "cache_mem_v": np.ascontiguousarray(f("cache_mem_v")[:, c * 16:(c + 1) * 16].reshape(f("cache_mem_v").shape[0], 16, 256, 1024))})
        m.update({"xin": np.ascontiguousarray(xin), "rope": rope, "vis": vis,
                  "page_idx": np.ascontiguousarray(f("page_table")[c * 16:(c + 1) * 16].reshape(1, -1).astype(np.int32)),
                  "state_conv": np.ascontiguousarray(f("state_conv")[:, c * 16:(c + 1) * 16]),
                  "state_ssm": np.ascontiguousarray(f("state_ssm")[:, c * 16:(c + 1) * 16])})
        maps.append(m)
    return maps


def kernel(**inputs):
    cfg = Cfg(depth=2, np_=16, npg=64, nphys=int(np.asarray(inputs["cache_mla_ckv"]).shape[1]), phases="ACDE")
    NP_, NPT_ = cfg.NP, cfg.NP * 128
    maps = host_inputs(inputs, cfg)
    nc = build(cfg)
    res = run_bass_kernel_spmd(nc, maps, core_ids=list(range(8))).results
    B, SEQ, NS, LS, DEPTH_ = 2, 4 * NPT_, 128, 8, cfg.DEPTH
    z = lambda *s: np.zeros(s, np.float32)
    y_p, y_s = z(B, SEQ, D), z(NS, LS, D)
    ckv_p, kr_p = z(DEPTH_, B, SEQ, 128), z(DEPTH_, B, SEQ, 32)
    ssm_p, conv_p = z(DEPTH_, B, 8, 64, 64), z(DEPTH_, B, 3, 768)
    mk, mv = z(DEPTH_, B, 256, 4, 256), z(DEPTH_, B, 256, 4, 256)
    gmv_p = z(DEPTH_, B, 128, 256)
    ckv_s, kr_s = z(DEPTH_, NS, LS, 128), z(DEPTH_, NS, LS, 32)
    ssm_s, conv_s, gmv_s = z(DEPTH_, NS, 8, 64, 64), z(DEPTH_, NS, 3, 768), z(DEPTH_, NS, LS, 256)
    for c in range(8):
        g, r = c // 4, c % 4
        o = res[c]
        sl = slice(r * NPT_, (r + 1) * NPT_)
        bs = slice(c * 16, (c + 1) * 16)
        y_p[g, sl] = o["y"][:NPT_]
        y_s[bs] = o["y"][NPT_:].reshape(16, 8, D)
        for l in range(DEPTH_):
            ckv_p[l, g, sl] = o["new_ckv"][l, :NPT_]; kr_p[l, g, sl] = o["new_kr"][l, :NPT_]
            ckv_s[l, bs] = o["new_ckv"][l, NPT_:].reshape(16, 8, 128); kr_s[l, bs] = o["new_kr"][l, NPT_:].reshape(16, 8, 32)
            ssm_s[l, bs] = o["new_ssm"][l, :16]; conv_s[l, bs] = o["new_conv"][l, :16]
            gmv_s[l, bs] = o["new_gmv"][l, 128:].reshape(16, 8, 256)
            if r == 3:
                ssm_p[l, g] = o["new_ssm"][l, 16]; conv_p[l, g] = o["new_conv"][l, 16]; gmv_p[l, g] = o["new_gmv"][l, :128]
        ml, half = r // 2, r % 2
        mk[ml, g, half * 128:(half + 1) * 128] = o["mem_k"].reshape(128, 4, 256)
        mv[ml, g, half * 128:(half + 1) * 128] = o["mem_v"].reshape(128, 4, 256)
    return (y_p, y_s, ckv_p, kr_p, ssm_p, conv_p, mk, mv, gmv_p, ckv_s, kr_s, ssm_s, conv_s, gmv_s)
```

```python
import math
from contextlib import ExitStack
import numpy as np
import concourse.bass as bass
import concourse.mybir as mybir
from concourse.bass_utils import run_bass_kernel_spmd

F32 = mybir.dt.float32
BF16 = mybir.dt.bfloat16
I32 = mybir.dt.int32
AF = mybir.ActivationFunctionType
ALU = mybir.AluOpType
AX = mybir.AxisListType
NDMA_SEM = 48
EPS = 1e-6
NEG = -30000.0


class Prog:
    ENGS = ("pe", "act", "dve", "pool", "sp")

    def __init__(self, nc):
        self.nc = nc
        self.ops = {e: [] for e in self.ENGS}
        self.cnt = {e: 0 for e in self.ENGS}
        self.lastw = {}
        self.readers = {}
        self.waited = {e: {} for e in self.ENGS}
        self.ndma = {"sp": 0, "pool": 0}
        self.dma_tokens = {"sp": [], "pool": []}
        self.sems = {}
        self.semval = {}

    @staticmethod
    def _key(r):
        if isinstance(r, (str, tuple)):
            return r
        return r.tensor.name

    def _deps(self, reads, writes):
        deps = []
        for r in reads:
            t = self.lastw.get(r)
            if t is not None:
                deps.append(t)
        for w in writes:
            t = self.lastw.get(w)
            if t is not None:
                deps.append(t)
            deps.extend(self.readers.get(w, ()))
        return deps

    def _waits(self, eng, deps, skip_pe=False):
        waits = []
        wd = self.waited[eng]
        for (semkey, val, deng) in deps:
            if skip_pe and deng == "pe" and eng == "pe":
                continue
            if wd.get(semkey, 0) >= val:
                continue
            wd[semkey] = val
            waits.append((semkey, val))
        return waits

    def _commit(self, tok, reads, writes):
        for r in reads:
            lst = self.readers.setdefault(r, [])
            lst.append(tok)
            if len(lst) > 64:
                best = {}
                for t in lst:
                    if t[0] not in best or best[t[0]][1] < t[1]:
                        best[t[0]] = t
                self.readers[r] = list(best.values())
        for w in writes:
            self.lastw[w] = tok
            self.readers[w] = []

    def op(self, eng, fn, reads=(), writes=()):
        reads = [self._key(r) for r in reads]
        writes = [self._key(w) for w in writes]
        deps = self._deps(reads, writes)
        waits = self._waits(eng, deps, skip_pe=True)
        self.cnt[eng] += 1
        tok = (eng, self.cnt[eng], eng)
        self.ops[eng].append((fn, waits, (eng, 1)))
        self._commit(tok, reads, writes)
        return tok

    def dma(self, fn, reads=(), writes=(), q="sp", inc=16):
        reads = [self._key(r) for r in reads]
        writes = [self._key(w) for w in writes]
        deps = self._deps(reads, writes)
        k = self.ndma[q]
        self.ndma[q] += 1
        semkey = ("dma", q, k % NDMA_SEM)
        prev = self.semval.get(semkey, 0)
        val = prev + inc
        self.semval[semkey] = val
        if prev > 0:
            deps.append((semkey, prev, "dma"))
        waits = self._waits(q, deps)
        tok = (semkey, val, "dma")
        self.ops[q].append((fn, waits, (semkey, inc)))
        self._commit(tok, reads, writes)
        self.dma_tokens[q].append(tok)
        return tok

    def barrier(self):
        toks = [(e, self.cnt[e], e) for e in self.ENGS if self.cnt[e] > 0]
        for q in ("sp", "pool"):
            toks.extend(self.dma_tokens[q][-NDMA_SEM:])
        for e in self.ENGS:
            waits = self._waits(e, toks)
            if waits:
                self.ops[e].append((None, waits, None))
        self.lastw = {}
        self.readers = {}

    def emit(self, stack):
        nc = self.nc
        self.barrier()
        semnames = set(self.ENGS)
        for q in ("sp", "pool"):
            for i in range(min(self.ndma[q], NDMA_SEM)):
                semnames.add(("dma", q, i))
        for sk in sorted(semnames, key=str):
            nm = sk if isinstance(sk, str) else "d_%s_%d" % (sk[1], sk[2])
            self.sems[sk] = stack.enter_context(nc.semaphore("s_" + nm))
        block = stack.enter_context(nc.Block())
        sems = self.sems

        def run(e, lst):
            for (fn, waits, inc) in lst:
                for (sk, val) in waits:
                    e.wait_ge(sems[sk], val)
                if fn is not None:
                    fn(e).then_inc(sems[inc[0]], inc[1])

        @block.tensor
        def _(e):
            run(e, self.ops["pe"])

        @block.scalar
        def _(e):
            run(e, self.ops["act"])

        @block.vector
        def _(e):
            run(e, self.ops["dve"])

        @block.gpsimd
        def _(e):
            run(e, self.ops["pool"])

        @block.sync
        def _(e):
            run(e, self.ops["sp"])


D = 1024
C_Z, C_XBC, C_DT, C_CQ, C_CKV, C_KR, C_U, C_V = 0, 512, 1280, 1288, 1544, 1672, 1704, 1960


class Cfg:
    def __init__(self, depth=2, np_=16, npg=64, nphys=10240, phases="ACDE"):
        self.DEPTH, self.NP, self.NPG, self.NPHYS, self.phases = depth, np_, npg, nphys, phases
        self.NT = np_ + 1
        self.GRP = 4
        self.NC = 8


DBG = set()

def build(cfg):
    NP, NT, DEPTH, GRP, NPG, NPHYS = cfg.NP, cfg.NT, cfg.DEPTH, cfg.GRP, cfg.NPG, cfg.NPHYS
    NPT = NP * 128
    nc = bass.Bass("TRN2", target_bir_lowering=False)
    I = {}
    O = {}

    def inp(name, shape, dt=F32):
        I[name] = nc.dram_tensor(name, list(shape), dt, kind="ExternalInput")
        return I[name]

    def outp(name, shape, dt=F32):
        O[name] = nc.dram_tensor(name, list(shape), dt, kind="ExternalOutput")
        return O[name]

    def scr(name, shape, dt=F32):
        return nc.dram_tensor(name, list(shape), dt, kind="Internal")

    inp("xin", [NT * 128, D]); inp("consts", [8, 128, 128]); inp("blkind", [128, 16]); inp("rope", [NT * 128, 32]); inp("vis", [128, 12])
    inp("w_in", [DEPTH, D, 2216]); inp("g_mix", [DEPTH, D]); inp("g_ckv", [DEPTH, 128]); inp("g_cq", [DEPTH, 256])
    inp("w_uq", [DEPTH, 256, 384]); inp("w_uk", [DEPTH, 128, 256]); inp("w_ukT", [DEPTH, 4, 64, 128]); inp("w_uv", [DEPTH, 128, 256])
    inp("gq96", [DEPTH, 96]); inp("gk96", [DEPTH, 96])
    inp("conv_wT", [DEPTH, 768, 4]); inp("conv_b", [DEPTH, 768]); inp("dt_bias", [DEPTH, 8]); inp("a_log", [DEPTH, 8]); inp("d_skip", [DEPTH, 8])
    inp("g_ssd_out", [DEPTH, 512]); inp("state_conv", [DEPTH, 16, 3, 768]); inp("state_ssm", [DEPTH, 16, 8, 64, 64])
    inp("gm_ln_g", [DEPTH, 256]); inp("gm_ln_b", [DEPTH, 256]); inp("gm_wsT", [DEPTH, 2, 4, 128, 128]); inp("gm_bsT", [DEPTH, 2, 128, 4])
    inp("mem_rows", [128, D]); inp("g_memm_l", [1, D]); inp("g_mk_l", [1, 256]); inp("w_mk_l", [D, D]); inp("w_mv_l", [D, D])
    outp("mem_k", [128, D]); outp("mem_v", [128, D])
    inp("mem_full", [256, D]); inp("g_memm", [DEPTH, D]); inp("g_mk", [DEPTH, 256]); inp("w_mk", [DEPTH, D, D]); inp("w_mv", [DEPTH, D, D])
    inp("g_memx", [DEPTH, D]); inp("w_mq", [DEPTH, D, D]); inp("g_mq", [DEPTH, 256]); inp("w_mo", [DEPTH, D, D])
    inp("cache_mem_k", [DEPTH, 16, 256, D]); inp("cache_mem_v", [DEPTH, 16, 256, D]); inp("colmask", [128, 16, 128])
    inp("g_ffn", [DEPTH, D]); inp("w_pq", [DEPTH, D, 2048]); inp("peer_kT", [DEPTH, 128, 16, 128])
    inp("peer_uT", [DEPTH, D, 16384]); inp("peer_v", [DEPTH, 16384, D])
    inp("w_out", [DEPTH, D, D]); inp("page_idx", [1, 16 * NPG], I32)
    inp("cache_ckv", [DEPTH, NPHYS * 128, 128]); inp("cache_kr", [DEPTH, NPHYS * 128, 32])
    outp("y", [NT * 128, D]); outp("new_ckv", [DEPTH, NT * 128, 128]); outp("new_kr", [DEPTH, NT * 128, 32])
    outp("new_conv", [DEPTH, 17, 3, 768]); outp("new_gmv", [DEPTH, 2 * 128, 256]); outp("new_ssm", [DEPTH, 17, 8, 64, 64])

    UTb = scr("UTb", [D, 16384], BF16); Vb = scr("Vb", [16384, D], BF16)
    s_d = scr("s_d", [NT, 128, 2048]); hT_d = scr("hT_d", [NT, 128, 1024], BF16)
    ccH = scr("ccH", [3, 768]); ccHg = scr("ccHg", [GRP * 3, 768])
    ccS = scr("ccS", [128, 264]); ccSg = scr("ccSg", [GRP * 128, 264])
    ctm_d = scr("ctm_d", [NT, 128, 2, 128], BF16); hs_d = scr("hs_d", [16, 128, 256]); ys_d = scr("ys_d", [NT, 128, 512]); zs_d = scr("zs_d", [NT, 128, 512]); ygm_d = scr("ygm_d", [NT, 128, 256], BF16)
    qlt_d = scr("qlt_d", [NT, 128, 512], BF16); qrt_d = scr("qrt_d", [NT, 32, 512], BF16)
    ccKT = scr("ccKT", [160, NPT], BF16); ccKTg = scr("ccKTg", [GRP * 160, NPT], BF16)
    ccKV = scr("ccKV", [NPT, 128], BF16); ccKVg = scr("ccKVg", [GRP * NPT, 128], BF16)
    ccRK = scr("ccRK", [NPT, 4]); ccRKg = scr("ccRKg", [GRP * NPT, 4])

    P = Prog(nc)
    with ExitStack() as st:
        uniq = [0]

        def sbt(stack, name, shape, dt=F32):
            uniq[0] += 1
            return stack.enter_context(nc.sbuf_tensor("%s_%d" % (name, uniq[0]), list(shape), dt))

        def dma(out, in_, q="sp", **kw):
            P.dma(lambda e: e.dma_start(out=out, in_=in_, **kw), reads=[in_], writes=[out], q=q)

        def mm(out, lhsT, rhs, start=True, stop=True, rk=None):
            P.op("pe", lambda e: e.matmul(out, lhsT=lhsT, rhs=rhs, start=start, stop=stop), reads=(rk if rk is not None else [lhsT, rhs]), writes=[out])

        def tr(out, in_, ident):
            P.op("pe", lambda e: e.transpose(out=out, in_=in_, identity=ident), reads=[in_, ident], writes=[out])

        def actf(out, in_, func, bias=None, scale=None, accum=None):
            kw = {}
            if bias is not None:
                kw["bias"] = bias
            if scale is not None:
                kw["scale"] = scale
            if accum is not None:
                kw["accum_out"] = accum
            r = [in_] + [a for a in (bias, scale) if a is not None and not isinstance(a, (int, float))]
            w = [out] + ([accum] if accum is not None else [])
            P.op("act", lambda e: e.activation(out=out, in_=in_, func=func, **kw), reads=r, writes=w)

        def cp(eng, out, in_):
            if eng == "act":
                P.op("act", lambda e: e.copy(out=out, in_=in_), reads=[in_], writes=[out])
            else:
                P.op(eng, lambda e: e.tensor_copy(out=out, in_=in_), reads=[in_], writes=[out])

        def tt(eng, out, a, b, op):
            P.op(eng, lambda e: e.tensor_tensor(out=out, in0=a, in1=b, op=op), reads=[a, b], writes=[out])

        def ts(eng, out, a, s1, op0, s2=None, op1=None):
            r = [a] + [s for s in (s1, s2) if s is not None and not isinstance(s, (int, float))]
            kw = {"op1": op1} if op1 is not None else {}
            P.op(eng, lambda e: e.tensor_scalar(out=out, in0=a, scalar1=s1, scalar2=s2, op0=op0, **kw), reads=r, writes=[out])

        def stt(out, a, s, b, op0, op1, wk=None):
            r = [a, b] + ([s] if not isinstance(s, (int, float)) else [])
            P.op("dve", lambda e: e.scalar_tensor_tensor(out=out, in0=a, scalar=s, in1=b, op0=op0, op1=op1), reads=r, writes=(wk if wk is not None else [out]))

        def red(out, in_, op=ALU.add):
            P.op("dve", lambda e: e.tensor_reduce(out=out, in_=in_, axis=AX.X, op=op), reads=[in_], writes=[out])

        def recip(out, in_):
            P.op("dve", lambda e: e.reciprocal(out=out, in_=in_), reads=[in_], writes=[out])

        def memset(eng, ap, val):
            P.op(eng, lambda e: e.memset(ap, val), writes=[ap])

        def bc_row(dt_, off, n):
            return bass.AP(dt_, off, [[0, 128], [1, n]])

        xres = sbt(st, "xres", [128, NT, D])
        cst = sbt(st, "cst", [128, 8, 128]); cstb = sbt(st, "cstb", [128, 8, 128], BF16)
        blkind = sbt(st, "blkind_sb", [128, 16])
        rope = sbt(st, "rope_sb", [128, NT, 32]); vis = sbt(st, "vis_sb", [128, 12])
        epsT = sbt(st, "epsT", [128, 1]); oneT = sbt(st, "oneT", [128, 1])
        junk = sbt(st, "junk", [128, D], BF16)
        sA = sbt(st, "sA", [128, 16]); sB = sbt(st, "sB", [128, 16])
        h = sbt(st, "h", [128, D], BF16); hT = sbt(st, "hT", [128, 8, 128], BF16)
        ps = [st.enter_context(nc.psum_tensor("ps%d" % i, [128, 512], F32)) for i in range(8)]
        IDf, IDb = cst[:, 0, :], cstb[:, 0, :]
        dma(xres[:], I["xin"].ap().rearrange("(k p) d -> p k d", p=128))
        dma(cst[:], I["consts"].ap().rearrange("k p n -> p k n"))
        dma(blkind[:], I["blkind"].ap())
        dma(rope[:], I["rope"].ap().rearrange("(k p) d -> p k d", p=128))
        dma(vis[:], I["vis"].ap())
        cp("dve", cstb[:], cst[:])
        memset("dve", epsT[:], EPS); memset("dve", oneT[:], 1.0)

        def rstd_of(out, ss, n):
            actf(out, ss, AF.Sqrt, bias=epsT[:, 0:1], scale=1.0 / n)
            recip(out, out)

        def norm_rows(dst, src, gain, n, col):
            actf(junk[:, 0:n], src, AF.Square, accum=sA[:, col:col + 1])
            rstd_of(sB[:, col:col + 1], sA[:, col:col + 1], n)
            stt(dst, src, sB[:, col:col + 1], gain, ALU.mult, ALU.mult)

        def to_featmajor(dstT, src_bf16, nchunk=8):
            pT = ps[0][:].bitcast(BF16).rearrange("p (k t) -> p k t", k=8)
            for c in range(nchunk):
                tr(pT[:, c, :], src_bf16[:, c * 128:(c + 1) * 128], IDb)
            cp("act", dstT[:, 0:nchunk, :], pT[:, 0:nchunk, :])

        def load_w(wst, wb, dram_ap2d, ncols, kch=8, step=280):
            src = dram_ap2d.rearrange("(k p) n -> p k n", p=128)
            for i, c0 in enumerate(range(0, ncols, step)):
                c1 = min(ncols, c0 + step)
                dma(wst[:, 0:kch, 0:c1 - c0], src[:, :, c0:c1])
                cp("pool" if i % 2 == 0 else "dve", wb[:, 0:kch, c0:c1], wst[:, 0:kch, 0:c1 - c0])

        def proj(dst_f32, srcT, wb, ncols, kch=8):
            for nb in range((ncols + 511) // 512):
                n0, n1 = nb * 512, min(ncols, nb * 512 + 512)
                pp = ps[1 + (nb % 2)]
                for c in range(kch):
                    mm(pp[:, 0:n1 - n0], srcT[:, c, :], wb[:, c, n0:n1], start=(c == 0), stop=(c == kch - 1))
                cp("dve" if nb % 2 == 0 else "act", dst_f32[:, n0:n1], pp[:, 0:n1 - n0])

        with ExitStack() as s0:
            wst0 = sbt(s0, "wst0", [128, 8, 280]); wb0 = sbt(s0, "wb0", [128, 8, 1024], BF16)
            gB0 = sbt(s0, "gB0", [128, D]); g20 = sbt(s0, "g20", [128, 256])
            memx = sbt(s0, "memx", [128, D]); kv0 = sbt(s0, "kv0", [128, D]); kvo = sbt(s0, "kvo", [128, D])
            dma(memx[:], I["mem_rows"].ap()); dma(gB0[:], bc_row(I["g_memm_l"], 0, D)); dma(g20[:], bc_row(I["g_mk_l"], 0, 256))
            norm_rows(h[:], memx[:], gB0[:], D, 0)
            to_featmajor(hT, h)
            load_w(wst0, wb0, I["w_mk_l"].ap(), D)
            proj(kv0, hT, wb0, D)
            for hd in range(4):
                norm_rows(kvo[:, hd * 256:(hd + 1) * 256], kv0[:, hd * 256:(hd + 1) * 256], g20[:], 256, 1 + hd)
            dma(O["mem_k"].ap(), kvo[:])
            load_w(wst0, wb0, I["w_mv_l"].ap(), D)
            proj(kv0, hT, wb0, D)
            dma(O["mem_v"].ap(), kv0[:])
            P.barrier()
        P.barrier()
        for l in range(DEPTH):
            with ExitStack() as sm:
                wst = sbt(sm, "wst", [128, 8, 280])
                wb = sbt(sm, "wb", [128, 8, 2216], BF16)
                ecor = sbt(sm, "ecor", [128, NT, 8])
                CT1 = sbt(sm, "CT1", [128, 2, 128], BF16)
                Hinb = sbt(sm, "Hinb", [128, 256], BF16)
                wuv = sbt(sm, "wuv", [128, 256], BF16)
                wuk2 = sbt(sm, "wuk2", [128, 256], BF16)
                KT1 = sbt(sm, "KT1", [128, 2, 128], BF16)
                KRT1 = sbt(sm, "KRT1", [32, 2, 128], BF16)
                KV1 = sbt(sm, "KV1", [128, 2, 129], BF16)
                RK1 = sbt(sm, "RK1", [128, 2, 4])
                sa = ExitStack(); sa.__enter__()
                gB = sbt(sa, "gB", [128, D])
                g2 = sbt(sa, "g2", [128, 1024])
                tok = sbt(sa, "tok", [128, 2216])
                ckv = sbt(sa, "ckv", [128, 128])
                kr = sbt(sa, "kr", [128, 32])
                r4 = sbt(sa, "r4", [128, 4, 64])
                xT = sbt(sa, "xT", [128, 6, 16, 11])
                xTp = sbt(sa, "xTp", [128, 6, 131])
                cwT = sbt(sa, "cwT", [128, 6, 4])
                cbT = sbt(sa, "cbT", [128, 6])
                acc = sbt(sa, "acc", [128, 128])
                xcT = sbt(sa, "xcT", [128, 6, 128])
                haloA = sbt(sa, "haloA", [128, GRP, 6, 3])
                hrow = None
                u_t = sbt(sa, "u_t", [128, 256])
                v_t = sbt(sa, "v_t", [128, 256])
                vn = sbt(sa, "vn", [128, 256])
                vnb = sbt(sa, "vnb", [128, 256], BF16)
                wm = tok[:, 0:1024].rearrange("p (s g i) -> p s g i", s=2, g=4)
                wmb = sbt(sa, "wmb", [128, 2, 4, 128], BF16)
                bsT = sbt(sa, "bsT", [128, 2, 4])
                dtb = sbt(sa, "dtb", [128, 8])
                aB = sbt(sa, "aB", [128, 8])
                dskB = sbt(sa, "dskB", [128, 8])
                dt = sbt(sa, "dt", [128, 8])
                dtA = sbt(sa, "dtA", [128, 8])
                eac = sbt(sa, "eac", [128, 8])
                te = sbt(sa, "te", [128, 8])
                cd = sbt(sa, "cd", [128, 8])
                cumD = sbt(sa, "cumD", [128, 8])
                Rm = sbt(sa, "Rm", [128, 8, 128])
                sc = sbt(sa, "sc", [128, 2, 128])
                dec = sbt(sa, "dec", [128, 8, 128], BF16)
                MT = sbt(sa, "MT", [128, 8, 128], BF16)
                xst = sbt(sa, "xst", [128, 512])
                Btok = sbt(sa, "Btok", [128, 128], BF16)
                BTm = sbt(sa, "BTm", [128, 2, 128], BF16)
                xdt = sbt(sa, "xdt", [128, 512], BF16)
                xte = sbt(sa, "xte", [128, 512], BF16)
                HT2 = sbt(sa, "HT2", [128, 256])
                HT2b = sbt(sa, "HT2b", [128, 256], BF16)
                ys = sbt(sa, "ys", [128, 512])
                ytmp = sbt(sa, "ytmp", [128, 512])
                SG = sbt(sa, "SG", [128, 264])
                Hin = sbt(sa, "Hin", [128, 256])
                dmr = sbt(sa, "dmr", [128, 8])
                Sm = sbt(sa, "Sm", [128, 256])
                Hf = sbt(sa, "Hf", [128, 256])
                Ho = None
                Hnat = None
                HS1 = sbt(sa, "HS1", [128, 256])
                HS1b = sbt(sa, "HS1b", [128, 256], BF16)
                CZ = [sbt(sa, "CZ%d" % i, [128, 2, 128], BF16) for i in range(2)]
                BZ = [sbt(sa, "BZ%d" % i, [128, 128], BF16) for i in range(2)]
                Wsel = sbt(sa, "Wsel", [128, 16, 8])
                cdS = sbt(sa, "cdS", [128, 16, 8])
                cqb = sbt(sa, "cqb", [128, 256], BF16)
                cqT = sbt(sa, "cqT", [128, 2, 128], BF16)
                wuq = sbt(sa, "wuq", [128, 2, 384], BF16)
                wuk = sbt(sa, "wuk", [128, 256], BF16)
                wukT = sbt(sa, "wukT", [64, 4, 128], BF16)
                gqk = sbt(sa, "gqk", [128, 96])
                q = sbt(sa, "q", [128, 4, 96])
                q2 = sbt(sa, "q2", [128, 4, 96])
                qgb = sbt(sa, "qgb", [128, 4, 96], BF16)
                qnT = sbt(sa, "qnT", [64, 4, 128], BF16)
                qrT = sbt(sa, "qrT", [32, 4, 128], BF16)
                qlT = sbt(sa, "qlT", [128, 512], BF16)
                ckvb = sbt(sa, "ckvb", [128, 128], BF16)
                krb = sbt(sa, "krb", [128, 32], BF16)
                ygb = sbt(sa, "ygb", [128, 256], BF16)
                dma(tok[:, 0:768].rearrange("p (k n) -> p k n", k=2), I["w_uq"].ap()[l].rearrange("(k p) n -> p k n", p=128))
                cp("dve", wuq[:], tok[:, 0:768].rearrange("p (k n) -> p k n", k=2))
                dma(tok[:, 768:1024], I["w_uk"].ap()[l]); cp("dve", wuk[:], tok[:, 768:1024]); cp("dve", wuk2[:], tok[:, 768:1024])
                dma(tok[:, 1024:1280], I["w_uv"].ap()[l]); cp("dve", wuv[:], tok[:, 1024:1280])
                dma(tok[0:64, 1280:1792].rearrange("p (h c) -> p h c", h=4), I["w_ukT"].ap()[l].rearrange("h d c -> d h c"))
                cp("dve", wukT[:], tok[0:64, 1280:1792].rearrange("p (h c) -> p h c", h=4))
                dma(g2[:, 640:896], bc_row(I["g_cq"], l * 256, 256))
                dma(gqk[:], bc_row(I["gq96"], l * 96, 96)); dma(g2[:, 896:992], bc_row(I["gk96"], l * 96, 96))
                tt("dve", gqk[:], gqk[:], g2[:, 896:992], ALU.mult)
                memset("dve", KV1[:, :, 128:129], 1.0)
                Ho = xst[0:64, :]; Hnat = ytmp[0:64, :]
                hrow = xcT[0:48, :, :].rearrange("p m t -> p (m t)")
                dma(dtb[:], bc_row(I["dt_bias"], l * 8, 8)); dma(aB[:], bc_row(I["a_log"], l * 8, 8)); dma(dskB[:], bc_row(I["d_skip"], l * 8, 8))
                actf(aB[:], aB[:], AF.Exp)
                ts("dve", aB[:], aB[:], -1.0, ALU.mult)
                memset("dve", HT2[:], 0.0); memset("dve", HT2b[:], 0.0); memset("dve", cumD[:], 1.0); pass
                def load_hs(b):
                    dma(Hnat.rearrange("p (h n) -> p h n", h=8), I["state_ssm"].ap()[l, b].rearrange("h p n -> p h n"))
                    for hh in range(8):
                        if hh < 4:
                            tr(ps[6][0:64, hh * 64:(hh + 1) * 64], Hnat[:, hh * 64:(hh + 1) * 64], IDf[0:64, 0:64])
                        else:
                            tr(ps[6][:, 256 + (hh - 4) * 64:256 + (hh - 3) * 64], Hnat[:, hh * 64 - 64:hh * 64 + 64], IDf[0:64, 0:64])
                    cp("act", HS1[0:64, :], ps[6][0:64, 0:256])
                    cp("dve", HS1[64:128, :], ps[6][64:128, 256:512])
                load_w(wst, wb, I["w_in"].ap()[l], 2216)
                dma(gB[:], bc_row(I["g_mix"], l * D, D))
                dma(g2[:, 0:128], bc_row(I["g_ckv"], l * 128, 128))
                dma(g2[:, 128:384], bc_row(I["gm_ln_g"], l * 256, 256))
                dma(g2[:, 384:640], bc_row(I["gm_ln_b"], l * 256, 256))
                dma(cwT[:], I["conv_wT"].ap()[l].rearrange("(m c) w -> c m w", c=128))
                dma(cbT[:], I["conv_b"].ap()[l].rearrange("(m c) -> c m", c=128), allow_slow_non_contiguous=True)
                for s_ in range(2):
                    dma(wm[:, s_], I["gm_wsT"].ap()[l, s_].rearrange("g j i -> j g i"))
                dma(bsT[:], I["gm_bsT"].ap()[l].rearrange("s i g -> i s g"))
                tt("dve", wm[:, 0], wm[:, 0], cst[:, 1:2, :].to_broadcast([128, 4, 128]), ALU.mult)
                tt("dve", wm[:, 1], wm[:, 1], cst[:, 4:5, :].to_broadcast([128, 4, 128]), ALU.mult)
                cp("dve", wmb[:], wm)

                def ssm_out(l_, slot):
                    for hl in range(4):
                        tr(ps[3][0:64, hl * 128:(hl + 1) * 128], Hf[:, hl * 64:(hl + 1) * 64], IDf)
                    cp("act", Ho.rearrange("p (g hl n) -> p hl g n", g=2, hl=4), ps[3][0:64, :].rearrange("p (hl g n) -> p hl g n", hl=4, g=2))
                    dma(O["new_ssm"].ap()[l_, slot].rearrange("h p n -> p h n"), Ho.rearrange("p (h n) -> p h n", h=8))

                def front(k):
                    norm_rows(h[:], xres[:, k, :], gB[:], D, 0)
                    to_featmajor(hT, h)
                    proj(tok, hT, wb, 2216)

                front(NP - 1)
                dma(ccH.ap(), tok[125:128, C_XBC:C_XBC + 768])
                dma(O["new_conv"].ap()[l, 16], tok[125:128, C_XBC:C_XBC + 768])
                P.dma(lambda e: e.collective_compute("AllGather", ALU.bypass, replica_groups=[[0, 1, 2, 3], [4, 5, 6, 7]],
                                                     ins=[ccH.ap()], outs=[ccHg.ap()]), reads=["ccH"], writes=["ccHg"], q="pool", inc=1)
                dma(hrow[0:GRP * 3, :], ccHg.ap())
                for m in range(6):
                    tr(ps[6][:, m * 48:m * 48 + GRP * 3], hrow[0:GRP * 3, m * 128:(m + 1) * 128], IDf[0:GRP * 3, 0:GRP * 3])
                cp("act", haloA[:].rearrange("p r m j -> p m r j"), ps[6][:, 0:288].rearrange("p (m x) -> p m x", m=6)[:, :, 0:GRP * 3].rearrange("p m (r j) -> p m r j", j=3))
                ts("dve", xTp[:, :, 0:3], haloA[:, 0], vis[:, 8:9], ALU.mult)
                for r in range(1, GRP):
                    stt(xTp[:, :, 0:3], haloA[:, r], vis[:, 8 + r:9 + r], xTp[:, :, 0:3], ALU.mult, ALU.add)
                dma(hrow[0:48, :], I["state_conv"].ap()[l].rearrange("b j c -> (b j) c"))
                for m in range(6):
                    tr(ps[7][:, m * 48:m * 48 + 48], hrow[0:48, m * 128:(m + 1) * 128], IDf[0:48, 0:48])
                cp("act", xT[:, :, :, 0:3], ps[7][:, 0:288].rearrange("p (m b j) -> p m b j", m=6, j=3))

                for k in range(NT):
                    smp = (k == NP)
                    if k != NP - 1 or NP == 1:
                        front(k)
                    elif NP > 1:
                        front(k)
                    norm_rows(ckv[:], tok[:, C_CKV:C_CKV + 128], g2[:, 0:128], 128, 1)
                    x1, x2 = tok[:, C_KR:C_KR + 16], tok[:, C_KR + 16:C_KR + 32]
                    cs, sn = rope[:, k, 0:16], rope[:, k, 16:32]
                    tt("dve", r4[:, 0, 0:16], x1, cs, ALU.mult); tt("dve", r4[:, 1, 0:16], x2, sn, ALU.mult)
                    tt("dve", kr[:, 0:16], r4[:, 0, 0:16], r4[:, 1, 0:16], ALU.subtract)
                    tt("dve", r4[:, 2, 0:16], x2, cs, ALU.mult); tt("dve", r4[:, 3, 0:16], x1, sn, ALU.mult)
                    tt("dve", kr[:, 16:32], r4[:, 2, 0:16], r4[:, 3, 0:16], ALU.add)
                    dma(O["new_ckv"].ap()[l, k * 128:(k + 1) * 128, :], ckv[:])
                    dma(O["new_kr"].ap()[l, k * 128:(k + 1) * 128, :], kr[:])
                    actf(u_t[:], tok[:, C_U:C_U + 256], AF.Gelu_apprx_tanh)
                    actf(v_t[:], tok[:, C_V:C_V + 256], AF.Gelu_apprx_tanh)
                    v3 = v_t[:].rearrange("p (g d) -> p g d", g=4)
                    red(sA[:, 4:8], v3)
                    ts("dve", sA[:, 4:8], sA[:, 4:8], -1.0 / 64, ALU.mult)
                    tt("dve", vn[:].rearrange("p (g d) -> p g d", g=4), v3, sA[:, 4:8].unsqueeze(2).to_broadcast([128, 4, 64]), ALU.add)
                    tt("dve", r4[:], vn[:].rearrange("p (g d) -> p g d", g=4), vn[:].rearrange("p (g d) -> p g d", g=4), ALU.mult)
                    red(sA[:, 8:12], r4[:])
                    rstd_of(sB[:, 8:12], sA[:, 8:12], 64)
                    tt("dve", vn[:].rearrange("p (g d) -> p g d", g=4), vn[:].rearrange("p (g d) -> p g d", g=4),
                       sB[:, 8:12].unsqueeze(2).to_broadcast([128, 4, 64]), ALU.mult)
                    tt("dve", vn[:], vn[:], g2[:, 128:384], ALU.mult)
                    tt("dve", vn[:], vn[:], g2[:, 384:640], ALU.add)
                    if k >= NP - 1:
                        dma(O["new_gmv"].ap()[l, (k - NP + 1) * 128:(k - NP + 2) * 128, :], vn[:])
                    cp("dve", vnb[:], vn[:])
                    si = 1 if smp else 0
                    for g in range(4):
                        mm(ps[3][:, g * 64:(g + 1) * 64], wmb[:, si, g, :], vnb[:, g * 64:(g + 1) * 64])
                    tt("dve", r4[:], ps[3][:, 0:256].rearrange("p (g d) -> p g d", g=4), bsT[:, si, :].unsqueeze(2).to_broadcast([128, 4, 64]), ALU.add)
                    tt("dve", ygb[:], r4[:].rearrange("p g d -> p (g d)"), u_t[:], ALU.mult)
                    dma(ygm_d.ap()[k], ygb[:])
                    dma(zs_d.ap()[k], tok[:, C_Z:C_Z + 512])
                    cp("pool", ckvb[:], ckv[:]); cp("pool", krb[:], kr[:])
                    cp("pool", KV1[:, si, 0:128], ckv[:])
                    pT0 = ps[0][:].bitcast(BF16)
                    tr(pT0[:, 0:128], ckvb[:], IDb)
                    tr(pT0[0:32, 128:256], krb[:], IDb)
                    cp("act", KT1[:, si, :], pT0[:, 0:128]); cp("act", KRT1[:, si, :], pT0[0:32, 128:256])
                    mm(ps[2][:, 0:256], KT1[:, si, :], wuk[:])
                    actf(r4[:].rearrange("p g d -> p (g d)"), ps[2][:, 0:256], AF.Square)
                    red(sA[:, 12:16], r4[:])
                    actf(junk[:, 0:32], kr[:], AF.Square, accum=sA[:, 2:3])
                    ts("dve", sA[:, 12:16], sA[:, 12:16], sA[:, 2:3], ALU.add)
                    rstd_of(sB[:, 12:16], sA[:, 12:16], 96)
                    ts("dve", RK1[:, si, :], sB[:, 12:16], 96 ** -0.5, ALU.mult)
                    if not smp:
                        dma(ccKT.ap()[0:128, k * 128:(k + 1) * 128], KT1[:, 0, :]); dma(ccKT.ap()[128:160, k * 128:(k + 1) * 128], KRT1[:, 0, :])
                        dma(ccKV.ap()[k * 128:(k + 1) * 128, :], KV1[:, 0, 0:128]); dma(ccRK.ap()[k * 128:(k + 1) * 128, :], RK1[:, 0, :])
                    norm_rows(cqb[:], tok[:, C_CQ:C_CQ + 256], g2[:, 640:896], 256, 3)
                    to_featmajor(cqT, cqb, 2)
                    for c in range(2):
                        mm(ps[1][:, 0:384], cqT[:, c, :], wuq[:, c, :], start=(c == 0), stop=(c == 1))
                    cp("dve", q[:].rearrange("p h d -> p (h d)"), ps[1][:, 0:384])
                    qx1, qx2 = q[:, :, 64:80], q[:, :, 80:96]
                    csb = cs.unsqueeze(1).to_broadcast([128, 4, 16]); snb = sn.unsqueeze(1).to_broadcast([128, 4, 16])
                    tt("dve", r4[:, :, 0:16], qx1, csb, ALU.mult); tt("dve", r4[:, :, 16:32], qx2, snb, ALU.mult)
                    tt("dve", r4[:, :, 32:48], qx2, csb, ALU.mult); tt("dve", r4[:, :, 48:64], qx1, snb, ALU.mult)
                    tt("dve", qx1, r4[:, :, 0:16], r4[:, :, 16:32], ALU.subtract)
                    tt("dve", qx2, r4[:, :, 32:48], r4[:, :, 48:64], ALU.add)
                    tt("dve", q2[:], q[:], q[:], ALU.mult)
                    red(sA[:, 12:16], q2[:])
                    rstd_of(sB[:, 12:16], sA[:, 12:16], 96)
                    tt("dve", q2[:], q[:], sB[:, 12:16].unsqueeze(2).to_broadcast([128, 4, 96]), ALU.mult)
                    tt("dve", qgb[:], q2[:], gqk[:].unsqueeze(1).to_broadcast([128, 4, 96]), ALU.mult)
                    pq = ps[0][:].bitcast(BF16).rearrange("p (k t) -> p k t", k=8)
                    for hh in range(4):
                        tr(pq[0:64, hh, :], qgb[:, hh, 0:64], IDb)
                        tr(pq[0:32, 4 + hh, :], qgb[:, hh, 64:96], IDb)
                    cp("act", qnT[:], pq[0:64, 0:4, :]); cp("act", qrT[:], pq[0:32, 4:8, :])
                    for hh in range(4):
                        mm(ps[1][:, hh * 128:(hh + 1) * 128], wukT[:, hh, :], qnT[:, hh, :])
                    cp("dve", qlT[:], ps[1][:])
                    dma(qlt_d.ap()[k], qlT[:]); dma(qrt_d.ap()[k], qrT[:].rearrange("p h t -> p (h t)"))
                    pX = ps[4][:]
                    pX2 = ps[5][:]
                    for m in range(6):
                        dst = (pX if m < 4 else pX2)[:, (m % 4) * 128:(m % 4 + 1) * 128]
                        tr(dst, tok[:, C_XBC + m * 128:C_XBC + (m + 1) * 128], IDf)
                    if not smp:
                        cp("act", xTp[:, 0:4, 3:131], pX.rearrange("p (m t) -> p m t", m=4))
                        cp("act", xTp[:, 4:6, 3:131], pX2[:, 0:256].rearrange("p (m t) -> p m t", m=2))
                    else:
                        cp("act", xT[:, 0:4, :, 3:11], pX.rearrange("p (m b i) -> p m b i", m=4, b=16))
                        cp("act", xT[:, 4:6, :, 3:11], pX2[:, 0:256].rearrange("p (m b i) -> p m b i", m=2, b=16))
                        for b in range(16):
                            dma(O["new_conv"].ap()[l, b], tok[8 * b + 5:8 * b + 8, C_XBC:C_XBC + 768])
                    for m in range(6):
                        if not smp:
                            win = lambda w: xTp[:, m, w:w + 128]
                            a_ = acc[:]
                        else:
                            win = lambda w: xT[:, m, :, w:w + 8]
                            a_ = acc[:].rearrange("p (b i) -> p b i", b=16)
                        ts("dve", a_, win(0), cwT[:, m, 0:1], ALU.mult)
                        for w in range(1, 4):
                            stt(a_, win(w), cwT[:, m, w:w + 1], a_, ALU.mult, ALU.add)
                        actf(xcT[:, m, :], acc[:], AF.Silu, bias=cbT[:, m:m + 1])
                    if not smp and k + 1 < NP:
                        cp("pool", xTp[:, :, 0:3], xTp[:, :, 128:131])
                    if 'nossd' in DBG:
                        continue
                    TRIm, LSTm, ALLm = cst[:, 1 + 3 * si, :], cst[:, 2 + 3 * si, :], cst[:, 3 + 3 * si, :]
                    tt("dve", dt[:], tok[:, C_DT:C_DT + 8], dtb[:], ALU.add)
                    actf(dt[:], dt[:], AF.Exp)
                    actf(dt[:], dt[:], AF.Ln, bias=oneT[:, 0:1])
                    tt("dve", dtA[:], dt[:], aB[:], ALU.mult)
                    for m in range(4):
                        tr(ps[4][:, m * 128:(m + 1) * 128], xcT[:, m, :], IDf)
                    cp("act", xst[:], ps[4][:])
                    tr(ps[5][:, 0:128], xcT[:, 4, :], IDf)
                    cp("dve", Btok[:], ps[5][:, 0:128])
                    gmask = cst[:, 7, 0:2]
                    tt("pool", BTm[:], xcT[:, 4:5, :].to_broadcast([128, 2, 128]), gmask.unsqueeze(2).to_broadcast([128, 2, 128]), ALU.mult)
                    tt("pool", CT1[:], xcT[:, 5:6, :].to_broadcast([128, 2, 128]), gmask.unsqueeze(2).to_broadcast([128, 2, 128]), ALU.mult)
                    dma(ctm_d.ap()[k], CT1[:])
                    for g in range(2):
                        mm(ps[6][:, g * 128:(g + 1) * 128], BTm[:, g, :], CT1[:, g, :])
                    tt("dve", sc[:], ps[6][:, 0:256].rearrange("p (g i) -> p g i", g=2), TRIm.unsqueeze(1).to_broadcast([128, 2, 128]), ALU.mult)
                    tt("pool", Rm[:], TRIm.unsqueeze(1).to_broadcast([128, 8, 128]), dtA[:].unsqueeze(2).to_broadcast([128, 8, 128]), ALU.mult)
                    mm(ps[1][:], LSTm, Rm[:, 0:4, :].rearrange("p h i -> p (h i)"))
                    mm(ps[2][:], LSTm, Rm[:, 4:8, :].rearrange("p h i -> p (h i)"))
                    actf(dec[:, 0:4, :].rearrange("p h i -> p (h i)"), ps[1][:], AF.Exp)
                    actf(dec[:, 4:8, :].rearrange("p h i -> p (h i)"), ps[2][:], AF.Exp)
                    for g in range(2):
                        tt("dve", MT[:, 4 * g:4 * g + 4, :], dec[:, 4 * g:4 * g + 4, :], sc[:, g:g + 1, :].to_broadcast([128, 4, 128]), ALU.mult)
                    mm(ps[7][:, 0:8], TRIm, dtA[:])
                    mm(ps[7][:, 8:16], ALLm, dtA[:])
                    actf(eac[:], ps[7][:, 0:8], AF.Exp)
                    actf(cd[:], ps[7][:, 8:16], AF.Exp)
                    tt("dve", te[:], ps[7][:, 8:16], ps[7][:, 0:8], ALU.subtract) if False else None
                    cp("dve", te[:], ps[7][:, 0:8])
                    tt("dve", te[:], ps[7][:, 8:16], te[:], ALU.subtract)
                    actf(te[:], te[:], AF.Exp)
                    tt("dve", te[:], te[:], dt[:], ALU.mult)
                    x3 = xst[:].rearrange("p (h d) -> p h d", h=8)
                    tt("dve", xdt[:].rearrange("p (h d) -> p h d", h=8), x3, dt[:].unsqueeze(2).to_broadcast([128, 8, 64]), ALU.mult)
                    tt("pool", xte[:].rearrange("p (h d) -> p h d", h=8), x3, te[:].unsqueeze(2).to_broadcast([128, 8, 64]), ALU.mult)
                    for hh in range(8):
                        mm(ps[4][:, hh * 64:(hh + 1) * 64], MT[:, hh, :], xdt[:, hh * 64:(hh + 1) * 64])
                    if not smp:
                        for g in range(2):
                            mm(ps[5][:, g * 256:(g + 1) * 256], CT1[:, g, :], HT2b[:])
                    else:
                        for b in range(16):
                            load_hs(b)
                            dma(hs_d.ap()[b], HS1[:])
                            cp("pool", HS1b[:], HS1[:])
                            memset("pool", CZ[b % 2][:], 0.0)
                            cp("pool", CZ[b % 2][:, :, 8 * b:8 * b + 8], CT1[:, :, 8 * b:8 * b + 8])
                            for g in range(2):
                                mm((ps[5] if g == 0 else ps[2])[:, 0:256], CZ[b % 2][:, g, :], HS1b[:], start=(b == 0), stop=(b == 15))
                    for g in range(2):
                        yo = (ps[5][:, 256 * g:256 * g + 256] if not smp else (ps[5] if g == 0 else ps[2])[:, 0:256])
                        tt("dve", ys[:, 256 * g:256 * g + 256].rearrange("p (h d) -> p h d", h=4), yo.rearrange("p (h d) -> p h d", h=4),
                           eac[:, 4 * g:4 * g + 4].unsqueeze(2).to_broadcast([128, 4, 64]), ALU.mult)
                    tt("dve", ys[:], ys[:], ps[4][:], ALU.add)
                    tt("pool", ytmp[:].rearrange("p (h d) -> p h d", h=8), x3, dskB[:].unsqueeze(2).to_broadcast([128, 8, 64]), ALU.mult)
                    tt("dve", ys[:], ys[:], ytmp[:], ALU.add)
                    dma(ys_d.ap()[k], ys[:])
                    if not smp:
                        tt("dve", ecor[:, k, :], eac[:], cumD[:], ALU.mult)
                        tt("dve", cumD[:], cumD[:], cd[:], ALU.mult)
                        for g in range(2):
                            mm(ps[6][:, 256 * g:256 * g + 256], Btok[:], xte[:, 256 * g:256 * g + 256])
                        for g in range(2):
                            pg = slice(64 * g, 64 * g + 64)
                            tt("dve", HT2[pg, :].rearrange("p (h d) -> p h d", h=4), HT2[pg, :].rearrange("p (h d) -> p h d", h=4),
                               cd[pg, 4 * g:4 * g + 4].unsqueeze(2).to_broadcast([64, 4, 64]), ALU.mult)
                            tt("dve", HT2[pg, :], HT2[pg, :], ps[6][pg, 256 * g:256 * g + 256], ALU.add)
                        cp("pool", HT2b[:], HT2[:])
                    else:
                        tt("dve", Wsel[:], blkind[:].unsqueeze(2).to_broadcast([128, 16, 8]), dtA[:].unsqueeze(1).to_broadcast([128, 16, 8]), ALU.mult)
                        mm(ps[7][:, 16:144], cst[:, 3, :], Wsel[:].rearrange("p b h -> p (b h)"))
                        actf(cdS[:].rearrange("p b h -> p (b h)"), ps[7][:, 16:144], AF.Exp)
                        for b in range(16):
                            pb = ps[1 + (b % 2)]
                            dma(HS1[:], hs_d.ap()[b])
                            ts("pool", BZ[b % 2][:], Btok[:], blkind[:, b:b + 1], ALU.mult)
                            for g in range(2):
                                mm(pb[:, 256 * g:256 * g + 256], BZ[b % 2][:], xte[:, 256 * g:256 * g + 256])
                            for g in range(2):
                                pg = slice(64 * g, 64 * g + 64)
                                tt("dve", Hf[pg, :].rearrange("p (h d) -> p h d", h=4), HS1[pg, :].rearrange("p (h d) -> p h d", h=4),
                                   cdS[pg, b, 4 * g:4 * g + 4].unsqueeze(2).to_broadcast([64, 4, 64]), ALU.mult)
                                tt("dve", Hf[pg, :], Hf[pg, :], pb[pg, 256 * g:256 * g + 256], ALU.add)
                            ssm_out(l, b)
                if 'noxch' in DBG:
                    P.barrier(); sa.close(); continue
                dma(ccS.ap()[:, 0:256], HT2[:]); dma(ccS.ap()[:, 256:264], cumD[:])
                P.dma(lambda e: e.collective_compute("AllGather", ALU.bypass, replica_groups=[[0, 1, 2, 3], [4, 5, 6, 7]],
                                                     ins=[ccS.ap()], outs=[ccSg.ap()]), reads=["ccS"], writes=["ccSg"], q="pool", inc=1)
                memset("dve", Hin[:], 0.0)
                for r in range(GRP):
                    mcol = vis[:, 4 + r:5 + r]
                    dma(SG[:], ccSg.ap()[r * 128:(r + 1) * 128, :])
                    ts("dve", dmr[:], SG[:, 256:264], -1.0, ALU.add, mcol, ALU.mult)
                    ts("dve", dmr[:], dmr[:], 1.0, ALU.add)
                    ts("dve", Sm[:], SG[:, 0:256], mcol, ALU.mult)
                    for g in range(2):
                        pg = slice(64 * g, 64 * g + 64)
                        tt("dve", Hin[pg, :].rearrange("p (h d) -> p h d", h=4), Hin[pg, :].rearrange("p (h d) -> p h d", h=4),
                           dmr[pg, 4 * g:4 * g + 4].unsqueeze(2).to_broadcast([64, 4, 64]), ALU.mult)
                    tt("dve", Hin[:], Hin[:], Sm[:], ALU.add)
                cp("dve", Hinb[:], Hin[:])
                for g in range(2):
                    pg = slice(64 * g, 64 * g + 64)
                    tt("dve", Hf[pg, :].rearrange("p (h d) -> p h d", h=4), Hin[pg, :].rearrange("p (h d) -> p h d", h=4),
                       cumD[pg, 4 * g:4 * g + 4].unsqueeze(2).to_broadcast([64, 4, 64]), ALU.mult)
                tt("dve", Hf[:], Hf[:], HT2[:], ALU.add)
                ssm_out(l, 16)
                RG = [[0, 1, 2, 3], [4, 5, 6, 7]]
                for (a_, b_) in ((ccKT, ccKTg), (ccKV, ccKVg), (ccRK, ccRKg)):
                    P.dma(lambda e, a_=a_, b_=b_: e.collective_compute("AllGather", ALU.bypass, replica_groups=RG, ins=[a_.ap()], outs=[b_.ap()]),
                          reads=[a_.ap()], writes=[b_.ap()], q="pool", inc=1)
                P.barrier()
                sa.close()
                P.barrier()
                if 'C' not in cfg.phases:
                    continue
                sc_ = ExitStack(); sc_.__enter__()
                NKT = GRP * NP
                NKT = (GRP - 1) * NP
                KTg = sbt(sc_, "KTg", [128, GRP - 1, NPT], BF16); KRTg = sbt(sc_, "KRTg", [32, GRP - 1, NPT], BF16)
                KVg = sbt(sc_, "KVg", [128, NKT, 129], BF16); RKg = sbt(sc_, "RKg", [128, NKT, 4])
                KTl = sbt(sc_, "KTl", [128, NP, 128], BF16); KRTl = sbt(sc_, "KRTl", [32, NP, 128], BF16)
                KVl = sbt(sc_, "KVl", [128, NP, 129], BF16); RKl = sbt(sc_, "RKl", [128, NP, 4])
                qlt = [sbt(sc_, "qlt%d" % i, [128, 512], BF16) for i in range(2)]; qrt = [sbt(sc_, "qrt%d" % i, [32, 512], BF16) for i in range(2)]
                pTb = [sbt(sc_, "pTb%d" % i, [128, 4, 128], BF16) for i in range(2)]
                orec = sbt(sc_, "orec", [128, 4]); olat = sbt(sc_, "olat", [128, 4, 128], BF16); olatT = sbt(sc_, "olatT", [128, 4, 128], BF16)
                cat = sbt(sc_, "cat", [128, 1024], BF16); ysl = sbt(sc_, "ysl", [128, 512]); zl = sbt(sc_, "zl", [128, 512]); gsB = sbt(sc_, "gsB", [128, 512])
                ycor = sbt(sc_, "ycor", [128, 512])
                IDXi = sbt(sc_, "IDXi", [128, 16 * NPG], I32); IDXf = sbt(sc_, "IDXf", [128, 256]); iotaP = sbt(sc_, "iotaP", [128, 1], I32); iotaF = sbt(sc_, "iotaF", [128, 1])
                pgcb = [sbt(sc_, "pgcb%d" % i, [128, 129], BF16) for i in range(2)]; pgkb = [sbt(sc_, "pgkb%d" % i, [128, 32], BF16) for i in range(2)]
                cTs = [sbt(sc_, "cTs%d" % i, [128, 128], BF16) for i in range(2)]; kTs = [sbt(sc_, "kTs%d" % i, [32, 128], BF16) for i in range(2)]
                rks = [sbt(sc_, "rks%d" % i, [128, 8]) for i in range(2)]; pTs = [sbt(sc_, "pTs%d" % i, [128, 4, 8], BF16) for i in range(2)]
                sq = sbt(sc_, "sq", [128, 4, 64]); sc32 = [sbt(sc_, "sc32_%d" % i, [128, 4, 8]) for i in range(2)]; ols = sbt(sc_, "ols", [32, 128], BF16); odn = sbt(sc_, "odn", [32, 1])
                load_w(wst, wb, I["w_out"].ap()[l], 1024)
                dma(gsB[:], bc_row(I["g_ssd_out"], l * 512, 512))
                for r in range(GRP - 1):
                    dma(KTg[:, r, :], ccKTg.ap()[r * 160:r * 160 + 128, :])
                    dma(KRTg[:, r, :], ccKTg.ap()[r * 160 + 128:r * 160 + 160, :])
                dma(KVg[:, :, 0:128], ccKVg.ap()[0:NKT * 128, :].rearrange("(n p) c -> p n c", p=128))
                dma(RKg[:], ccRKg.ap()[0:NKT * 128, :].rearrange("(n p) c -> p n c", p=128))
                memset("dve", KVg[:, :, 128:129], 1.0); memset("dve", KVl[:, :, 128:129], 1.0)
                dma(KTl[:], ccKT.ap()[0:128, :].rearrange("p (k t) -> p k t", k=NP)); dma(KRTl[:], ccKT.ap()[128:160, :].rearrange("p (k t) -> p k t", k=NP))
                dma(KVl[:, :, 0:128], ccKV.ap().rearrange("(k p) c -> p k c", p=128)); dma(RKl[:], ccRK.ap().rearrange("(k p) c -> p k c", p=128))
                for i in range(2):
                    memset("dve", pgcb[i][:, 128:129], 1.0)
                dma(IDXi[:], bc_row(I["page_idx"], 0, 16 * NPG))
                P.op("pool", lambda e, iotaP=iotaP: e.iota(iotaP[:], pattern=[[0, 1]], base=0, channel_multiplier=1), writes=[iotaP[:]])
                cp("dve", iotaF[:], iotaP[:])
                for c0 in range(0, 16 * NPG, 256):
                    c1 = min(16 * NPG, c0 + 256)
                    cp("dve", IDXf[:, 0:c1 - c0], IDXi[:, c0:c1])
                    ts("dve", IDXf[:, 0:c1 - c0], IDXf[:, 0:c1 - c0], 128.0, ALU.mult, iotaF[:, 0:1], ALU.add)
                    if l > 0:
                        ts("dve", IDXf[:, 0:c1 - c0], IDXf[:, 0:c1 - c0], float(l * NPHYS * 128), ALU.add)
                    cp("dve", IDXi[:, c0:c1], IDXf[:, 0:c1 - c0])
                ckv_flat = I["cache_ckv"].ap().rearrange("l r c -> (l r) c"); kr_flat = I["cache_kr"].ap().rearrange("l r c -> (l r) c")
                TRIb, TRISb = cstb[:, 1, :], cstb[:, 4, :]

                def attn_tail(k):
                    for hh in range(4):
                        ob = ps[3 + hh][:, 0:129]
                        recip(orec[:, hh:hh + 1], ob[:, 128:129])
                        ts("dve", olat[:, hh, :], ob[:, 0:128], orec[:, hh:hh + 1], ALU.mult)
                    po = ps[0][:].bitcast(BF16).rearrange("p (k t) -> p k t", k=8)
                    for hh in range(4):
                        tr(po[:, hh, :], olat[:, hh, :], IDb)
                    cp("act", olatT[:], po[:, 0:4, :])
                    ymla()

                def ymla():
                    for hh in range(4):
                        mm(ps[7][:, hh * 64:(hh + 1) * 64], olatT[:, hh, :], wuv[:, hh * 64:(hh + 1) * 64])
                    cp("act", cat[:, 512:768], ps[7][:, 0:256])

                for k in range(NT):
                    smp = (k == NP)
                    ql, qr = qlt[k % 2], qrt[k % 2]
                    dma(ql[:], qlt_d.ap()[k]); dma(qr[:], qrt_d.ap()[k])
                    if not smp:
                        keys = []
                        for r in range(GRP - 1):
                            for kt in range(NP):
                                keys.append((KTg[:, r, kt * 128:(kt + 1) * 128], KRTg[:, r, kt * 128:(kt + 1) * 128], KVg[:, r * NP + kt, :], RKg[:, r * NP + kt, :], vis[:, r:r + 1], False))
                        for kt in range(k + 1):
                            keys.append((KTl[:, kt, :], KRTl[:, kt, :], KVl[:, kt, :], RKl[:, kt, :], None, kt == k))
                        N = len(keys)
                        for n, (kT_, krT_, kv_, rk_, bias_, diag_) in enumerate(keys):
                            sps = ps[1 + n % 2]; pT = pTb[n % 2]
                            mm(sps[:], kT_, ql[:], start=True, stop=False)
                            mm(sps[:], krT_, qr[:], start=False, stop=True)
                            for hh in range(4):
                                actf(pT[:, hh, :], sps[:, hh * 128:(hh + 1) * 128], AF.Exp, scale=rk_[:, hh:hh + 1], bias=bias_)
                            if diag_:
                                tt("pool", pT[:], pT[:], TRIb.unsqueeze(1).to_broadcast([128, 4, 128]), ALU.mult)
                            for hh in range(4):
                                mm(ps[3 + hh][:, 0:129], pT[:, hh, :], kv_, start=(n == 0), stop=(n == N - 1))
                        attn_tail(k)
                    else:
                        q4 = ql[:].rearrange("p (h t) -> p h t", h=4); qr4 = qr[:].rearrange("p (h t) -> p h t", h=4)
                        for b in range(16 if 'nosamp' not in DBG else 0):
                            for pg in range(NPG + 1):
                                n = b * (NPG + 1) + pg; i2 = n % 2
                                if pg < NPG and 'nopage' in DBG:
                                    continue
                                if pg < NPG:
                                    col = b * NPG + pg
                                    P.dma(lambda e, i2=i2, col=col, pgcb=pgcb, ckv_flat=ckv_flat, IDXi=IDXi: e.indirect_dma_start(out=pgcb[i2][:, 0:128], out_offset=None, in_=ckv_flat,
                                          in_offset=bass.IndirectOffsetOnAxis(ap=IDXi[:, col:col + 1], axis=0)), reads=[IDXi[:]], writes=[pgcb[i2][:]], q="pool")
                                    P.dma(lambda e, i2=i2, col=col, pgkb=pgkb, kr_flat=kr_flat, IDXi=IDXi: e.indirect_dma_start(out=pgkb[i2][:], out_offset=None, in_=kr_flat,
                                          in_offset=bass.IndirectOffsetOnAxis(ap=IDXi[:, col:col + 1], axis=0)), reads=[IDXi[:]], writes=[pgkb[i2][:]], q="pool")
                                    pz = ps[0][:].bitcast(BF16)
                                    tr(pz[:, 0:128], pgcb[i2][:, 0:128], IDb); tr(pz[0:32, 128:256], pgkb[i2][:], IDb)
                                    cp("act", cTs[i2][:], pz[:, 0:128]); cp("act", kTs[i2][:], pz[0:32, 128:256])
                                    mm(ps[2][:, 0:256], cTs[i2][:], wuk2[:])
                                    actf(sq[:].rearrange("p g d -> p (g d)"), ps[2][:, 0:256], AF.Square)
                                    red(rks[i2][:, 0:4], sq[:])
                                    actf(junk[:, 0:32], pgkb[i2][:], AF.Square, accum=rks[i2][:, 4:5])
                                    ts("dve", rks[i2][:, 0:4], rks[i2][:, 0:4], rks[i2][:, 4:5], ALU.add)
                                    rstd_of(rks[i2][:, 0:4], rks[i2][:, 0:4], 96)
                                    ts("dve", rks[i2][:, 0:4], rks[i2][:, 0:4], 96 ** -0.5, ALU.mult)
                                    kT_, krT_, kv_, rk_ = cTs[i2][:], kTs[i2][:], pgcb[i2][:], rks[i2]
                                else:
                                    kT_, krT_, kv_, rk_ = KT1[:, 1, :], KRT1[:, 1, :], KV1[:, 1, :], RK1[:, 1, :]
                                sps = ps[1][:, (n % 4) * 32:(n % 4) * 32 + 32]
                                mm(sps, kT_, q4[:, :, 8 * b:8 * b + 8], start=True, stop=False)
                                mm(sps, krT_, qr4[:, :, 8 * b:8 * b + 8], start=False, stop=True)
                                pT = pTs[i2]
                                tt("dve", sc32[i2][:], sps.rearrange("p (h i) -> p h i", h=4), rk_[:, 0:4].unsqueeze(2).to_broadcast([128, 4, 8]), ALU.mult)
                                actf(pT[:].rearrange("p h i -> p (h i)"), sc32[i2][:].rearrange("p h i -> p (h i)"), AF.Exp)
                                if pg == NPG:
                                    tt("pool", pT[:], pT[:], TRISb[:, 8 * b:8 * b + 8].unsqueeze(1).to_broadcast([128, 4, 8]), ALU.mult)
                                mm(ps[3][0:32, 0:129], pT[:].rearrange("p h i -> p (h i)"), kv_, start=(pg == 0), stop=(pg == NPG))
                            recip(odn[:], ps[3][0:32, 128:129])
                            ts("dve", ols[:], ps[3][0:32, 0:128], odn[:, 0:1], ALU.mult)
                            pz = ps[0][:].bitcast(BF16)
                            tr(pz[:, 256:288], ols[:], IDb[0:32, 0:32])
                            cp("act", olatT[:, :, 8 * b:8 * b + 8], pz[:, 256:288].rearrange("p (h i) -> p h i", h=4))
                        ymla()
                    dma(ysl[:], ys_d.ap()[k]); dma(zl[:], zs_d.ap()[k]); dma(cat[:, 768:1024], ygm_d.ap()[k])
                    if not smp:
                        dma(CT1[:], ctm_d.ap()[k])
                        for g in range(2):
                            mm(ps[6][:, 256 * g:256 * g + 256], CT1[:, g, :], Hinb[:])
                        tt("dve", ycor[:].rearrange("p (h d) -> p h d", h=8), ps[6][:].rearrange("p (h d) -> p h d", h=8),
                           ecor[:, k, :].unsqueeze(2).to_broadcast([128, 8, 64]), ALU.mult)
                        tt("dve", ysl[:], ysl[:], ycor[:], ALU.add)
                    actf(zl[:], zl[:], AF.Silu)
                    tt("dve", ysl[:], ysl[:], zl[:], ALU.mult)
                    for g in range(2):
                        norm_rows(cat[:, 256 * g:256 * g + 256], ysl[:, 256 * g:256 * g + 256], gsB[:, 256 * g:256 * g + 256], 256, 4 + g)
                    to_featmajor(hT, cat)
                    for nb in range(2):
                        for c in range(8):
                            mm(ps[6 + nb][:], hT[:, c, :], wb[:, c, nb * 512:(nb + 1) * 512], start=(c == 0), stop=(c == 7))
                        tt("dve", xres[:, k, nb * 512:(nb + 1) * 512], xres[:, k, nb * 512:(nb + 1) * 512], ps[6 + nb][:], ALU.add)
                P.barrier()
                sc_.close()
                P.barrier()
            if 'D' in cfg.phases:
                with ExitStack() as sd:
                    wst = sbt(sd, "wstd", [128, 8, 280]); wq = sbt(sd, "wq", [128, 8, 1024], BF16); wo = sbt(sd, "wo", [128, 8, 1024], BF16)
                    gmx = sbt(sd, "gmx", [128, D]); gq = sbt(sd, "gq", [128, 256]); gk = sbt(sd, "gk", [128, 256]); gmm = sbt(sd, "gmm", [128, D])
                    memx = sbt(sd, "memxd", [128, D]); raw = sbt(sd, "raw", [128, D]); nb_ = sbt(sd, "nb_", [128, D], BF16)
                    hTm = sbt(sd, "hTm", [128, 2, 8, 128], BF16)
                    kT_p = sbt(sd, "kT_p", [128, 4, 2, 256], BF16); vE_p = sbt(sd, "vE_p", [128, 2, 4, 257], BF16)
                    kT_s = sbt(sd, "kT_s", [128, 4, 2, 256], BF16); vE_s = sbt(sd, "vE_s", [128, 2, 4, 257], BF16)
                    qT = sbt(sd, "qT", [128, 4, 2, 128], BF16); pTm = [sbt(sd, "pTm%d" % i, [128, 128], BF16) for i in range(2)]
                    ob = sbt(sd, "ob", [128, D], BF16); orc = sbt(sd, "orc", [128, 4])
                    Kf = sbt(sd, "Kf", [128, 2, D]); Vf = sbt(sd, "Vf", [128, 2, D]); kbs = sbt(sd, "kbs", [128, 2, D], BF16)
                    cmask = sbt(sd, "cmask", [128, 16, 128]); cmaskb = sbt(sd, "cmaskb", [128, 16, 128], BF16)
                    dma(gmx[:], bc_row(I["g_memx"], l * D, D)); dma(gq[:], bc_row(I["g_mq"], l * 256, 256))
                    dma(gk[:], bc_row(I["g_mk"], l * 256, 256)); dma(gmm[:], bc_row(I["g_memm"], l * D, D))
                    dma(cmask[:], I["colmask"].ap()); cp("pool", cmaskb[:], cmask[:])
                    memset("dve", vE_p[:, :, :, 256:257], 1.0); memset("dve", vE_s[:, :, :, 256:257], 1.0)

                    def make_kT(dst, src_b):
                        for mt in range(2):
                            pz = ps[0][:].bitcast(BF16).rearrange("p (k t) -> p k t", k=8)
                            for c8 in range(8):
                                tr(pz[:, c8, :], src_b[:, mt, c8 * 128:(c8 + 1) * 128], IDb)
                            cp("act", dst[:, :, :, mt * 128:(mt + 1) * 128], pz.rearrange("p (h dc) t -> p h dc t", h=4))

                    for mt in range(2):
                        dma(memx[:], I["mem_full"].ap()[mt * 128:(mt + 1) * 128, :])
                        norm_rows(h[:], memx[:], gmm[:], D, 0)
                        to_featmajor(hT, h)
                        cp("pool", hTm[:, mt], hT[:])
                    load_w(wst, wo, I["w_mk"].ap()[l], D)
                    for mt in range(2):
                        proj(raw, hTm[:, mt], wo, D)
                        for hd in range(4):
                            norm_rows(kbs[:, mt, hd * 256:(hd + 1) * 256], raw[:, hd * 256:(hd + 1) * 256], gk[:], 256, 1 + hd)
                    make_kT(kT_p, kbs)
                    load_w(wst, wo, I["w_mv"].ap()[l], D)
                    for mt in range(2):
                        proj(raw, hTm[:, mt], wo, D)
                        cp("pool", vE_p[:, mt, :, 0:256], raw[:].rearrange("p (h d) -> p h d", h=4))
                    load_w(wst, wq, I["w_mq"].ap()[l], D)
                    load_w(wst, wo, I["w_mo"].ap()[l], D)

                    for k in range(NT):
                        smp = (k == NP)
                        norm_rows(h[:], xres[:, k, :], gmx[:], D, 0)
                        to_featmajor(hT, h)
                        proj(raw, hT, wq, D)
                        for hd in range(4):
                            norm_rows(nb_[:, hd * 256:(hd + 1) * 256], raw[:, hd * 256:(hd + 1) * 256], gq[:], 256, 1 + hd)
                        pz = ps[0][:].bitcast(BF16).rearrange("p (k t) -> p k t", k=8)
                        for c8 in range(8):
                            tr(pz[:, c8, :], nb_[:, c8 * 128:(c8 + 1) * 128], IDb)
                        cp("act", qT[:], pz.rearrange("p (h dc) t -> p h dc t", h=4))
                        nseq = 16 if smp else 1
                        it = 0
                        for b in range(nseq):
                            if smp:
                                dma(Kf[:], I["cache_mem_k"].ap()[l, b].rearrange("(mt p) d -> p mt d", p=128))
                                dma(Vf[:], I["cache_mem_v"].ap()[l, b].rearrange("(mt p) d -> p mt d", p=128))
                                cp("pool", kbs[:], Kf[:])
                                for mt in range(2):
                                    cp("dve" if mt == 0 else "pool", vE_s[:, mt, :, 0:256], Vf[:, mt, :].rearrange("p (h d) -> p h d", h=4))
                                make_kT(kT_s, kbs)
                                kT_, vE_ = kT_s, vE_s
                            else:
                                kT_, vE_ = kT_p, vE_p
                            for hd in range(4):
                                for mt in range(2):
                                    sps = ps[1 + it % 2][:, 0:128]; pT = pTm[it % 2]; it += 1
                                    for dc in range(2):
                                        mm(sps, kT_[:, hd, dc, mt * 128:(mt + 1) * 128], qT[:, hd, dc, :], start=(dc == 0), stop=(dc == 1))
                                    actf(pT[:], sps, AF.Exp, scale=1.0 / 16.0)
                                    if smp:
                                        tt("pool", pT[:], pT[:], cmaskb[:, b, :], ALU.mult)
                                    mm(ps[3 + hd][:, 0:257], pT[:], vE_[:, mt, hd, :], start=(b == 0 and mt == 0), stop=(b == nseq - 1 and mt == 1))
                        for hd in range(4):
                            recip(orc[:, hd:hd + 1], ps[3 + hd][:, 256:257])
                            ts("dve", ob[:, hd * 256:(hd + 1) * 256], ps[3 + hd][:, 0:256], orc[:, hd:hd + 1], ALU.mult)
                        to_featmajor(hT, ob)
                        for nb2 in range(2):
                            for c in range(8):
                                mm(ps[1 + nb2][:], hT[:, c, :], wo[:, c, nb2 * 512:(nb2 + 1) * 512], start=(c == 0), stop=(c == 7))
                            tt("dve", xres[:, k, nb2 * 512:(nb2 + 1) * 512], xres[:, k, nb2 * 512:(nb2 + 1) * 512], ps[1 + nb2][:], ALU.add)
                    P.barrier()
                P.barrier()
            if 'E' in cfg.phases:
                with ExitStack() as se0:
                    stg = [sbt(se0, "stg%d" % i, [128, 2048]) for i in range(2)]; stb = [sbt(se0, "stb%d" % i, [128, 2048], BF16) for i in range(2)]
                    uT_v = I["peer_uT"].ap()[l].rearrange("(k p) e -> p k e", p=128); uTb_v = UTb.ap().rearrange("(k p) e -> p k e", p=128)
                    v_v = I["peer_v"].ap()[l].rearrange("(c p) d -> p c d", p=128); vb_v = Vb.ap().rearrange("(c p) d -> p c d", p=128)
                    engs = ("pool", "dve", "act")
                    for i in range(64):
                        a_, b_ = stg[i % 2], stb[i % 2]
                        dma(a_[:].rearrange("p (k e) -> p k e", k=8), uT_v[:, :, i * 256:(i + 1) * 256])
                        cp(engs[i % 3], b_[:], a_[:])
                        dma(uTb_v[:, :, i * 256:(i + 1) * 256], b_[:].rearrange("p (k e) -> p k e", k=8))
                    for i in range(64):
                        a_, b_ = stg[i % 2], stb[i % 2]
                        dma(a_[:].rearrange("p (c d) -> p c d", c=2), v_v[:, 2 * i:2 * i + 2, :])
                        cp(engs[i % 3], b_[:], a_[:])
                        dma(vb_v[:, 2 * i:2 * i + 2, :], b_[:].rearrange("p (c d) -> p c d", c=2))
                    P.barrier()
                P.barrier()
                thr_all = sbt(st, "thr_all", [128, NT, 8]); nb_all = sbt(st, "nb_all", [128, NT, 8])
                with ExitStack() as sea:
                    wst = sbt(sea, "wste", [128, 8, 280]); wpq = sbt(sea, "wpq", [128, 8, 2048], BF16)
                    kT = sbt(sea, "kTe", [128, 16, 128], BF16); gff = sbt(sea, "gff", [128, D])
                    qTs = sbt(sea, "qTs", [128, 16, 128], BF16); S = sbt(sea, "S", [128, 16, 128]); Sw = sbt(sea, "Sw", [128, 256])
                    V16 = sbt(sea, "V16", [128, 16, 16]); cand = sbt(sea, "cand", [128, 8, 256]); SC = sbt(sea, "SC", [128, 8, 16])
                    SCm = sbt(sea, "SCm", [128, 8, 16]); Zs = sbt(sea, "Zs", [128, 8])
                    load_w(wst, wpq, I["w_pq"].ap()[l], 2048)
                    dma(S[:].rearrange("p c k -> p (c k)"), I["peer_kT"].ap()[l].rearrange("d c k -> d (c k)")); cp("dve", kT[:], S[:])
                    dma(gff[:], bc_row(I["g_ffn"], l * D, D))
                    for k in range(NT):
                        norm_rows(h[:], xres[:, k, :], gff[:], D, 0)
                        to_featmajor(hT, h)
                        dma(hT_d.ap()[k], hT[:].rearrange("p k t -> p (k t)"))
                        for c4 in range(4):
                            bank = ps[1 + c4 % 2]
                            for ci in range(4):
                                c = c4 * 4 + ci
                                for kk in range(8):
                                    mm(bank[:, ci * 128:(ci + 1) * 128], wpq[:, kk, c * 128:(c + 1) * 128], hT[:, kk, :], start=(kk == 0), stop=(kk == 7))
                            cp("act", qTs[:, c4 * 4:(c4 + 1) * 4, :], bank[:].rearrange("p (c t) -> p c t", c=4))
                        for c4 in range(4):
                            bank = ps[3 + c4 % 2]
                            for ci in range(4):
                                c = c4 * 4 + ci
                                mm(bank[:, ci * 128:(ci + 1) * 128], qTs[:, c, :], kT[:, c, :])
                            cp("dve", S[:, c4 * 4:(c4 + 1) * 4, :], bank[:].rearrange("p (c t) -> p c t", c=4))
                        dma(s_d.ap()[k], S[:].rearrange("p c k -> p (c k)"))
                        for c in range(16):
                            P.op("dve", lambda e, c=c, S=S, V16=V16: e.max(out=V16[:, c, 0:8], in_=S[:, c, :]), reads=[S[:]], writes=[V16[:]])
                            P.op("dve", lambda e, c=c, S=S, V16=V16, Sw=Sw: e.match_replace(out=Sw[:, 0:128], in_to_replace=V16[:, c, 0:8], in_values=S[:, c, :], imm_value=-1e30), reads=[S[:], V16[:]], writes=[Sw[:]])
                            P.op("dve", lambda e, c=c, V16=V16, Sw=Sw: e.max(out=V16[:, c, 8:16], in_=Sw[:, 0:128]), reads=[Sw[:]], writes=[V16[:]])
                        V4 = V16[:].rearrange("p (h s) k -> p h s k", s=2)
                        tt("pool", cand[:].rearrange("p h (a b) -> p h a b", a=16), V4[:, :, 0, :].unsqueeze(3).to_broadcast([128, 8, 16, 16]),
                           V4[:, :, 1, :].unsqueeze(2).to_broadcast([128, 8, 16, 16]), ALU.add)
                        for hh in range(8):
                            P.op("dve", lambda e, hh=hh, SC=SC, cand=cand: e.max(out=SC[:, hh, 0:8], in_=cand[:, hh, :]), reads=[cand[:]], writes=[SC[:]])
                            P.op("dve", lambda e, hh=hh, SC=SC, cand=cand, Sw=Sw: e.match_replace(out=Sw[:], in_to_replace=SC[:, hh, 0:8], in_values=cand[:, hh, :], imm_value=-1e30), reads=[cand[:], SC[:]], writes=[Sw[:]])
                            P.op("dve", lambda e, hh=hh, SC=SC, Sw=Sw: e.max(out=SC[:, hh, 8:16], in_=Sw[:]), reads=[Sw[:]], writes=[SC[:]])
                        cp("dve", thr_all[:, k, :], SC[:, :, 15])
                        tt("dve", SCm[:], SC[:], SC[:, :, 0:1].to_broadcast([128, 8, 16]), ALU.subtract)
                        actf(SCm[:], SCm[:], AF.Exp)
                        red(Zs[:], SCm[:])
                        actf(Zs[:], Zs[:], AF.Ln)
                        tt("dve", Zs[:], Zs[:], SC[:, :, 0], ALU.add)
                        ts("dve", nb_all[:, k, :], Zs[:], -1.0, ALU.mult)
                    P.barrier()
                P.barrier()
                with ExitStack() as seb:
                    S2 = [sbt(seb, "Sb%d" % i, [128, 16, 128]) for i in range(2)]; SUM = [sbt(seb, "SUM%d" % i, [128, 8, 128]) for i in range(2)]
                    Eb = [sbt(seb, "Eb%d" % i, [128, 1024], BF16) for i in range(2)]; Gh2 = [sbt(seb, "Gh%d" % i, [128, 8, 1024], BF16) for i in range(2)]
                    hT2 = sbt(seb, "hT2", [128, 8, 256], BF16)
                    UTk = [sbt(seb, "UTk%d" % i, [128, 8, 1024], BF16) for i in range(2)]; Vk = [sbt(seb, "Vk%d" % i, [128, 8, 1024], BF16) for i in range(1)]
                    gA = [sbt(seb, "gA%d" % i, [128, 512], BF16) for i in range(2)]; WT = [sbt(seb, "WT%d" % i, [128, 512], BF16) for i in range(2)]
                    uTb_v = UTb.ap().rearrange("(k p) e -> p k e", p=128); vb_v = Vb.ap().rearrange("(c p) d -> p c d", p=128)
                    it = 0
                    for k0 in range(0, NT, 2):
                        gs = min(2, NT - k0)
                        for j in range(gs):
                            dma(S2[j][:].rearrange("p c k -> p (c k)"), s_d.ap()[k0 + j])
                            dma(hT2[:, :, j * 128:(j + 1) * 128], hT_d.ap()[k0 + j].rearrange("p (k t) -> p k t", k=8))
                        for ab in range(16):
                            ub, vb2 = UTk[ab % 2], Vk[0]
                            dma(ub[:], uTb_v[:, :, ab * 1024:(ab + 1) * 1024]); dma(vb2[:], vb_v[:, ab * 8:(ab + 1) * 8, :])
                            for j in range(gs):
                                S = S2[j]
                                for hh in range(8):
                                    sm_, eb_ = SUM[it % 2], Eb[it % 2]; it += 1
                                    tt("pool", sm_[:], S[:, 2 * hh, ab * 8:(ab + 1) * 8].unsqueeze(2).to_broadcast([128, 8, 128]),
                                       S[:, 2 * hh + 1, :].unsqueeze(1).to_broadcast([128, 8, 128]), ALU.add)
                                    sflat = sm_[:].rearrange("p a b -> p (a b)")
                                    actf(eb_[:], sflat, AF.Exp, bias=nb_all[:, k0 + j, hh:hh + 1])
                                    stt(Gh2[j][:, hh, :], sflat, thr_all[:, k0 + j, hh:hh + 1], eb_[:], ALU.is_ge, ALU.mult, wk=[("Gh", j, hh)])
                            for c2 in range(4):
                                gtb, atb = ps[c2 % 2], ps[2 + c2 % 2]
                                for ci in range(2):
                                    cc_ = c2 * 2 + ci
                                    for j in range(gs):
                                        for hh in range(8):
                                            mm(gtb[:, ci * 256 + j * 128:ci * 256 + (j + 1) * 128], Gh2[j][:, hh, cc_ * 128:(cc_ + 1) * 128], IDb, start=(hh == 0), stop=(hh == 7), rk=[("Gh", j, hh), IDb])
                                    for kk in range(8):
                                        mm(atb[:, ci * 256:ci * 256 + gs * 128], ub[:, kk, cc_ * 128:(cc_ + 1) * 128], hT2[:, kk, 0:gs * 128], start=(kk == 0), stop=(kk == 7))
                                actf(gA[c2 % 2][:], atb[:], AF.Gelu_apprx_tanh)
                                tt("dve", WT[c2 % 2][:], gA[c2 % 2][:], gtb[:], ALU.mult)
                                for ci in range(2):
                                    cc_ = c2 * 2 + ci
                                    first = (ab == 0 and cc_ == 0); last = (ab == 15 and cc_ == 7)
                                    for j in range(gs):
                                        for half in range(2):
                                            mm(ps[4 + 2 * j + half][:], WT[c2 % 2][:, ci * 256 + j * 128:ci * 256 + (j + 1) * 128], vb2[:, cc_, half * 512:(half + 1) * 512], start=first, stop=last)
                        for j in range(gs):
                            for half in range(2):
                                tt("dve", xres[:, k0 + j, half * 512:(half + 1) * 512], xres[:, k0 + j, half * 512:(half + 1) * 512], ps[4 + 2 * j + half][:], ALU.add)
                    P.barrier()
                P.barrier()
        dma(O["y"].ap().rearrange("(k p) d -> p k d", p=128), xres[:])
        P.emit(st)
    return nc


def host_inputs(inputs, cfg):
    f = lambda k: np.ascontiguousarray(np.asarray(inputs[k]))
    NP, NT, NPG = cfg.NP, cfg.NT, cfg.NPG
    x_prompt, x_sample = f("x_prompt"), f("x_sample")
    past_len = NPG * 128
    inv = (1.0 / (10000.0 ** (np.arange(16, dtype=np.float32) / 16))).astype(np.float32)
    ii = np.arange(128)
    consts = np.zeros((8, 128, 128), np.float32)
    same = (ii[:, None] // 8) == (ii[None, :] // 8)
    consts[0] = np.eye(128)
    consts[1] = (ii[None, :] >= ii[:, None])
    consts[2] = (ii[:, None] > ii[None, :])
    consts[3] = 1.0
    consts[4] = consts[1] * same
    consts[5] = consts[2] * same
    consts[6] = same
    consts[7, :64, 0] = 1.0
    consts[7, 64:, 1] = 1.0
    blkind = (ii[:, None] // 8 == np.arange(16)[None, :]).astype(np.float32)
    qk = lambda g: np.concatenate([g, g[:, 64:]], axis=1)
    wsT = f("gm_ws").transpose(0, 1, 3, 2)
    wsT_s = np.tile(wsT[:, :, :8, :8], (1, 1, 16, 16))
    bs = f("gm_bs")
    bsT = bs.transpose(0, 2, 1)
    bsT_s = np.tile(bsT[:, :8, :], (1, 16, 1))
    shared = {
        "consts": consts, "blkind": blkind,
        "w_in": f("w_in"), "g_mix": f("g_mix"), "g_ckv": f("g_ckv"), "g_cq": f("g_cq"), "w_uq": f("w_uq"),
        "w_uk": f("w_uk").reshape(-1, 128, 256), "w_ukT": np.ascontiguousarray(f("w_uk").transpose(0, 2, 3, 1)),
        "w_uv": f("w_uv").reshape(-1, 128, 256), "gq96": qk(f("g_qk_q")), "gk96": qk(f("g_qk_k")),
        "conv_wT": np.ascontiguousarray(f("conv_w").transpose(0, 2, 1)), "conv_b": f("conv_b"), "dt_bias": f("dt_bias"),
        "a_log": f("a_log"), "d_skip": f("d_skip"), "g_ssd_out": f("g_ssd_out"),
        "gm_ln_g": f("gm_ln_g"), "gm_ln_b": f("gm_ln_b"),
        "gm_wsT": np.ascontiguousarray(np.stack([wsT, wsT_s], 1)), "gm_bsT": np.ascontiguousarray(np.stack([bsT, bsT_s], 1)),
        "g_ffn": f("g_ffn"), "w_pq": f("w_pq"),
        "peer_kT": np.ascontiguousarray(np.stack([f("peer_k1"), f("peer_k2")], 2).transpose(0, 4, 1, 2, 3).reshape(f("peer_k1").shape[0], 128, 16, 128)),
        "peer_uT": np.ascontiguousarray(f("peer_u").transpose(0, 2, 1)), "peer_v": f("peer_v"),
        "w_out": f("w_out"), "g_memm": f("g_memm"), "g_mk": f("g_mk"), "w_mk": f("w_mk"), "w_mv": f("w_mv"),
        "g_memx": f("g_memx"), "w_mq": f("w_mq"), "g_mq": f("g_mq"), "w_mo": f("w_mo"),
        "colmask": np.ascontiguousarray(np.broadcast_to((np.arange(128)[None, None, :] // 8 == np.arange(16)[None, :, None]), (128, 16, 128)).astype(np.float32)),
        "cache_ckv": f("cache_mla_ckv").reshape(f("cache_mla_ckv").shape[0], -1, 128),
        "cache_kr": f("cache_mla_krope").reshape(f("cache_mla_krope").shape[0], -1, 32),
    }
    maps = []
    for c in range(8):
        g, r = c // 4, c % 4
        xin = np.concatenate([x_prompt[g, r * NP * 128:(r + 1) * NP * 128], x_sample[c * 16:(c + 1) * 16].reshape(128, D)], 0)
        pos = np.concatenate([r * NP * 128 + np.arange(NP * 128), past_len + (np.arange(128) % 8)]).astype(np.float32)
        ang = pos[:, None] * inv[None, :]
        rope = np.concatenate([np.cos(ang), np.sin(ang)], 1).astype(np.float32)
        vis = np.zeros((128, 12), np.float32)
        for rr in range(4):
            vis[:, rr] = 0.0 if rr < r else NEG
            vis[:, 4 + rr] = 1.0 if rr < r else 0.0
            vis[:, 8 + rr] = 1.0 if rr == r - 1 else 0.0
        m = dict(shared)
        ml, half = r // 2, r % 2
        ml = min(ml, f("g_memm").shape[0] - 1)
        m.update({"mem_rows": np.ascontiguousarray(f("mem_prompt")[g, half * 128:(half + 1) * 128]),
                  "g_memm_l": f("g_memm")[ml:ml + 1], "g_mk_l": f("g_mk")[ml:ml + 1], "w_mk_l": f("w_mk")[ml], "w_mv_l": f("w_mv")[ml]})
        m.update({"mem_full": np.ascontiguousarray(f("mem_prompt")[g]),
                  "cache_mem_k": np.ascontiguousarray(f("cache_mem_k")[:, c * 16:(c + 1) * 16].reshape(f("cache_mem_k").shape[0], 16, 256, 1024)),
                  "cache_mem_v": np.ascontiguousarray(f("cache_mem_v")[:, c * 16:(c + 1) * 16].reshape(f("cache_mem_v").shape[0], 16, 256, 1024))})
        m.update({"xin": np.ascontiguousarray(xin), "rope": rope, "vis": vis,
                  "page_idx": np.ascontiguousarray(f("page_table")[c * 16:(c + 1) * 16].reshape(1, -1).astype(np.int32)),
                  "state_conv": np.ascontiguousarray(f("state_conv")[:, c * 16:(c + 1) * 16]),
                  "state_ssm": np.ascontiguousarray(f("state_ssm")[:, c * 16:(c + 1) * 16])})
        maps.append(m)
    return maps


def kernel(**inputs):
    cfg = Cfg(depth=2, np_=16, npg=64, nphys=int(np.asarray(inputs["cache_mla_ckv"]).shape[1]), phases="ACDE")
    NP_, NPT_ = cfg.NP, cfg.NP * 128
    maps = host_inputs(inputs, cfg)
    nc = build(cfg)
    res = run_bass_kernel_spmd(nc, maps, core_ids=list(range(8))).results
    B, SEQ, NS, LS, DEPTH_ = 2, 4 * NPT_, 128, 8, cfg.DEPTH
    z = lambda *s: np.zeros(s, np.float32)
    y_p, y_s = z(B, SEQ, D), z(NS, LS, D)
    ckv_p, kr_p = z(DEPTH_, B, SEQ, 128), z(DEPTH_, B, SEQ, 32)
    ssm_p, conv_p = z(DEPTH_, B, 8, 64, 64), z(DEPTH_, B, 3, 768)
    mk, mv = z(DEPTH_, B, 256, 4, 256), z(DEPTH_, B, 256, 4, 256)
    gmv_p = z(DEPTH_, B, 128, 256)
    ckv_s, kr_s = z(DEPTH_, NS, LS, 128), z(DEPTH_, NS, LS, 32)
    ssm_s, conv_s, gmv_s = z(DEPTH_, NS, 8, 64, 64), z(DEPTH_, NS, 3, 768), z(DEPTH_, NS, LS, 256)
    for c in range(8):
        g, r = c // 4, c % 4
        o = res[c]
        sl = slice(r * NPT_, (r + 1) * NPT_)
        bs = slice(c * 16, (c + 1) * 16)
        y_p[g, sl] = o["y"][:NPT_]
        y_s[bs] = o["y"][NPT_:].reshape(16, 8, D)
        for l in range(DEPTH_):
            ckv_p[l, g, sl] = o["new_ckv"][l, :NPT_]; kr_p[l, g, sl] = o["new_kr"][l, :NPT_]
            ckv_s[l, bs] = o["new_ckv"][l, NPT_:].reshape(16, 8, 128); kr_s[l, bs] = o["new_kr"][l, NPT_:].reshape(16, 8, 32)
            ssm_s[l, bs] = o["new_ssm"][l, :16]; conv_s[l, bs] = o["new_conv"][l, :16]
            gmv_s[l, bs] = o["new_gmv"][l, 128:].reshape(16, 8, 256)
            if r == 3:
                ssm_p[l, g] = o["new_ssm"][l, 16]; conv_p[l, g] = o["new_conv"][l, 16]; gmv_p[l, g] = o["new_gmv"][l, :128]
        ml, half = r // 2, r % 2
        mk[ml, g, half * 128:(half + 1) * 128] = o["mem_k"].reshape(128, 4, 256)
        mv[ml, g, half * 128:(half + 1) * 128] = o["mem_v"].reshape(128, 4, 256)
    return (y_p, y_s, ckv_p, kr_p, ssm_p, conv_p, mk, mv, gmv_p, ckv_s, kr_s, ssm_s, conv_s, gmv_s)
```

```python
import math
from contextlib import ExitStack
import numpy as np
import concourse.bass as bass
import concourse.mybir as mybir
from concourse.bass_utils import run_bass_kernel_spmd

F32 = mybir.dt.float32
BF16 = mybir.dt.bfloat16
I32 = mybir.dt.int32
AF = mybir.ActivationFunctionType
ALU = mybir.AluOpType
AX = mybir.AxisListType
NDMA_SEM = 48
EPS = 1e-6
NEG = -30000.0


class Prog:
    ENGS = ("pe", "act", "dve", "pool", "sp")

    def __init__(self, nc):
        self.nc = nc
        self.ops = {e: [] for e in self.ENGS}
        self.cnt = {e: 0 for e in self.ENGS}
        self.lastw = {}
        self.readers = {}
        self.waited = {e: {} for e in self.ENGS}
        self.ndma = {"sp": 0, "pool": 0}
        self.dma_tokens = {"sp": [], "pool": []}
        self.sems = {}
        self.semval = {}

    @staticmethod
    def _key(r):
        if isinstance(r, (str, tuple)):
            return r
        return r.tensor.name

    def _deps(self, reads, writes):
        deps = []
        for r in reads:
            t = self.lastw.get(r)
            if t is not None:
                deps.append(t)
        for w in writes:
            t = self.lastw.get(w)
            if t is not None:
                deps.append(t)
            deps.extend(self.readers.get(w, ()))
        return deps

    def _waits(self, eng, deps, skip_pe=False):
        waits = []
        wd = self.waited[eng]
        for (semkey, val, deng) in deps:
            if skip_pe and deng == "pe" and eng == "pe":
                continue
            if wd.get(semkey, 0) >= val:
                continue
            wd[semkey] = val
            waits.append((semkey, val))
        return waits

    def _commit(self, tok, reads, writes):
        for r in reads:
            lst = self.readers.setdefault(r, [])
            lst.append(tok)
            if len(lst) > 64:
                best = {}
                for t in lst:
                    if t[0] not in best or best[t[0]][1] < t[1]:
                        best[t[0]] = t
                self.readers[r] = list(best.values())
        for w in writes:
            self.lastw[w] = tok
            self.readers[w] = []

    def op(self, eng, fn, reads=(), writes=()):
        reads = [self._key(r) for r in reads]
        writes = [self._key(w) for w in writes]
        deps = self._deps(reads, writes)
        waits = self._waits(eng, deps, skip_pe=True)
        self.cnt[eng] += 1
        tok = (eng, self.cnt[eng], eng)
        self.ops[eng].append((fn, waits, (eng, 1)))
        self._commit(tok, reads, writes)
        return tok

    def dma(self, fn, reads=(), writes=(), q="sp", inc=16):
        reads = [self._key(r) for r in reads]
        writes = [self._key(w) for w in writes]
        deps = self._deps(reads, writes)
        k = self.ndma[q]
        self.ndma[q] += 1
        semkey = ("dma", q, k % NDMA_SEM)
        prev = self.semval.get(semkey, 0)
        val = prev + inc
        self.semval[semkey] = val
        if prev > 0:
            deps.append((semkey, prev, "dma"))
        waits = self._waits(q, deps)
        tok = (semkey, val, "dma")
        self.ops[q].append((fn, waits, (semkey, inc)))
        self._commit(tok, reads, writes)
        self.dma_tokens[q].append(tok)
        return tok

    def barrier(self):
        toks = [(e, self.cnt[e], e) for e in self.ENGS if self.cnt[e] > 0]
        for q in ("sp", "pool"):
            toks.extend(self.dma_tokens[q][-NDMA_SEM:])
        for e in self.ENGS:
            waits = self._waits(e, toks)
            if waits:
                self.ops[e].append((None, waits, None))
        self.lastw = {}
        self.readers = {}

    def emit(self, stack):
        nc = self.nc
        self.barrier()
        semnames = set(self.ENGS)
        for q in ("sp", "pool"):
            for i in range(min(self.ndma[q], NDMA_SEM)):
                semnames.add(("dma", q, i))
        for sk in sorted(semnames, key=str):
            nm = sk if isinstance(sk, str) else "d_%s_%d" % (sk[1], sk[2])
            self.sems[sk] = stack.enter_context(nc.semaphore("s_" + nm))
        block = stack.enter_context(nc.Block())
        sems = self.sems

        def run(e, lst):
            for (fn, waits, inc) in lst:
                for (sk, val) in waits:
                    e.wait_ge(sems[sk], val)
                if fn is not None:
                    fn(e).then_inc(sems[inc[0]], inc[1])

        @block.tensor
        def _(e):
            run(e, self.ops["pe"])

        @block.scalar
        def _(e):
            run(e, self.ops["act"])

        @block.vector
        def _(e):
            run(e, self.ops["dve"])

        @block.gpsimd
        def _(e):
            run(e, self.ops["pool"])

        @block.sync
        def _(e):
            run(e, self.ops["sp"])


D = 1024
C_Z, C_XBC, C_DT, C_CQ, C_CKV, C_KR, C_U, C_V = 0, 512, 1280, 1288, 1544, 1672, 1704, 1960


class Cfg:
    def __init__(self, depth=2, np_=16, npg=64, nphys=10240, phases="ACDE"):
        self.DEPTH, self.NP, self.NPG, self.NPHYS, self.phases = depth, np_, npg, nphys, phases
        self.NT = np_ + 1
        self.GRP = 4
        self.NC = 8


DBG = set()

def build(cfg):
    NP, NT, DEPTH, GRP, NPG, NPHYS = cfg.NP, cfg.NT, cfg.DEPTH, cfg.GRP, cfg.NPG, cfg.NPHYS
    NPT = NP * 128
    nc = bass.Bass("TRN2", target_bir_lowering=False)
    I = {}
    O = {}

    def inp(name, shape, dt=F32):
        I[name] = nc.dram_tensor(name, list(shape), dt, kind="ExternalInput")
        return I[name]

    def outp(name, shape, dt=F32):
        O[name] = nc.dram_tensor(name, list(shape), dt, kind="ExternalOutput")
        return O[name]

    def scr(name, shape, dt=F32):
        return nc.dram_tensor(name, list(shape), dt, kind="Internal")

    inp("xin", [NT * 128, D]); inp("consts", [8, 128, 128]); inp("blkind", [128, 16]); inp("rope", [NT * 128, 32]); inp("vis", [128, 12])
    inp("w_in", [DEPTH, D, 2216]); inp("g_mix", [DEPTH, D]); inp("g_ckv", [DEPTH, 128]); inp("g_cq", [DEPTH, 256])
    inp("w_uq", [DEPTH, 256, 384]); inp("w_uk", [DEPTH, 128, 256]); inp("w_ukT", [DEPTH, 4, 64, 128]); inp("w_uv", [DEPTH, 128, 256])
    inp("gq96", [DEPTH, 96]); inp("gk96", [DEPTH, 96])
    inp("conv_wT", [DEPTH, 768, 4]); inp("conv_b", [DEPTH, 768]); inp("dt_bias", [DEPTH, 8]); inp("a_log", [DEPTH, 8]); inp("d_skip", [DEPTH, 8])
    inp("g_ssd_out", [DEPTH, 512]); inp("state_conv", [DEPTH, 16, 3, 768]); inp("state_ssm", [DEPTH, 16, 8, 64, 64])
    inp("gm_ln_g", [DEPTH, 256]); inp("gm_ln_b", [DEPTH, 256]); inp("gm_wsT", [DEPTH, 2, 4, 128, 128]); inp("gm_bsT", [DEPTH, 2, 128, 4])
    inp("mem_rows", [128, D]); inp("g_memm_l", [1, D]); inp("g_mk_l", [1, 256]); inp("w_mk_l", [D, D]); inp("w_mv_l", [D, D])
    outp("mem_k", [128, D]); outp("mem_v", [128, D])
    inp("mem_full", [256, D]); inp("g_memm", [DEPTH, D]); inp("g_mk", [DEPTH, 256]); inp("w_mk", [DEPTH, D, D]); inp("w_mv", [DEPTH, D, D])
    inp("g_memx", [DEPTH, D]); inp("w_mq", [DEPTH, D, D]); inp("g_mq", [DEPTH, 256]); inp("w_mo", [DEPTH, D, D])
    inp("cache_mem_k", [DEPTH, 16, 256, D]); inp("cache_mem_v", [DEPTH, 16, 256, D]); inp("colmask", [128, 16, 128])
    inp("g_ffn", [DEPTH, D]); inp("w_pq", [DEPTH, D, 2048]); inp("peer_kT", [DEPTH, 128, 16, 128])
    inp("peer_uT", [DEPTH, D, 16384]); inp("peer_v", [DEPTH, 16384, D])
    inp("w_out", [DEPTH, D, D]); inp("page_idx", [1, 16 * NPG], I32)
    inp("cache_ckv", [DEPTH, NPHYS * 128, 128]); inp("cache_kr", [DEPTH, NPHYS * 128, 32])
    outp("y", [NT * 128, D]); outp("new_ckv", [DEPTH, NT * 128, 128]); outp("new_kr", [DEPTH, NT * 128, 32])
    outp("new_conv", [DEPTH, 17, 3, 768]); outp("new_gmv", [DEPTH, 2 * 128, 256]); outp("new_ssm", [DEPTH, 17, 8, 64, 64])

    UTb = scr("UTb", [D, 16384], BF16); Vb = scr("Vb", [16384, D], BF16)
    s_d = scr("s_d", [NT, 128, 2048]); hT_d = scr("hT_d", [NT, 128, 1024], BF16)
    ccH = scr("ccH", [3, 768]); ccHg = scr("ccHg", [GRP * 3, 768])
    ccS = scr("ccS", [128, 264]); ccSg = scr("ccSg", [GRP * 128, 264])
    ctm_d = scr("ctm_d", [NT, 128, 2, 128], BF16); hs_d = scr("hs_d", [16, 128, 256]); ys_d = scr("ys_d", [NT, 128, 512]); zs_d = scr("zs_d", [NT, 128, 512]); ygm_d = scr("ygm_d", [NT, 128, 256], BF16)
    qlt_d = scr("qlt_d", [NT, 128, 512], BF16); qrt_d = scr("qrt_d", [NT, 32, 512], BF16)
    ccKT = scr("ccKT", [160, NPT], BF16); ccKTg = scr("ccKTg", [GRP * 160, NPT], BF16)
    ccKV = scr("ccKV", [NPT, 128], BF16); ccKVg = scr("ccKVg", [GRP * NPT, 128], BF16)
    ccRK = scr("ccRK", [NPT, 4]); ccRKg = scr("ccRKg", [GRP * NPT, 4])

    P = Prog(nc)
    with ExitStack() as st:
        uniq = [0]

        def sbt(stack, name, shape, dt=F32):
            uniq[0] += 1
            return stack.enter_context(nc.sbuf_tensor("%s_%d" % (name, uniq[0]), list(shape), dt))

        def dma(out, in_, q="sp", **kw):
            P.dma(lambda e: e.dma_start(out=out, in_=in_, **kw), reads=[in_], writes=[out], q=q)

        def mm(out, lhsT, rhs, start=True, stop=True, rk=None):
            P.op("pe", lambda e: e.matmul(out, lhsT=lhsT, rhs=rhs, start=start, stop=stop), reads=(rk if rk is not None else [lhsT, rhs]), writes=[out])

        def tr(out, in_, ident):
            P.op("pe", lambda e: e.transpose(out=out, in_=in_, identity=ident), reads=[in_, ident], writes=[out])

        def actf(out, in_, func, bias=None, scale=None, accum=None):
            kw = {}
            if bias is not None:
                kw["bias"] = bias
            if scale is not None:
                kw["scale"] = scale
            if accum is not None:
                kw["accum_out"] = accum
            r = [in_] + [a for a in (bias, scale) if a is not None and not isinstance(a, (int, float))]
            w = [out] + ([accum] if accum is not None else [])
            P.op("act", lambda e: e.activation(out=out, in_=in_, func=func, **kw), reads=r, writes=w)

        def cp(eng, out, in_):
            if eng == "act":
                P.op("act", lambda e: e.copy(out=out, in_=in_), reads=[in_], writes=[out])
            else:
                P.op(eng, lambda e: e.tensor_copy(out=out, in_=in_), reads=[in_], writes=[out])

        def tt(eng, out, a, b, op):
            P.op(eng, lambda e: e.tensor_tensor(out=out, in0=a, in1=b, op=op), reads=[a, b], writes=[out])

        def ts(eng, out, a, s1, op0, s2=None, op1=None):
            r = [a] + [s for s in (s1, s2) if s is not None and not isinstance(s, (int, float))]
            kw = {"op1": op1} if op1 is not None else {}
            P.op(eng, lambda e: e.tensor_scalar(out=out, in0=a, scalar1=s1, scalar2=s2, op0=op0, **kw), reads=r, writes=[out])

        def stt(out, a, s, b, op0, op1, wk=None):
            r = [a, b] + ([s] if not isinstance(s, (int, float)) else [])
            P.op("dve", lambda e: e.scalar_tensor_tensor(out=out, in0=a, scalar=s, in1=b, op0=op0, op1=op1), reads=r, writes=(wk if wk is not None else [out]))

        def red(out, in_, op=ALU.add):
            P.op("dve", lambda e: e.tensor_reduce(out=out, in_=in_, axis=AX.X, op=op), reads=[in_], writes=[out])

        def recip(out, in_):
            P.op("dve", lambda e: e.reciprocal(out=out, in_=in_), reads=[in_], writes=[out])

        def memset(eng, ap, val):
            P.op(eng, lambda e: e.memset(ap, val), writes=[ap])

        def bc_row(dt_, off, n):
            return bass.AP(dt_, off, [[0, 128], [1, n]])

        xres = sbt(st, "xres", [128, NT, D])
        cst = sbt(st, "cst", [128, 8, 128]); cstb = sbt(st, "cstb", [128, 8, 128], BF16)
        blkind = sbt(st, "blkind_sb", [128, 16])
        rope = sbt(st, "rope_sb", [128, NT, 32]); vis = sbt(st, "vis_sb", [128, 12])
        epsT = sbt(st, "epsT", [128, 1]); oneT = sbt(st, "oneT", [128, 1])
        junk = sbt(st, "junk", [128, D], BF16)
        sA = sbt(st, "sA", [128, 16]); sB = sbt(st, "sB", [128, 16])
        h = sbt(st, "h", [128, D], BF16); hT = sbt(st, "hT", [128, 8, 128], BF16)
        ps = [st.enter_context(nc.psum_tensor("ps%d" % i, [128, 512], F32)) for i in range(8)]
        IDf, IDb = cst[:, 0, :], cstb[:, 0, :]
        dma(xres[:], I["xin"].ap().rearrange("(k p) d -> p k d", p=128))
        dma(cst[:], I["consts"].ap().rearrange("k p n -> p k n"))
        dma(blkind[:], I["blkind"].ap())
        dma(rope[:], I["rope"].ap().rearrange("(k p) d -> p k d", p=128))
        dma(vis[:], I["vis"].ap())
        cp("dve", cstb[:], cst[:])
        memset("dve", epsT[:], EPS); memset("dve", oneT[:], 1.0)

        def rstd_of(out, ss, n):
            actf(out, ss, AF.Sqrt, bias=epsT[:, 0:1], scale=1.0 / n)
            recip(out, out)

        def norm_rows(dst, src, gain, n, col):
            actf(junk[:, 0:n], src, AF.Square, accum=sA[:, col:col + 1])
            rstd_of(sB[:, col:col + 1], sA[:, col:col + 1], n)
            stt(dst, src, sB[:, col:col + 1], gain, ALU.mult, ALU.mult)

        def to_featmajor(dstT, src_bf16, nchunk=8):
            pT = ps[0][:].bitcast(BF16).rearrange("p (k t) -> p k t", k=8)
            for c in range(nchunk):
                tr(pT[:, c, :], src_bf16[:, c * 128:(c + 1) * 128], IDb)
            cp("act", dstT[:, 0:nchunk, :], pT[:, 0:nchunk, :])

        def load_w(wst, wb, dram_ap2d, ncols, kch=8, step=280):
            src = dram_ap2d.rearrange("(k p) n -> p k n", p=128)
            for i, c0 in enumerate(range(0, ncols, step)):
                c1 = min(ncols, c0 + step)
                dma(wst[:, 0:kch, 0:c1 - c0], src[:, :, c0:c1])
                cp("pool" if i % 2 == 0 else "dve", wb[:, 0:kch, c0:c1], wst[:, 0:kch, 0:c1 - c0])

        def proj(dst_f32, srcT, wb, ncols, kch=8):
            for nb in range((ncols + 511) // 512):
                n0, n1 = nb * 512, min(ncols, nb * 512 + 512)
                pp = ps[1 + (nb % 2)]
                for c in range(kch):
                    mm(pp[:, 0:n1 - n0], srcT[:, c, :], wb[:, c, n0:n1], start=(c == 0), stop=(c == kch - 1))
                cp("dve" if nb % 2 == 0 else "act", dst_f32[:, n0:n1], pp[:, 0:n1 - n0])

        with ExitStack() as s0:
            wst0 = sbt(s0, "wst0", [128, 8, 280]); wb0 = sbt(s0, "wb0", [128, 8, 1024], BF16)
            gB0 = sbt(s0, "gB0", [128, D]); g20 = sbt(s0, "g20", [128, 256])
            memx = sbt(s0, "memx", [128, D]); kv0 = sbt(s0, "kv0", [128, D]); kvo = sbt(s0, "kvo", [128, D])
            dma(memx[:], I["mem_rows"].ap()); dma(gB0[:], bc_row(I["g_memm_l"], 0, D)); dma(g20[:], bc_row(I["g_mk_l"], 0, 256))
            norm_rows(h[:], memx[:], gB0[:], D, 0)
            to_featmajor(hT, h)
            load_w(wst0, wb0, I["w_mk_l"].ap(), D)
            proj(kv0, hT, wb0, D)
            for hd in range(4):
                norm_rows(kvo[:, hd * 256:(hd + 1) * 256], kv0[:, hd * 256:(hd + 1) * 256], g20[:], 256, 1 + hd)
            dma(O["mem_k"].ap(), kvo[:])
            load_w(wst0, wb0, I["w_mv_l"].ap(), D)
            proj(kv0, hT, wb0, D)
            dma(O["mem_v"].ap(), kv0[:])
            P.barrier()
        P.barrier()
        for l in range(DEPTH):
            with ExitStack() as sm:
                wst = sbt(sm, "wst", [128, 8, 280])
                wb = sbt(sm, "wb", [128, 8, 2216], BF16)
                ecor = sbt(sm, "ecor", [128, NT, 8])
                CT1 = sbt(sm, "CT1", [128, 2, 128], BF16)
                Hinb = sbt(sm, "Hinb", [128, 256], BF16)
                wuv = sbt(sm, "wuv", [128, 256], BF16)
                wuk2 = sbt(sm, "wuk2", [128, 256], BF16)
                KT1 = sbt(sm, "KT1", [128, 2, 128], BF16)
                KRT1 = sbt(sm, "KRT1", [32, 2, 128], BF16)
                KV1 = sbt(sm, "KV1", [128, 2, 129], BF16)
                RK1 = sbt(sm, "RK1", [128, 2, 4])
                sa = ExitStack(); sa.__enter__()
                gB = sbt(sa, "gB", [128, D])
                g2 = sbt(sa, "g2", [128, 1024])
                tok = sbt(sa, "tok", [128, 2216])
                ckv = sbt(sa, "ckv", [128, 128])
                kr = sbt(sa, "kr", [128, 32])
                r4 = sbt(sa, "r4", [128, 4, 64])
                xT = sbt(sa, "xT", [128, 6, 16, 11])
                xTp = sbt(sa, "xTp", [128, 6, 131])
                cwT = sbt(sa, "cwT", [128, 6, 4])
                cbT = sbt(sa, "cbT", [128, 6])
                acc = sbt(sa, "acc", [128, 128])
                xcT = sbt(sa, "xcT", [128, 6, 128])
                haloA = sbt(sa, "haloA", [128, GRP, 6, 3])
                hrow = None
                u_t = sbt(sa, "u_t", [128, 256])
                v_t = sbt(sa, "v_t", [128, 256])
                vn = sbt(sa, "vn", [128, 256])
                vnb = sbt(sa, "vnb", [128, 256], BF16)
                wm = tok[:, 0:1024].rearrange("p (s g i) -> p s g i", s=2, g=4)
                wmb = sbt(sa, "wmb", [128, 2, 4, 128], BF16)
                bsT = sbt(sa, "bsT", [128, 2, 4])
                dtb = sbt(sa, "dtb", [128, 8])
                aB = sbt(sa, "aB", [128, 8])
                dskB = sbt(sa, "dskB", [128, 8])
                dt = sbt(sa, "dt", [128, 8])
                dtA = sbt(sa, "dtA", [128, 8])
                eac = sbt(sa, "eac", [128, 8])
                te = sbt(sa, "te", [128, 8])
                cd = sbt(sa, "cd", [128, 8])
                cumD = sbt(sa, "cumD", [128, 8])
                Rm = sbt(sa, "Rm", [128, 8, 128])
                sc = sbt(sa, "sc", [128, 2, 128])
                dec = sbt(sa, "dec", [128, 8, 128], BF16)
                MT = sbt(sa, "MT", [128, 8, 128], BF16)
                xst = sbt(sa, "xst", [128, 512])
                Btok = sbt(sa, "Btok", [128, 128], BF16)
                BTm = sbt(sa, "BTm", [128, 2, 128], BF16)
                xdt = sbt(sa, "xdt", [128, 512], BF16)
                xte = sbt(sa, "xte", [128, 512], BF16)
                HT2 = sbt(sa, "HT2", [128, 256])
                HT2b = sbt(sa, "HT2b", [128, 256], BF16)
                ys = sbt(sa, "ys", [128, 512])
                ytmp = sbt(sa, "ytmp", [128, 512])
                SG = sbt(sa, "SG", [128, 264])
                Hin = sbt(sa, "Hin", [128, 256])
                dmr = sbt(sa, "dmr", [128, 8])
                Sm = sbt(sa, "Sm", [128, 256])
                Hf = sbt(sa, "Hf", [128, 256])
                Ho = None
                Hnat = None
                HS1 = sbt(sa, "HS1", [128, 256])
                HS1b = sbt(sa, "HS1b", [128, 256], BF16)
                CZ = [sbt(sa, "CZ%d" % i, [128, 2, 128], BF16) for i in range(2)]
                BZ = [sbt(sa, "BZ%d" % i, [128, 128], BF16) for i in range(2)]
                Wsel = sbt(sa, "Wsel", [128, 16, 8])
                cdS = sbt(sa, "cdS", [128, 16, 8])
                cqb = sbt(sa, "cqb", [128, 256], BF16)
                cqT = sbt(sa, "cqT", [128, 2, 128], BF16)
                wuq = sbt(sa, "wuq", [128, 2, 384], BF16)
                wuk = sbt(sa, "wuk", [128, 256], BF16)
                wukT = sbt(sa, "wukT", [64, 4, 128], BF16)
                gqk = sbt(sa, "gqk", [128, 96])
                q = sbt(sa, "q", [128, 4, 96])
                q2 = sbt(sa, "q2", [128, 4, 96])
                qgb = sbt(sa, "qgb", [128, 4, 96], BF16)
                qnT = sbt(sa, "qnT", [64, 4, 128], BF16)
                qrT = sbt(sa, "qrT", [32, 4, 128], BF16)
                qlT = sbt(sa, "qlT", [128, 512], BF16)
                ckvb = sbt(sa, "ckvb", [128, 128], BF16)
                krb = sbt(sa, "krb", [128, 32], BF16)
                ygb = sbt(sa, "ygb", [128, 256], BF16)
                dma(tok[:, 0:768].rearrange("p (k n) -> p k n", k=2), I["w_uq"].ap()[l].rearrange("(k p) n -> p k n", p=128))
                cp("dve", wuq[:], tok[:, 0:768].rearrange("p (k n) -> p k n", k=2))
                dma(tok[:, 768:1024], I["w_uk"].ap()[l]); cp("dve", wuk[:], tok[:, 768:1024]); cp("dve", wuk2[:], tok[:, 768:1024])
                dma(tok[:, 1024:1280], I["w_uv"].ap()[l]); cp("dve", wuv[:], tok[:, 1024:1280])
                dma(tok[0:64, 1280:1792].rearrange("p (h c) -> p h c", h=4), I["w_ukT"].ap()[l].rearrange("h d c -> d h c"))
                cp("dve", wukT[:], tok[0:64, 1280:1792].rearrange("p (h c) -> p h c", h=4))
                dma(g2[:, 640:896], bc_row(I["g_cq"], l * 256, 256))
                dma(gqk[:], bc_row(I["gq96"], l * 96, 96)); dma(g2[:, 896:992], bc_row(I["gk96"], l * 96, 96))
                tt("dve", gqk[:], gqk[:], g2[:, 896:992], ALU.mult)
                memset("dve", KV1[:, :, 128:129], 1.0)
                Ho = xst[0:64, :]; Hnat = ytmp[0:64, :]
                hrow = xcT[0:48, :, :].rearrange("p m t -> p (m t)")
                dma(dtb[:], bc_row(I["dt_bias"], l * 8, 8)); dma(aB[:], bc_row(I["a_log"], l * 8, 8)); dma(dskB[:], bc_row(I["d_skip"], l * 8, 8))
                actf(aB[:], aB[:], AF.Exp)
                ts("dve", aB[:], aB[:], -1.0, ALU.mult)
                memset("dve", HT2[:], 0.0); memset("dve", HT2b[:], 0.0); memset("dve", cumD[:], 1.0); pass
                def load_hs(b):
                    dma(Hnat.rearrange("p (h n) -> p h n", h=8), I["state_ssm"].ap()[l, b].rearrange("h p n -> p h n"))
                    for hh in range(8):
                        if hh < 4:
                            tr(ps[6][0:64, hh * 64:(hh + 1) * 64], Hnat[:, hh * 64:(hh + 1) * 64], IDf[0:64, 0:64])
                        else:
                            tr(ps[6][:, 256 + (hh - 4) * 64:256 + (hh - 3) * 64], Hnat[:, hh * 64 - 64:hh * 64 + 64], IDf[0:64, 0:64])
                    cp("act", HS1[0:64, :], ps[6][0:64, 0:256])
                    cp("dve", HS1[64:128, :], ps[6][64:128, 256:512])
                load_w(wst, wb, I["w_in"].ap()[l], 2216)
                dma(gB[:], bc_row(I["g_mix"], l * D, D))
                dma(g2[:, 0:128], bc_row(I["g_ckv"], l * 128, 128))
                dma(g2[:, 128:384], bc_row(I["gm_ln_g"], l * 256, 256))
                dma(g2[:, 384:640], bc_row(I["gm_ln_b"], l * 256, 256))
                dma(cwT[:], I["conv_wT"].ap()[l].rearrange("(m c) w -> c m w", c=128))
                dma(cbT[:], I["conv_b"].ap()[l].rearrange("(m c) -> c m", c=128), allow_slow_non_contiguous=True)
                for s_ in range(2):
                    dma(wm[:, s_], I["gm_wsT"].ap()[l, s_].rearrange("g j i -> j g i"))
                dma(bsT[:], I["gm_bsT"].ap()[l].rearrange("s i g -> i s g"))
                tt("dve", wm[:, 0], wm[:, 0], cst[:, 1:2, :].to_broadcast([128, 4, 128]), ALU.mult)
                tt("dve", wm[:, 1], wm[:, 1], cst[:, 4:5, :].to_broadcast([128, 4, 128]), ALU.mult)
                cp("dve", wmb[:], wm)

                def ssm_out(l_, slot):
                    for hl in range(4):
                        tr(ps[3][0:64, hl * 128:(hl + 1) * 128], Hf[:, hl * 64:(hl + 1) * 64], IDf)
                    cp("act", Ho.rearrange("p (g hl n) -> p hl g n", g=2, hl=4), ps[3][0:64, :].rearrange("p (hl g n) -> p hl g n", hl=4, g=2))
                    dma(O["new_ssm"].ap()[l_, slot].rearrange("h p n -> p h n"), Ho.rearrange("p (h n) -> p h n", h=8))

                def front(k):
                    norm_rows(h[:], xres[:, k, :], gB[:], D, 0)
                    to_featmajor(hT, h)
                    proj(tok, hT, wb, 2216)

                front(NP - 1)
                dma(ccH.ap(), tok[125:128, C_XBC:C_XBC + 768])
                dma(O["new_conv"].ap()[l, 16], tok[125:128, C_XBC:C_XBC + 768])
                P.dma(lambda e: e.collective_compute("AllGather", ALU.bypass, replica_groups=[[0, 1, 2, 3], [4, 5, 6, 7]],
                                                     ins=[ccH.ap()], outs=[ccHg.ap()]), reads=["ccH"], writes=["ccHg"], q="pool", inc=1)
                dma(hrow[0:GRP * 3, :], ccHg.ap())
                for m in range(6):
                    tr(ps[6][:, m * 48:m * 48 + GRP * 3], hrow[0:GRP * 3, m * 128:(m + 1) * 128], IDf[0:GRP * 3, 0:GRP * 3])
                cp("act", haloA[:].rearrange("p r m j -> p m r j"), ps[6][:, 0:288].rearrange("p (m x) -> p m x", m=6)[:, :, 0:GRP * 3].rearrange("p m (r j) -> p m r j", j=3))
                ts("dve", xTp[:, :, 0:3], haloA[:, 0], vis[:, 8:9], ALU.mult)
                for r in range(1, GRP):
                    stt(xTp[:, :, 0:3], haloA[:, r], vis[:, 8 + r:9 + r], xTp[:, :, 0:3], ALU.mult, ALU.add)
                dma(hrow[0:48, :], I["state_conv"].ap()[l].rearrange("b j c -> (b j) c"))
                for m in range(6):
                    tr(ps[7][:, m * 48:m * 48 + 48], hrow[0:48, m * 128:(m + 1) * 128], IDf[0:48, 0:48])
                cp("act", xT[:, :, :, 0:3], ps[7][:, 0:288].rearrange("p (m b j) -> p m b j", m=6, j=3))

                for k in range(NT):
                    smp = (k == NP)
                    if k != NP - 1 or NP == 1:
                        front(k)
                    elif NP > 1:
                        front(k)
                    norm_rows(ckv[:], tok[:, C_CKV:C_CKV + 128], g2[:, 0:128], 128, 1)
                    x1, x2 = tok[:, C_KR:C_KR + 16], tok[:, C_KR + 16:C_KR + 32]
                    cs, sn = rope[:, k, 0:16], rope[:, k, 16:32]
                    tt("dve", r4[:, 0, 0:16], x1, cs, ALU.mult); tt("dve", r4[:, 1, 0:16], x2, sn, ALU.mult)
                    tt("dve", kr[:, 0:16], r4[:, 0, 0:16], r4[:, 1, 0:16], ALU.subtract)
                    tt("dve", r4[:, 2, 0:16], x2, cs, ALU.mult); tt("dve", r4[:, 3, 0:16], x1, sn, ALU.mult)
                    tt("dve", kr[:, 16:32], r4[:, 2, 0:16], r4[:, 3, 0:16], ALU.add)
                    dma(O["new_ckv"].ap()[l, k * 128:(k + 1) * 128, :], ckv[:])
                    dma(O["new_kr"].ap()[l, k * 128:(k + 1) * 128, :], kr[:])
                    actf(u_t[:], tok[:, C_U:C_U + 256], AF.Gelu_apprx_tanh)
                    actf(v_t[:], tok[:, C_V:C_V + 256], AF.Gelu_apprx_tanh)
                    v3 = v_t[:].rearrange("p (g d) -> p g d", g=4)
                    red(sA[:, 4:8], v3)
                    ts("dve", sA[:, 4:8], sA[:, 4:8], -1.0 / 64, ALU.mult)
                    tt("dve", vn[:].rearrange("p (g d) -> p g d", g=4), v3, sA[:, 4:8].unsqueeze(2).to_broadcast([128, 4, 64]), ALU.add)
                    tt("dve", r4[:], vn[:].rearrange("p (g d) -> p g d", g=4), vn[:].rearrange("p (g d) -> p g d", g=4), ALU.mult)
                    red(sA[:, 8:12], r4[:])
                    rstd_of(sB[:, 8:12], sA[:, 8:12], 64)
                    tt("dve", vn[:].rearrange("p (g d) -> p g d", g=4), vn[:].rearrange("p (g d) -> p g d", g=4),
                       sB[:, 8:12].unsqueeze(2).to_broadcast([128, 4, 64]), ALU.mult)
                    tt("dve", vn[:], vn[:], g2[:, 128:384], ALU.mult)
                    tt("dve", vn[:], vn[:], g2[:, 384:640], ALU.add)
                    if k >= NP - 1:
                        dma(O["new_gmv"].ap()[l, (k - NP + 1) * 128:(k - NP + 2) * 128, :], vn[:])
                    cp("dve", vnb[:], vn[:])
                    si = 1 if smp else 0
                    for g in range(4):
                        mm(ps[3][:, g * 64:(g + 1) * 64], wmb[:, si, g, :], vnb[:, g * 64:(g + 1) * 64])
                    tt("dve", r4[:], ps[3][:, 0:256].rearrange("p (g d) -> p g d", g=4), bsT[:, si, :].unsqueeze(2).to_broadcast([128, 4, 64]), ALU.add)
                    tt("dve", ygb[:], r4[:].rearrange("p g d -> p (g d)"), u_t[:], ALU.mult)
                    dma(ygm_d.ap()[k], ygb[:])
                    dma(zs_d.ap()[k], tok[:, C_Z:C_Z + 512])
                    cp("pool", ckvb[:], ckv[:]); cp("pool", krb[:], kr[:])
                    cp("pool", KV1[:, si, 0:128], ckv[:])
                    pT0 = ps[0][:].bitcast(BF16)
                    tr(pT0[:, 0:128], ckvb[:], IDb)
                    tr(pT0[0:32, 128:256], krb[:], IDb)
                    cp("act", KT1[:, si, :], pT0[:, 0:128]); cp("act", KRT1[:, si, :], pT0[0:32, 128:256])
                    mm(ps[2][:, 0:256], KT1[:, si, :], wuk[:])
                    actf(r4[:].rearrange("p g d -> p (g d)"), ps[2][:, 0:256], AF.Square)
                    red(sA[:, 12:16], r4[:])
                    actf(junk[:, 0:32], kr[:], AF.Square, accum=sA[:, 2:3])
                    ts("dve", sA[:, 12:16], sA[:, 12:16], sA[:, 2:3], ALU.add)
                    rstd_of(sB[:, 12:16], sA[:, 12:16], 96)
                    ts("dve", RK1[:, si, :], sB[:, 12:16], 96 ** -0.5, ALU.mult)
                    if not smp:
                        dma(ccKT.ap()[0:128, k * 128:(k + 1) * 128], KT1[:, 0, :]); dma(ccKT.ap()[128:160, k * 128:(k + 1) * 128], KRT1[:, 0, :])
                        dma(ccKV.ap()[k * 128:(k + 1) * 128, :], KV1[:, 0, 0:128]); dma(ccRK.ap()[k * 128:(k + 1) * 128, :], RK1[:, 0, :])
                    norm_rows(cqb[:], tok[:, C_CQ:C_CQ + 256], g2[:, 640:896], 256, 3)
                    to_featmajor(cqT, cqb, 2)
                    for c in range(2):
                        mm(ps[1][:, 0:384], cqT[:, c, :], wuq[:, c, :], start=(c == 0), stop=(c == 1))
                    cp("dve", q[:].rearrange("p h d -> p (h d)"), ps[1][:, 0:384])
                    qx1, qx2 = q[:, :, 64:80], q[:, :, 80:96]
                    csb = cs.unsqueeze(1).to_broadcast([128, 4, 16]); snb = sn.unsqueeze(1).to_broadcast([128, 4, 16])
                    tt("dve", r4[:, :, 0:16], qx1, csb, ALU.mult); tt("dve", r4[:, :, 16:32], qx2, snb, ALU.mult)
                    tt("dve", r4[:, :, 32:48], qx2, csb, ALU.mult); tt("dve", r4[:, :, 48:64], qx1, snb, ALU.mult)
                    tt("dve", qx1, r4[:, :, 0:16], r4[:, :, 16:32], ALU.subtract)
                    tt("dve", qx2, r4[:, :, 32:48], r4[:, :, 48:64], ALU.add)
                    tt("dve", q2[:], q[:], q[:], ALU.mult)
                    red(sA[:, 12:16], q2[:])
                    rstd_of(sB[:, 12:16], sA[:, 12:16], 96)
                    tt("dve", q2[:], q[:], sB[:, 12:16].unsqueeze(2).to_broadcast([128, 4, 96]), ALU.mult)
                    tt("dve", qgb[:], q2[:], gqk[:].unsqueeze(1).to_broadcast([128, 4, 96]), ALU.mult)
                    pq = ps[0][:].bitcast(BF16).rearrange("p (k t) -> p k t", k=8)
                    for hh in range(4):
                        tr(pq[0:64, hh, :], qgb[:, hh, 0:64], IDb)
                        tr(pq[0:32, 4 + hh, :], qgb[:, hh, 64:96], IDb)
                    cp("act", qnT[:], pq[0:64, 0:4, :]); cp("act", qrT[:], pq[0:32, 4:8, :])
                    for hh in range(4):
                        mm(ps[1][:, hh * 128:(hh + 1) * 128], wukT[:, hh, :], qnT[:, hh, :])
                    cp("dve", qlT[:], ps[1][:])
                    dma(qlt_d.ap()[k], qlT[:]); dma(qrt_d.ap()[k], qrT[:].rearrange("p h t -> p (h t)"))
                    pX = ps[4][:]
                    pX2 = ps[5][:]
                    for m in range(6):
                        dst = (pX if m < 4 else pX2)[:, (m % 4) * 128:(m % 4 + 1) * 128]
                        tr(dst, tok[:, C_XBC + m * 128:C_XBC + (m + 1) * 128], IDf)
                    if not smp:
                        cp("act", xTp[:, 0:4, 3:131], pX.rearrange("p (m t) -> p m t", m=4))
                        cp("act", xTp[:, 4:6, 3:131], pX2[:, 0:256].rearrange("p (m t) -> p m t", m=2))
                    else:
                        cp("act", xT[:, 0:4, :, 3:11], pX.rearrange("p (m b i) -> p m b i", m=4, b=16))
                        cp("act", xT[:, 4:6, :, 3:11], pX2[:, 0:256].rearrange("p (m b i) -> p m b i", m=2, b=16))
                        for b in range(16):
                            dma(O["new_conv"].ap()[l, b], tok[8 * b + 5:8 * b + 8, C_XBC:C_XBC + 768])
                    for m in range(6):
                        if not smp:
                            win = lambda w: xTp[:, m, w:w + 128]
                            a_ = acc[:]
                        else:
                            win = lambda w: xT[:, m, :, w:w + 8]
                            a_ = acc[:].rearrange("p (b i) -> p b i", b=16)
                        ts("dve", a_, win(0), cwT[:, m, 0:1], ALU.mult)
                        for w in range(1, 4):
                            stt(a_, win(w), cwT[:, m, w:w + 1], a_, ALU.mult, ALU.add)
                        actf(xcT[:, m, :], acc[:], AF.Silu, bias=cbT[:, m:m + 1])
                    if not smp and k + 1 < NP:
                        cp("pool", xTp[:, :, 0:3], xTp[:, :, 128:131])
                    if 'nossd' in DBG:
                        continue
                    TRIm, LSTm, ALLm = cst[:, 1 + 3 * si, :], cst[:, 2 + 3 * si, :], cst[:, 3 + 3 * si, :]
                    tt("dve", dt[:], tok[:, C_DT:C_DT + 8], dtb[:], ALU.add)
                    actf(dt[:], dt[:], AF.Exp)
                    actf(dt[:], dt[:], AF.Ln, bias=oneT[:, 0:1])
                    tt("dve", dtA[:], dt[:], aB[:], ALU.mult)
                    for m in range(4):
                        tr(ps[4][:, m * 128:(m + 1) * 128], xcT[:, m, :], IDf)
                    cp("act", xst[:], ps[4][:])
                    tr(ps[5][:, 0:128], xcT[:, 4, :], IDf)
                    cp("dve", Btok[:], ps[5][:, 0:128])
                    gmask = cst[:, 7, 0:2]
                    tt("pool", BTm[:], xcT[:, 4:5, :].to_broadcast([128, 2, 128]), gmask.unsqueeze(2).to_broadcast([128, 2, 128]), ALU.mult)
                    tt("pool", CT1[:], xcT[:, 5:6, :].to_broadcast([128, 2, 128]), gmask.unsqueeze(2).to_broadcast([128, 2, 128]), ALU.mult)
                    dma(ctm_d.ap()[k], CT1[:])
                    for g in range(2):
                        mm(ps[6][:, g * 128:(g + 1) * 128], BTm[:, g, :], CT1[:, g, :])
                    tt("dve", sc[:], ps[6][:, 0:256].rearrange("p (g i) -> p g i", g=2), TRIm.unsqueeze(1).to_broadcast([128, 2, 128]), ALU.mult)
                    tt("pool", Rm[:], TRIm.unsqueeze(1).to_broadcast([128, 8, 128]), dtA[:].unsqueeze(2).to_broadcast([128, 8, 128]), ALU.mult)
                    mm(ps[1][:], LSTm, Rm[:, 0:4, :].rearrange("p h i -> p (h i)"))
                    mm(ps[2][:], LSTm, Rm[:, 4:8, :].rearrange("p h i -> p (h i)"))
                    actf(dec[:, 0:4, :].rearrange("p h i -> p (h i)"), ps[1][:], AF.Exp)
                    actf(dec[:, 4:8, :].rearrange("p h i -> p (h i)"), ps[2][:], AF.Exp)
                    for g in range(2):
                        tt("dve", MT[:, 4 * g:4 * g + 4, :], dec[:, 4 * g:4 * g + 4, :], sc[:, g:g + 1, :].to_broadcast([128, 4, 128]), ALU.mult)
                    mm(ps[7][:, 0:8], TRIm, dtA[:])
                    mm(ps[7][:, 8:16], ALLm, dtA[:])
                    actf(eac[:], ps[7][:, 0:8], AF.Exp)
                    actf(cd[:], ps[7][:, 8:16], AF.Exp)
                    tt("dve", te[:], ps[7][:, 8:16], ps[7][:, 0:8], ALU.subtract) if False else None
                    cp("dve", te[:], ps[7][:, 0:8])
                    tt("dve", te[:], ps[7][:, 8:16], te[:], ALU.subtract)
                    actf(te[:], te[:], AF.Exp)
                    tt("dve", te[:], te[:], dt[:], ALU.mult)
                    x3 = xst[:].rearrange("p (h d) -> p h d", h=8)
                    tt("dve", xdt[:].rearrange("p (h d) -> p h d", h=8), x3, dt[:].unsqueeze(2).to_broadcast([128, 8, 64]), ALU.mult)
                    tt("pool", xte[:].rearrange("p (h d) -> p h d", h=8), x3, te[:].unsqueeze(2).to_broadcast([128, 8, 64]), ALU.mult)
                    for hh in range(8):
                        mm(ps[4][:, hh * 64:(hh + 1) * 64], MT[:, hh, :], xdt[:, hh * 64:(hh + 1) * 64])
                    if not smp:
                        for g in range(2):
                            mm(ps[5][:, g * 256:(g + 1) * 256], CT1[:, g, :], HT2b[:])
                    else:
                        for b in range(16):
                            load_hs(b)
                            dma(hs_d.ap()[b], HS1[:])
                            cp("pool", HS1b[:], HS1[:])
                            memset("pool", CZ[b % 2][:], 0.0)
                            cp("pool", CZ[b % 2][:, :, 8 * b:8 * b + 8], CT1[:, :, 8 * b:8 * b + 8])
                            for g in range(2):
                                mm((ps[5] if g == 0 else ps[2])[:, 0:256], CZ[b % 2][:, g, :], HS1b[:], start=(b == 0), stop=(b == 15))
                    for g in range(2):
                        yo = (ps[5][:, 256 * g:256 * g + 256] if not smp else (ps[5] if g == 0 else ps[2])[:, 0:256])
                        tt("dve", ys[:, 256 * g:256 * g + 256].rearrange("p (h d) -> p h d", h=4), yo.rearrange("p (h d) -> p h d", h=4),
                           eac[:, 4 * g:4 * g + 4].unsqueeze(2).to_broadcast([128, 4, 64]), ALU.mult)
                    tt("dve", ys[:], ys[:], ps[4][:], ALU.add)
                    tt("pool", ytmp[:].rearrange("p (h d) -> p h d", h=8), x3, dskB[:].unsqueeze(2).to_broadcast([128, 8, 64]), ALU.mult)
                    tt("dve", ys[:], ys[:], ytmp[:], ALU.add)
                    dma(ys_d.ap()[k], ys[:])
                    if not smp:
                        tt("dve", ecor[:, k, :], eac[:], cumD[:], ALU.mult)
                        tt("dve", cumD[:], cumD[:], cd[:], ALU.mult)
                        for g in range(2):
                            mm(ps[6][:, 256 * g:256 * g + 256], Btok[:], xte[:, 256 * g:256 * g + 256])
                        for g in range(2):
                            pg = slice(64 * g, 64 * g + 64)
                            tt("dve", HT2[pg, :].rearrange("p (h d) -> p h d", h=4), HT2[pg, :].rearrange("p (h d) -> p h d", h=4),
                               cd[pg, 4 * g:4 * g + 4].unsqueeze(2).to_broadcast([64, 4, 64]), ALU.mult)
                            tt("dve", HT2[pg, :], HT2[pg, :], ps[6][pg, 256 * g:256 * g + 256], ALU.add)
                        cp("pool", HT2b[:], HT2[:])
                    else:
                        tt("dve", Wsel[:], blkind[:].unsqueeze(2).to_broadcast([128, 16, 8]), dtA[:].unsqueeze(1).to_broadcast([128, 16, 8]), ALU.mult)
                        mm(ps[7][:, 16:144], cst[:, 3, :], Wsel[:].rearrange("p b h -> p (b h)"))
                        actf(cdS[:].rearrange("p b h -> p (b h)"), ps[7][:, 16:144], AF.Exp)
                        for b in range(16):
                            pb = ps[1 + (b % 2)]
                            dma(HS1[:], hs_d.ap()[b])
                            ts("pool", BZ[b % 2][:], Btok[:], blkind[:, b:b + 1], ALU.mult)
                            for g in range(2):
                                mm(pb[:, 256 * g:256 * g + 256], BZ[b % 2][:], xte[:, 256 * g:256 * g + 256])
                            for g in range(2):
                                pg = slice(64 * g, 64 * g + 64)
                                tt("dve", Hf[pg, :].rearrange("p (h d) -> p h d", h=4), HS1[pg, :].rearrange("p (h d) -> p h d", h=4),
                                   cdS[pg, b, 4 * g:4 * g + 4].unsqueeze(2).to_broadcast([64, 4, 64]), ALU.mult)
                                tt("dve", Hf[pg, :], Hf[pg, :], pb[pg, 256 * g:256 * g + 256], ALU.add)
                            ssm_out(l, b)
                if 'noxch' in DBG:
                    P.barrier(); sa.close(); continue
                dma(ccS.ap()[:, 0:256], HT2[:]); dma(ccS.ap()[:, 256:264], cumD[:])
                P.dma(lambda e: e.collective_compute("AllGather", ALU.bypass, replica_groups=[[0, 1, 2, 3], [4, 5, 6, 7]],
                                                     ins=[ccS.ap()], outs=[ccSg.ap()]), reads=["ccS"], writes=["ccSg"], q="pool", inc=1)
                memset("dve", Hin[:], 0.0)
                for r in range(GRP):
                    mcol = vis[:, 4 + r:5 + r]
                    dma(SG[:], ccSg.ap()[r * 128:(r + 1) * 128, :])
                    ts("dve", dmr[:], SG[:, 256:264], -1.0, ALU.add, mcol, ALU.mult)
                    ts("dve", dmr[:], dmr[:], 1.0, ALU.add)
                    ts("dve", Sm[:], SG[:, 0:256], mcol, ALU.mult)
                    for g in range(2):
                        pg = slice(64 * g, 64 * g + 64)
                        tt("dve", Hin[pg, :].rearrange("p (h d) -> p h d", h=4), Hin[pg, :].rearrange("p (h d) -> p h d", h=4),
                           dmr[pg, 4 * g:4 * g + 4].unsqueeze(2).to_broadcast([64, 4, 64]), ALU.mult)
                    tt("dve", Hin[:], Hin[:], Sm[:], ALU.add)
                cp("dve", Hinb[:], Hin[:])
                for g in range(2):
                    pg = slice(64 * g, 64 * g + 64)
                    tt("dve", Hf[pg, :].rearrange("p (h d) -> p h d", h=4), Hin[pg, :].rearrange("p (h d) -> p h d", h=4),
                       cumD[pg, 4 * g:4 * g + 4].unsqueeze(2).to_broadcast([64, 4, 64]), ALU.mult)
                tt("dve", Hf[:], Hf[:], HT2[:], ALU.add)
                ssm_out(l, 16)
                RG = [[0, 1, 2, 3], [4, 5, 6, 7]]
                for (a_, b_) in ((ccKT, ccKTg), (ccKV, ccKVg), (ccRK, ccRKg)):
                    P.dma(lambda e, a_=a_, b_=b_: e.collective_compute("AllGather", ALU.bypass, replica_groups=RG, ins=[a_.ap()], outs=[b_.ap()]),
                          reads=[a_.ap()], writes=[b_.ap()], q="pool", inc=1)
                P.barrier()
                sa.close()
                P.barrier()
                if 'C' not in cfg.phases:
                    continue
                sc_ = ExitStack(); sc_.__enter__()
                NKT = GRP * NP
                NKT = (GRP - 1) * NP
                KTg = sbt(sc_, "KTg", [128, GRP - 1, NPT], BF16); KRTg = sbt(sc_, "KRTg", [32, GRP - 1, NPT], BF16)
                KVg = sbt(sc_, "KVg", [128, NKT, 129], BF16); RKg = sbt(sc_, "RKg", [128, NKT, 4])
                KTl = sbt(sc_, "KTl", [128, NP, 128], BF16); KRTl = sbt(sc_, "KRTl", [32, NP, 128], BF16)
                KVl = sbt(sc_, "KVl", [128, NP, 129], BF16); RKl = sbt(sc_, "RKl", [128, NP, 4])
                qlt = [sbt(sc_, "qlt%d" % i, [128, 512], BF16) for i in range(2)]; qrt = [sbt(sc_, "qrt%d" % i, [32, 512], BF16) for i in range(2)]
                pTb = [sbt(sc_, "pTb%d" % i, [128, 4, 128], BF16) for i in range(2)]
                orec = sbt(sc_, "orec", [128, 4]); olat = sbt(sc_, "olat", [128, 4, 128], BF16); olatT = sbt(sc_, "olatT", [128, 4, 128], BF16)
                cat = sbt(sc_, "cat", [128, 1024], BF16); ysl = sbt(sc_, "ysl", [128, 512]); zl = sbt(sc_, "zl", [128, 512]); gsB = sbt(sc_, "gsB", [128, 512])
                ycor = sbt(sc_, "ycor", [128, 512])
                scf = [ycor[:].rearrange("p (h i) -> p h i", h=4)]
                IDXi = sbt(sc_, "IDXi", [128, 16 * NPG], I32); IDXf = sbt(sc_, "IDXf", [128, 256]); iotaP = sbt(sc_, "iotaP", [128, 1], I32); iotaF = sbt(sc_, "iotaF", [128, 1])
                pgcb = [sbt(sc_, "pgcb%d" % i, [128, 129], BF16) for i in range(2)]; pgkb = [sbt(sc_, "pgkb%d" % i, [128, 32], BF16) for i in range(2)]
                cTs = [sbt(sc_, "cTs%d" % i, [128, 128], BF16) for i in range(2)]; kTs = [sbt(sc_, "kTs%d" % i, [32, 128], BF16) for i in range(2)]
                rks = [sbt(sc_, "rks%d" % i, [128, 8]) for i in range(2)]; pTs = [sbt(sc_, "pTs%d" % i, [128, 4, 8], BF16) for i in range(2)]
                sq = sbt(sc_, "sq", [128, 4, 64]); sc32 = [sbt(sc_, "sc32_%d" % i, [128, 4, 8]) for i in range(2)]; ols = sbt(sc_, "ols", [32, 128], BF16); odn = sbt(sc_, "odn", [32, 1])
                load_w(wst, wb, I["w_out"].ap()[l], 1024)
                dma(gsB[:], bc_row(I["g_ssd_out"], l * 512, 512))
                for r in range(GRP - 1):
                    dma(KTg[:, r, :], ccKTg.ap()[r * 160:r * 160 + 128, :])
                    dma(KRTg[:, r, :], ccKTg.ap()[r * 160 + 128:r * 160 + 160, :])
                dma(KVg[:, :, 0:128], ccKVg.ap()[0:NKT * 128, :].rearrange("(n p) c -> p n c", p=128))
                dma(RKg[:], ccRKg.ap()[0:NKT * 128, :].rearrange("(n p) c -> p n c", p=128))
                memset("dve", KVg[:, :, 128:129], 1.0); memset("dve", KVl[:, :, 128:129], 1.0)
                dma(KTl[:], ccKT.ap()[0:128, :].rearrange("p (k t) -> p k t", k=NP)); dma(KRTl[:], ccKT.ap()[128:160, :].rearrange("p (k t) -> p k t", k=NP))
                dma(KVl[:, :, 0:128], ccKV.ap().rearrange("(k p) c -> p k c", p=128)); dma(RKl[:], ccRK.ap().rearrange("(k p) c -> p k c", p=128))
                for i in range(2):
                    memset("dve", pgcb[i][:, 128:129], 1.0)
                dma(IDXi[:], bc_row(I["page_idx"], 0, 16 * NPG))
                P.op("pool", lambda e, iotaP=iotaP: e.iota(iotaP[:], pattern=[[0, 1]], base=0, channel_multiplier=1), writes=[iotaP[:]])
                cp("dve", iotaF[:], iotaP[:])
                for c0 in range(0, 16 * NPG, 256):
                    c1 = min(16 * NPG, c0 + 256)
                    cp("dve", IDXf[:, 0:c1 - c0], IDXi[:, c0:c1])
                    ts("dve", IDXf[:, 0:c1 - c0], IDXf[:, 0:c1 - c0], 128.0, ALU.mult, iotaF[:, 0:1], ALU.add)
                    if l > 0:
                        ts("dve", IDXf[:, 0:c1 - c0], IDXf[:, 0:c1 - c0], float(l * NPHYS * 128), ALU.add)
                    cp("dve", IDXi[:, c0:c1], IDXf[:, 0:c1 - c0])
                ckv_flat = I["cache_ckv"].ap().rearrange("l r c -> (l r) c"); kr_flat = I["cache_kr"].ap().rearrange("l r c -> (l r) c")
                TRIb, TRISb = cstb[:, 1, :], cstb[:, 4, :]

                def attn_tail(k):
                    for hh in range(4):
                        ob = ps[3 + hh][:, 0:129]
                        recip(orec[:, hh:hh + 1], ob[:, 128:129])
                        ts("dve", olat[:, hh, :], ob[:, 0:128], orec[:, hh:hh + 1], ALU.mult)
                    po = ps[0][:].bitcast(BF16).rearrange("p (k t) -> p k t", k=8)
                    for hh in range(4):
                        tr(po[:, hh, :], olat[:, hh, :], IDb)
                    cp("act", olatT[:], po[:, 0:4, :])
                    ymla()

                def ymla():
                    for hh in range(4):
                        mm(ps[7][:, hh * 64:(hh + 1) * 64], olatT[:, hh, :], wuv[:, hh * 64:(hh + 1) * 64])
                    cp("act", cat[:, 512:768], ps[7][:, 0:256])

                for k in range(NT):
                    smp = (k == NP)
                    ql, qr = qlt[k % 2], qrt[k % 2]
                    dma(ql[:], qlt_d.ap()[k]); dma(qr[:], qrt_d.ap()[k])
                    if not smp:
                        keys = []
                        for r in range(GRP - 1):
                            for kt in range(NP):
                                keys.append((KTg[:, r, kt * 128:(kt + 1) * 128], KRTg[:, r, kt * 128:(kt + 1) * 128], KVg[:, r * NP + kt, :], RKg[:, r * NP + kt, :], vis[:, r:r + 1], False))
                        for kt in range(k + 1):
                            keys.append((KTl[:, kt, :], KRTl[:, kt, :], KVl[:, kt, :], RKl[:, kt, :], None, kt == k))
                        N = len(keys)
                        for n, (kT_, krT_, kv_, rk_, bias_, diag_) in enumerate(keys):
                            sps = ps[1 + n % 2]; pT = pTb[n % 2]
                            mm(sps[:], kT_, ql[:], start=True, stop=False)
                            mm(sps[:], krT_, qr[:], start=False, stop=True)
                            tt("dve", scf[0][:], sps[:].rearrange("p (h i) -> p h i", h=4), rk_[:, 0:4].unsqueeze(2).to_broadcast([128, 4, 128]), ALU.mult)
                            actf(pT[:].rearrange("p h i -> p (h i)"), scf[0][:].rearrange("p h i -> p (h i)"), AF.Exp, bias=bias_)
                            if diag_:
                                tt("pool", pT[:], pT[:], TRIb.unsqueeze(1).to_broadcast([128, 4, 128]), ALU.mult)
                            for hh in range(4):
                                mm(ps[3 + hh][:, 0:129], pT[:, hh, :], kv_, start=(n == 0), stop=(n == N - 1))
                        attn_tail(k)
                    else:
                        q4 = ql[:].rearrange("p (h t) -> p h t", h=4); qr4 = qr[:].rearrange("p (h t) -> p h t", h=4)
                        for b in range(16 if 'nosamp' not in DBG else 0):
                            for pg in range(NPG + 1):
                                n = b * (NPG + 1) + pg; i2 = n % 2
                                if pg < NPG and 'nopage' in DBG:
                                    continue
                                if pg < NPG:
                                    col = b * NPG + pg
                                    P.dma(lambda e, i2=i2, col=col, pgcb=pgcb, ckv_flat=ckv_flat, IDXi=IDXi: e.indirect_dma_start(out=pgcb[i2][:, 0:128], out_offset=None, in_=ckv_flat,
                                          in_offset=bass.IndirectOffsetOnAxis(ap=IDXi[:, col:col + 1], axis=0)), reads=[IDXi[:]], writes=[pgcb[i2][:]], q="pool")
                                    P.dma(lambda e, i2=i2, col=col, pgkb=pgkb, kr_flat=kr_flat, IDXi=IDXi: e.indirect_dma_start(out=pgkb[i2][:], out_offset=None, in_=kr_flat,
                                          in_offset=bass.IndirectOffsetOnAxis(ap=IDXi[:, col:col + 1], axis=0)), reads=[IDXi[:]], writes=[pgkb[i2][:]], q="pool")
                                    pz = ps[0][:].bitcast(BF16)
                                    tr(pz[:, 0:128], pgcb[i2][:, 0:128], IDb); tr(pz[0:32, 128:256], pgkb[i2][:], IDb)
                                    cp("act", cTs[i2][:], pz[:, 0:128]); cp("act", kTs[i2][:], pz[0:32, 128:256])
                                    mm(ps[2][:, 0:256], cTs[i2][:], wuk2[:])
                                    actf(sq[:].rearrange("p g d -> p (g d)"), ps[2][:, 0:256], AF.Square)
                                    red(rks[i2][:, 0:4], sq[:])
                                    actf(junk[:, 0:32], pgkb[i2][:], AF.Square, accum=rks[i2][:, 4:5])
                                    ts("dve", rks[i2][:, 0:4], rks[i2][:, 0:4], rks[i2][:, 4:5], ALU.add)
                                    rstd_of(rks[i2][:, 0:4], rks[i2][:, 0:4], 96)
                                    ts("dve", rks[i2][:, 0:4], rks[i2][:, 0:4], 96 ** -0.5, ALU.mult)
                                    kT_, krT_, kv_, rk_ = cTs[i2][:], kTs[i2][:], pgcb[i2][:], rks[i2]
                                else:
                                    kT_, krT_, kv_, rk_ = KT1[:, 1, :], KRT1[:, 1, :], KV1[:, 1, :], RK1[:, 1, :]
                                sps = ps[1][:, (n % 4) * 32:(n % 4) * 32 + 32]
                                mm(sps, kT_, q4[:, :, 8 * b:8 * b + 8], start=True, stop=False)
                                mm(sps, krT_, qr4[:, :, 8 * b:8 * b + 8], start=False, stop=True)
                                pT = pTs[i2]
                                tt("dve", sc32[i2][:], sps.rearrange("p (h i) -> p h i", h=4), rk_[:, 0:4].unsqueeze(2).to_broadcast([128, 4, 8]), ALU.mult)
                                actf(pT[:].rearrange("p h i -> p (h i)"), sc32[i2][:].rearrange("p h i -> p (h i)"), AF.Exp)
                                if pg == NPG:
                                    tt("pool", pT[:], pT[:], TRISb[:, 8 * b:8 * b + 8].unsqueeze(1).to_broadcast([128, 4, 8]), ALU.mult)
                                mm(ps[3][0:32, 0:129], pT[:].rearrange("p h i -> p (h i)"), kv_, start=(pg == 0), stop=(pg == NPG))
                            recip(odn[:], ps[3][0:32, 128:129])
                            ts("dve", ols[:], ps[3][0:32, 0:128], odn[:, 0:1], ALU.mult)
                            pz = ps[0][:].bitcast(BF16)
                            tr(pz[:, 256:288], ols[:], IDb[0:32, 0:32])
                            cp("act", olatT[:, :, 8 * b:8 * b + 8], pz[:, 256:288].rearrange("p (h i) -> p h i", h=4))
                        ymla()
                    dma(ysl[:], ys_d.ap()[k]); dma(zl[:], zs_d.ap()[k]); dma(cat[:, 768:1024], ygm_d.ap()[k])
                    if not smp:
                        dma(CT1[:], ctm_d.ap()[k])
                        for g in range(2):
                            mm(ps[6][:, 256 * g:256 * g + 256], CT1[:, g, :], Hinb[:])
                        tt("dve", ycor[:].rearrange("p (h d) -> p h d", h=8), ps[6][:].rearrange("p (h d) -> p h d", h=8),
                           ecor[:, k, :].unsqueeze(2).to_broadcast([128, 8, 64]), ALU.mult)
                        tt("dve", ysl[:], ysl[:], ycor[:], ALU.add)
                    actf(zl[:], zl[:], AF.Silu)
                    tt("dve", ysl[:], ysl[:], zl[:], ALU.mult)
                    for g in range(2):
                        norm_rows(cat[:, 256 * g:256 * g + 256], ysl[:, 256 * g:256 * g + 256], gsB[:, 256 * g:256 * g + 256], 256, 4 + g)
                    to_featmajor(hT, cat)
                    for nb in range(2):
                        for c in range(8):
                            mm(ps[6 + nb][:], hT[:, c, :], wb[:, c, nb * 512:(nb + 1) * 512], start=(c == 0), stop=(c == 7))
                        tt("dve", xres[:, k, nb * 512:(nb + 1) * 512], xres[:, k, nb * 512:(nb + 1) * 512], ps[6 + nb][:], ALU.add)
                P.barrier()
                sc_.close()
                P.barrier()
            if 'D' in cfg.phases:
                with ExitStack() as sd:
                    wst = sbt(sd, "wstd", [128, 8, 280]); wq = sbt(sd, "wq", [128, 8, 1024], BF16); wo = sbt(sd, "wo", [128, 8, 1024], BF16)
                    gmx = sbt(sd, "gmx", [128, D]); gq = sbt(sd, "gq", [128, 256]); gk = sbt(sd, "gk", [128, 256]); gmm = sbt(sd, "gmm", [128, D])
                    memx = sbt(sd, "memxd", [128, D]); raw = sbt(sd, "raw", [128, D]); nb_ = sbt(sd, "nb_", [128, D], BF16)
                    hTm = sbt(sd, "hTm", [128, 2, 8, 128], BF16)
                    kT_p = sbt(sd, "kT_p", [128, 4, 2, 256], BF16); vE_p = sbt(sd, "vE_p", [128, 2, 4, 257], BF16)
                    kT_s = sbt(sd, "kT_s", [128, 4, 2, 256], BF16); vE_s = sbt(sd, "vE_s", [128, 2, 4, 257], BF16)
                    qT = sbt(sd, "qT", [128, 4, 2, 128], BF16); pTm = [sbt(sd, "pTm%d" % i, [128, 128], BF16) for i in range(2)]
                    ob = sbt(sd, "ob", [128, D], BF16); orc = sbt(sd, "orc", [128, 4])
                    Kf = sbt(sd, "Kf", [128, 2, D]); Vf = sbt(sd, "Vf", [128, 2, D]); kbs = sbt(sd, "kbs", [128, 2, D], BF16)
                    cmask = sbt(sd, "cmask", [128, 16, 128]); cmaskb = sbt(sd, "cmaskb", [128, 16, 128], BF16)
                    dma(gmx[:], bc_row(I["g_memx"], l * D, D)); dma(gq[:], bc_row(I["g_mq"], l * 256, 256))
                    dma(gk[:], bc_row(I["g_mk"], l * 256, 256)); dma(gmm[:], bc_row(I["g_memm"], l * D, D))
                    dma(cmask[:], I["colmask"].ap()); cp("pool", cmaskb[:], cmask[:])
                    memset("dve", vE_p[:, :, :, 256:257], 1.0); memset("dve", vE_s[:, :, :, 256:257], 1.0)

                    def make_kT(dst, src_b):
                        for mt in range(2):
                            pz = ps[0][:].bitcast(BF16).rearrange("p (k t) -> p k t", k=8)
                            for c8 in range(8):
                                tr(pz[:, c8, :], src_b[:, mt, c8 * 128:(c8 + 1) * 128], IDb)
                            cp("act", dst[:, :, :, mt * 128:(mt + 1) * 128], pz.rearrange("p (h dc) t -> p h dc t", h=4))

                    for mt in range(2):
                        dma(memx[:], I["mem_full"].ap()[mt * 128:(mt + 1) * 128, :])
                        norm_rows(h[:], memx[:], gmm[:], D, 0)
                        to_featmajor(hT, h)
                        cp("pool", hTm[:, mt], hT[:])
                    load_w(wst, wo, I["w_mk"].ap()[l], D)
                    for mt in range(2):
                        proj(raw, hTm[:, mt], wo, D)
                        for hd in range(4):
                            norm_rows(kbs[:, mt, hd * 256:(hd + 1) * 256], raw[:, hd * 256:(hd + 1) * 256], gk[:], 256, 1 + hd)
                    make_kT(kT_p, kbs)
                    load_w(wst, wo, I["w_mv"].ap()[l], D)
                    for mt in range(2):
                        proj(raw, hTm[:, mt], wo, D)
                        cp("pool", vE_p[:, mt, :, 0:256], raw[:].rearrange("p (h d) -> p h d", h=4))
                    load_w(wst, wq, I["w_mq"].ap()[l], D)
                    load_w(wst, wo, I["w_mo"].ap()[l], D)

                    for k in range(NT):
                        smp = (k == NP)
                        norm_rows(h[:], xres[:, k, :], gmx[:], D, 0)
                        to_featmajor(hT, h)
                        proj(raw, hT, wq, D)
                        for hd in range(4):
                            norm_rows(nb_[:, hd * 256:(hd + 1) * 256], raw[:, hd * 256:(hd + 1) * 256], gq[:], 256, 1 + hd)
                        pz = ps[0][:].bitcast(BF16).rearrange("p (k t) -> p k t", k=8)
                        for c8 in range(8):
                            tr(pz[:, c8, :], nb_[:, c8 * 128:(c8 + 1) * 128], IDb)
                        cp("act", qT[:], pz.rearrange("p (h dc) t -> p h dc t", h=4))
                        nseq = 16 if smp else 1
                        it = 0
                        for b in range(nseq):
                            if smp:
                                dma(Kf[:], I["cache_mem_k"].ap()[l, b].rearrange("(mt p) d -> p mt d", p=128))
                                dma(Vf[:], I["cache_mem_v"].ap()[l, b].rearrange("(mt p) d -> p mt d", p=128))
                                cp("pool", kbs[:], Kf[:])
                                for mt in range(2):
                                    cp("dve" if mt == 0 else "pool", vE_s[:, mt, :, 0:256], Vf[:, mt, :].rearrange("p (h d) -> p h d", h=4))
                                make_kT(kT_s, kbs)
                                kT_, vE_ = kT_s, vE_s
                            else:
                                kT_, vE_ = kT_p, vE_p
                            for hd in range(4):
                                for mt in range(2):
                                    sps = ps[1 + it % 2][:, 0:128]; pT = pTm[it % 2]; it += 1
                                    for dc in range(2):
                                        mm(sps, kT_[:, hd, dc, mt * 128:(mt + 1) * 128], qT[:, hd, dc, :], start=(dc == 0), stop=(dc == 1))
                                    actf(pT[:], sps, AF.Exp, scale=1.0 / 16.0)
                                    if smp:
                                        tt("pool", pT[:], pT[:], cmaskb[:, b, :], ALU.mult)
                                    mm(ps[3 + hd][:, 0:257], pT[:], vE_[:, mt, hd, :], start=(b == 0 and mt == 0), stop=(b == nseq - 1 and mt == 1))
                        for hd in range(4):
                            recip(orc[:, hd:hd + 1], ps[3 + hd][:, 256:257])
                            ts("dve", ob[:, hd * 256:(hd + 1) * 256], ps[3 + hd][:, 0:256], orc[:, hd:hd + 1], ALU.mult)
                        to_featmajor(hT, ob)
                        for nb2 in range(2):
                            for c in range(8):
                                mm(ps[1 + nb2][:], hT[:, c, :], wo[:, c, nb2 * 512:(nb2 + 1) * 512], start=(c == 0), stop=(c == 7))
                            tt("dve", xres[:, k, nb2 * 512:(nb2 + 1) * 512], xres[:, k, nb2 * 512:(nb2 + 1) * 512], ps[1 + nb2][:], ALU.add)
                    P.barrier()
                P.barrier()
            if 'E' in cfg.phases:
                with ExitStack() as se0:
                    stg = [sbt(se0, "stg%d" % i, [128, 2048]) for i in range(2)]; stb = [sbt(se0, "stb%d" % i, [128, 2048], BF16) for i in range(2)]
                    uT_v = I["peer_uT"].ap()[l].rearrange("(k p) e -> p k e", p=128); uTb_v = UTb.ap().rearrange("(k p) e -> p k e", p=128)
                    v_v = I["peer_v"].ap()[l].rearrange("(c p) d -> p c d", p=128); vb_v = Vb.ap().rearrange("(c p) d -> p c d", p=128)
                    engs = ("pool", "dve", "act")
                    for i in range(64):
                        a_, b_ = stg[i % 2], stb[i % 2]
                        dma(a_[:].rearrange("p (k e) -> p k e", k=8), uT_v[:, :, i * 256:(i + 1) * 256])
                        cp(engs[i % 3], b_[:], a_[:])
                        dma(uTb_v[:, :, i * 256:(i + 1) * 256], b_[:].rearrange("p (k e) -> p k e", k=8))
                    for i in range(64):
                        a_, b_ = stg[i % 2], stb[i % 2]
                        dma(a_[:].rearrange("p (c d) -> p c d", c=2), v_v[:, 2 * i:2 * i + 2, :])
                        cp(engs[i % 3], b_[:], a_[:])
                        dma(vb_v[:, 2 * i:2 * i + 2, :], b_[:].rearrange("p (c d) -> p c d", c=2))
                    P.barrier()
                P.barrier()
                thr_all = sbt(st, "thr_all", [128, NT, 8]); nb_all = sbt(st, "nb_all", [128, NT, 8])
                with ExitStack() as sea:
                    wst = sbt(sea, "wste", [128, 8, 280]); wpq = sbt(sea, "wpq", [128, 8, 2048], BF16)
                    kT = sbt(sea, "kTe", [128, 16, 128], BF16); gff = sbt(sea, "gff", [128, D])
                    qTs = sbt(sea, "qTs", [128, 16, 128], BF16); S = sbt(sea, "S", [128, 16, 128]); Sw = sbt(sea, "Sw", [128, 256])
                    V16 = sbt(sea, "V16", [128, 16, 16]); cand = sbt(sea, "cand", [128, 8, 256]); SC = sbt(sea, "SC", [128, 8, 16])
                    SCm = sbt(sea, "SCm", [128, 8, 16]); Zs = sbt(sea, "Zs", [128, 8])
                    load_w(wst, wpq, I["w_pq"].ap()[l], 2048)
                    dma(S[:].rearrange("p c k -> p (c k)"), I["peer_kT"].ap()[l].rearrange("d c k -> d (c k)")); cp("dve", kT[:], S[:])
                    dma(gff[:], bc_row(I["g_ffn"], l * D, D))
                    for k in range(NT):
                        norm_rows(h[:], xres[:, k, :], gff[:], D, 0)
                        to_featmajor(hT, h)
                        dma(hT_d.ap()[k], hT[:].rearrange("p k t -> p (k t)"))
                        for c4 in range(4):
                            bank = ps[1 + c4 % 2]
                            for ci in range(4):
                                c = c4 * 4 + ci
                                for kk in range(8):
                                    mm(bank[:, ci * 128:(ci + 1) * 128], wpq[:, kk, c * 128:(c + 1) * 128], hT[:, kk, :], start=(kk == 0), stop=(kk == 7))
                            cp("act", qTs[:, c4 * 4:(c4 + 1) * 4, :], bank[:].rearrange("p (c t) -> p c t", c=4))
                        for c4 in range(4):
                            bank = ps[3 + c4 % 2]
                            for ci in range(4):
                                c = c4 * 4 + ci
                                mm(bank[:, ci * 128:(ci + 1) * 128], qTs[:, c, :], kT[:, c, :])
                            cp("dve", S[:, c4 * 4:(c4 + 1) * 4, :], bank[:].rearrange("p (c t) -> p c t", c=4))
                        dma(s_d.ap()[k], S[:].rearrange("p c k -> p (c k)"))
                        for c in range(16):
                            P.op("dve", lambda e, c=c, S=S, V16=V16: e.max(out=V16[:, c, 0:8], in_=S[:, c, :]), reads=[S[:]], writes=[V16[:]])
                            P.op("dve", lambda e, c=c, S=S, V16=V16, Sw=Sw: e.match_replace(out=Sw[:, 0:128], in_to_replace=V16[:, c, 0:8], in_values=S[:, c, :], imm_value=-1e30), reads=[S[:], V16[:]], writes=[Sw[:]])
                            P.op("dve", lambda e, c=c, V16=V16, Sw=Sw: e.max(out=V16[:, c, 8:16], in_=Sw[:, 0:128]), reads=[Sw[:]], writes=[V16[:]])
                        V4 = V16[:].rearrange("p (h s) k -> p h s k", s=2)
                        tt("pool", cand[:].rearrange("p h (a b) -> p h a b", a=16), V4[:, :, 0, :].unsqueeze(3).to_broadcast([128, 8, 16, 16]),
                           V4[:, :, 1, :].unsqueeze(2).to_broadcast([128, 8, 16, 16]), ALU.add)
                        for hh in range(8):
                            P.op("dve", lambda e, hh=hh, SC=SC, cand=cand: e.max(out=SC[:, hh, 0:8], in_=cand[:, hh, :]), reads=[cand[:]], writes=[SC[:]])
                            P.op("dve", lambda e, hh=hh, SC=SC, cand=cand, Sw=Sw: e.match_replace(out=Sw[:], in_to_replace=SC[:, hh, 0:8], in_values=cand[:, hh, :], imm_value=-1e30), reads=[cand[:], SC[:]], writes=[Sw[:]])
                            P.op("dve", lambda e, hh=hh, SC=SC, Sw=Sw: e.max(out=SC[:, hh, 8:16], in_=Sw[:]), reads=[Sw[:]], writes=[SC[:]])
                        cp("dve", thr_all[:, k, :], SC[:, :, 15])
                        tt("dve", SCm[:], SC[:], SC[:, :, 0:1].to_broadcast([128, 8, 16]), ALU.subtract)
                        actf(SCm[:], SCm[:], AF.Exp)
                        red(Zs[:], SCm[:])
                        actf(Zs[:], Zs[:], AF.Ln)
                        tt("dve", Zs[:], Zs[:], SC[:, :, 0], ALU.add)
                        ts("dve", nb_all[:, k, :], Zs[:], -1.0, ALU.mult)
                    P.barrier()
                P.barrier()
                with ExitStack() as seb:
                    S2 = [sbt(seb, "Sb%d" % i, [128, 16, 128]) for i in range(2)]; SUM = [sbt(seb, "SUM%d" % i, [128, 8, 128]) for i in range(3)]
                    Eb = [sbt(seb, "Eb%d" % i, [128, 1024], BF16) for i in range(3)]; Gh2 = [sbt(seb, "Gh%d" % i, [128, 8, 1024], BF16) for i in range(2)]
                    hT2 = sbt(seb, "hT2", [128, 8, 256], BF16)
                    UTk = [sbt(seb, "UTk%d" % i, [128, 8, 1024], BF16) for i in range(2)]; Vk = [sbt(seb, "Vk%d" % i, [128, 8, 1024], BF16) for i in range(1)]
                    gA = [sbt(seb, "gA%d" % i, [128, 512], BF16) for i in range(2)]; WT = [sbt(seb, "WT%d" % i, [128, 512], BF16) for i in range(2)]
                    uTb_v = UTb.ap().rearrange("(k p) e -> p k e", p=128); vb_v = Vb.ap().rearrange("(c p) d -> p c d", p=128)
                    it = 0
                    for k0 in range(0, NT, 2):
                        gs = min(2, NT - k0)
                        for j in range(gs):
                            dma(S2[j][:].rearrange("p c k -> p (c k)"), s_d.ap()[k0 + j])
                            dma(hT2[:, :, j * 128:(j + 1) * 128], hT_d.ap()[k0 + j].rearrange("p (k t) -> p k t", k=8))
                        for ab in range(16):
                            ub, vb2 = UTk[ab % 2], Vk[0]
                            dma(ub[:], uTb_v[:, :, ab * 1024:(ab + 1) * 1024]); dma(vb2[:], vb_v[:, ab * 8:(ab + 1) * 8, :])
                            for j in range(gs):
                                S = S2[j]
                                for hh in range(8):
                                    sm_, eb_ = SUM[it % 3], Eb[it % 3]; it += 1
                                    tt("pool", sm_[:], S[:, 2 * hh, ab * 8:(ab + 1) * 8].unsqueeze(2).to_broadcast([128, 8, 128]),
                                       S[:, 2 * hh + 1, :].unsqueeze(1).to_broadcast([128, 8, 128]), ALU.add)
                                    sflat = sm_[:].rearrange("p a b -> p (a b)")
                                    actf(eb_[:], sflat, AF.Exp, bias=nb_all[:, k0 + j, hh:hh + 1])
                                    stt(Gh2[j][:, hh, :], sflat, thr_all[:, k0 + j, hh:hh + 1], eb_[:], ALU.is_ge, ALU.mult, wk=[("Gh", j, hh)])
                            for c2 in range(4):
                                gtb, atb = ps[c2 % 2], ps[2 + c2 % 2]
                                for ci in range(2):
                                    cc_ = c2 * 2 + ci
                                    for j in range(gs):
                                        for hh in range(8):
                                            mm(gtb[:, ci * 256 + j * 128:ci * 256 + (j + 1) * 128], Gh2[j][:, hh, cc_ * 128:(cc_ + 1) * 128], IDb, start=(hh == 0), stop=(hh == 7), rk=[("Gh", j, hh), IDb])
                                    for kk in range(8):
                                        mm(atb[:, ci * 256:ci * 256 + gs * 128], ub[:, kk, cc_ * 128:(cc_ + 1) * 128], hT2[:, kk, 0:gs * 128], start=(kk == 0), stop=(kk == 7))
                                actf(gA[c2 % 2][:], atb[:], AF.Gelu_apprx_tanh)
                                tt("dve", WT[c2 % 2][:], gA[c2 % 2][:], gtb[:], ALU.mult)
                                for ci in range(2):
                                    cc_ = c2 * 2 + ci
                                    first = (ab == 0 and cc_ == 0); last = (ab == 15 and cc_ == 7)
                                    for j in range(gs):
                                        for half in range(2):
                                            mm(ps[4 + 2 * j + half][:], WT[c2 % 2][:, ci * 256 + j * 128:ci * 256 + (j + 1) * 128], vb2[:, cc_, half * 512:(half + 1) * 512], start=first, stop=last)
                        for j in range(gs):
                            for half in range(2):
                                tt("dve", xres[:, k0 + j, half * 512:(half + 1) * 512], xres[:, k0 + j, half * 512:(half + 1) * 512], ps[4 + 2 * j + half][:], ALU.add)
                    P.barrier()
                P.barrier()
        dma(O["y"].ap().rearrange("(k p) d -> p k d", p=128), xres[:])
        P.emit(st)
    return nc


def host_inputs(inputs, cfg):
    f = lambda k: np.ascontiguousarray(np.asarray(inputs[k]))
    NP, NT, NPG = cfg.NP, cfg.NT, cfg.NPG
    x_prompt, x_sample = f("x_prompt"), f("x_sample")
    past_len = NPG * 128
    inv = (1.0 / (10000.0 ** (np.arange(16, dtype=np.float32) / 16))).astype(np.float32)
    ii = np.arange(128)
    consts = np.zeros((8, 128, 128), np.float32)
    same = (ii[:, None] // 8) == (ii[None, :] // 8)
    consts[0] = np.eye(128)
    consts[1] = (ii[None, :] >= ii[:, None])
    consts[2] = (ii[:, None] > ii[None, :])
    consts[3] = 1.0
    consts[4] = consts[1] * same
    consts[5] = consts[2] * same
    consts[6] = same
    consts[7, :64, 0] = 1.0
    consts[7, 64:, 1] = 1.0
    blkind = (ii[:, None] // 8 == np.arange(16)[None, :]).astype(np.float32)
    qk = lambda g: np.concatenate([g, g[:, 64:]], axis=1)
    wsT = f("gm_ws").transpose(0, 1, 3, 2)
    wsT_s = np.tile(wsT[:, :, :8, :8], (1, 1, 16, 16))
    bs = f("gm_bs")
    bsT = bs.transpose(0, 2, 1)
    bsT_s = np.tile(bsT[:, :8, :], (1, 16, 1))
    shared = {
        "consts": consts, "blkind": blkind,
        "w_in": f("w_in"), "g_mix": f("g_mix"), "g_ckv": f("g_ckv"), "g_cq": f("g_cq"), "w_uq": f("w_uq"),
        "w_uk": f("w_uk").reshape(-1, 128, 256), "w_ukT": np.ascontiguousarray(f("w_uk").transpose(0, 2, 3, 1)),
        "w_uv": f("w_uv").reshape(-1, 128, 256), "gq96": qk(f("g_qk_q")), "gk96": qk(f("g_qk_k")),
        "conv_wT": np.ascontiguousarray(f("conv_w").transpose(0, 2, 1)), "conv_b": f("conv_b"), "dt_bias": f("dt_bias"),
        "a_log": f("a_log"), "d_skip": f("d_skip"), "g_ssd_out": f("g_ssd_out"),
        "gm_ln_g": f("gm_ln_g"), "gm_ln_b": f("gm_ln_b"),
        "gm_wsT": np.ascontiguousarray(np.stack([wsT, wsT_s], 1)), "gm_bsT": np.ascontiguousarray(np.stack([bsT, bsT_s], 1)),
        "g_ffn": f("g_ffn"), "w_pq": f("w_pq"),
        "peer_kT": np.ascontiguousarray(np.stack([f("peer_k1"), f("peer_k2")], 2).transpose(0, 4, 1, 2, 3).reshape(f("peer_k1").shape[0], 128, 16, 128)),
        "peer_uT": np.ascontiguousarray(f("peer_u").transpose(0, 2, 1)), "peer_v": f("peer_v"),
        "w_out": f("w_out"), "g_memm": f("g_memm"), "g_mk": f("g_mk"), "w_mk": f("w_mk"), "w_mv": f("w_mv"),
        "g_memx": f("g_memx"), "w_mq": f("w_mq"), "g_mq": f("g_mq"), "w_mo": f("w_mo"),
        "colmask": np.ascontiguousarray(np.broadcast_to((np.arange(128)[None, None, :] // 8 == np.arange(16)[None, :, None]), (128, 16, 128)).astype(np.float32)),
        "cache_ckv": f("cache_mla_ckv").reshape(f("cache_mla_ckv").shape[0], -1, 128),
        "cache_kr": f("cache_mla_krope").reshape(f("cache_mla_krope").shape[0], -1, 32),
    }
    maps = []
    for c in range(8):
        g, r = c // 4, c % 4
        xin = np.concatenate([x_prompt[g, r * NP * 128:(r + 1) * NP * 128], x_sample[c * 16:(c + 1) * 16].reshape(128, D)], 0)
        pos = np.concatenate([r * NP * 128 + np.arange(NP * 128), past_len + (np.arange(128) % 8)]).astype(np.float32)
        ang = pos[:, None] * inv[None, :]
        rope = np.concatenate([np.cos(ang), np.sin(ang)], 1).astype(np.float32)
        vis = np.zeros((128, 12), np.float32)
        for rr in range(4):
            vis[:, rr] = 0.0 if rr < r else NEG
            vis[:, 4 + rr] = 1.0 if rr < r else 0.0
            vis[:, 8 + rr] = 1.0 if rr == r - 1 else 0.0
        m = dict(shared)
        ml, half = r // 2, r % 2
        ml = min(ml, f("g_memm").shape[0] - 1)
        m.update({"mem_rows": np.ascontiguousarray(f("mem_prompt")[g, half * 128:(half + 1) * 128]),
                  "g_memm_l": f("g_memm")[ml:ml + 1], "g_mk_l": f("g_mk")[ml:ml + 1], "w_mk_l": f("w_mk")[ml], "w_mv_l": f("w_mv")[ml]})
        m.update({"mem_full": np.ascontiguousarray(f("mem_prompt")[g]),
                  "cache_mem_k": np.ascontiguousarray(f("cache_mem_k")[:, c * 16:(c + 1) * 16].reshape(f("cache_mem_k").shape[0], 16, 256, 1024)),
                  "cache_mem_v": np.ascontiguousarray(f("cache_mem_v")[:, c * 16:(c + 1) * 16].reshape(f("cache_mem_v").shape[0], 16, 256, 1024))})
        m.update({"xin": np.ascontiguousarray(xin), "rope": rope, "vis": vis,
                  "page_idx": np.ascontiguousarray(f("page_table")[c * 16:(c + 1) * 16].reshape(1, -1).astype(np.int32)),
                  "state_conv": np.ascontiguousarray(f("state_conv")[:, c * 16:(c + 1) * 16]),
                  "state_ssm": np.ascontiguousarray(f("state_ssm")[:, c * 16:(c + 1) * 16])})
        maps.append(m)
    return maps


def kernel(**inputs):
    cfg = Cfg(depth=2, np_=16, npg=64, nphys=int(np.asarray(inputs["cache_mla_ckv"]).shape[1]), phases="ACDE")
    NP_, NPT_ = cfg.NP, cfg.NP * 128
    maps = host_inputs(inputs, cfg)
    nc = build(cfg)
    res = run_bass_kernel_spmd(nc, maps, core_ids=list(range(8))).results
    B, SEQ, NS, LS, DEPTH_ = 2, 4 * NPT_, 128, 8, cfg.DEPTH
    z = lambda *s: np.zeros(s, np.float32)
    y_p, y_s = z(B, SEQ, D), z(NS, LS, D)
    ckv_p, kr_p = z(DEPTH_, B, SEQ, 128), z(DEPTH_, B, SEQ, 32)
    ssm_p, conv_p = z(DEPTH_, B, 8, 64, 64), z(DEPTH_, B, 3, 768)
    mk, mv = z(DEPTH_, B, 256, 4, 256), z(DEPTH_, B, 256, 4, 256)
    gmv_p = z(DEPTH_, B, 128, 256)
    ckv_s, kr_s = z(DEPTH_, NS, LS, 128), z(DEPTH_, NS, LS, 32)
    ssm_s, conv_s, gmv_s = z(DEPTH_, NS, 8, 64, 64), z(DEPTH_, NS, 3, 768), z(DEPTH_, NS, LS, 256)
    for c in range(8):
        g, r = c // 4, c % 4
        o = res[c]
        sl = slice(r * NPT_, (r + 1) * NPT_)
        bs = slice(c * 16, (c + 1) * 16)
        y_p[g, sl] = o["y"][:NPT_]
        y_s[bs] = o["y"][NPT_:].reshape(16, 8, D)
        for l in range(DEPTH_):
            ckv_p[l, g, sl] = o["new_ckv"][l, :NPT_]; kr_p[l, g, sl] = o["new_kr"][l, :NPT_]
            ckv_s[l, bs] = o["new_ckv"][l, NPT_:].reshape(16, 8, 128); kr_s[l, bs] = o["new_kr"][l, NPT_:].reshape(16, 8, 32)
            ssm_s[l, bs] = o["new_ssm"][l, :16]; conv_s[l, bs] = o["new_conv"][l, :16]
            gmv_s[l, bs] = o["new_gmv"][l, 128:].reshape(16, 8, 256)
            if r == 3:
                ssm_p[l, g] = o["new_ssm"][l, 16]; conv_p[l, g] = o["new_conv"][l, 16]; gmv_p[l, g] = o["new_gmv"][l, :128]
        ml, half = r // 2, r % 2
        mk[ml, g, half * 128:(half + 1) * 128] = o["mem_k"].reshape(128, 4, 256)
        mv[ml, g, half * 128:(half + 1) * 128] = o["mem_v"].reshape(128, 4, 256)
    return (y_p, y_s, ckv_p, kr_p, ssm_p, conv_p, mk, mv, gmv_p, ckv_s, kr_s, ssm_s, conv_s, gmv_s)
```
